# Optimizing a Trainium2 kernel written in Bass

```python
import math
import jax, jax.numpy as jnp
from jax import lax
import numpy as np

D_MODEL = 1024
BATCH = 16
SEQ = 4096
DEPTH = 4

MEM_TOKENS = 256
ROPE_THETA = 10000.0
NORM_EPS = 1e-6
Q_BLOCK = 128

MLA_HEADS = 8
MLA_Q_LORA = 256
MLA_KV_LORA = 128
MLA_NOPE_DIM = 64
MLA_ROPE_DIM = 32
MLA_V_DIM = 64

DIFF_HEADS = 8
DIFF_HEAD_DIM = 32

XATTN_HEADS = 4
XATTN_HEAD_DIM = 128

N_BRANCHES = 3
BRANCH_WIDTH = MLA_HEADS * MLA_V_DIM

IN_SPLITS = (
    MLA_Q_LORA,
    MLA_KV_LORA,
    MLA_ROPE_DIM,
    DIFF_HEADS * 2 * DIFF_HEAD_DIM,
    DIFF_HEADS * 2 * DIFF_HEAD_DIM,
    DIFF_HEADS * 2 * DIFF_HEAD_DIM,
    XATTN_HEADS * XATTN_HEAD_DIM,
    N_BRANCHES * BRANCH_WIDTH,
    N_BRANCHES * D_MODEL,
)
IN_WIDTH = sum(IN_SPLITS)
IN_OFFSETS = tuple(int(v) for v in np.cumsum(IN_SPLITS)[:-1])

kernel_name = "hybrid_mla_diffattn_memxattn_gated_block"


def rmsnorm(x, g):
    xf = x.astype(jnp.float32)
    y = xf * lax.rsqrt(jnp.mean(xf * xf, axis=-1, keepdims=True) + NORM_EPS)
    return (y * g.astype(jnp.float32)).astype(x.dtype)


def rope_tables(positions, dim):
    inv = ROPE_THETA ** (-jnp.arange(0, dim, 2, dtype=jnp.float32) / dim)
    ang = positions.astype(jnp.float32)[:, None] * inv[None, :]
    return jnp.cos(ang), jnp.sin(ang)


def apply_rope(x, cos, sin):
    half = x.shape[-1] // 2
    x1, x2 = x[..., :half], x[..., half:]
    c, s = cos.astype(x.dtype), sin.astype(x.dtype)
    return jnp.concatenate([x1 * c - x2 * s, x2 * c + x1 * s], axis=-1)


def causal_map_attention(q, k, v, coef, scale):
    B, G, H, S, Dk = q.shape
    n_blk = S // Q_BLOCK
    qb = jnp.moveaxis(q.reshape(B, G, H, n_blk, Q_BLOCK, Dk), 3, 0)
    key_pos = jnp.arange(S)
    coef32 = coef.astype(jnp.float32)

    def one_block(args):
        q_blk, start = args
        s = jnp.einsum('bghqd,bghkd->bghqk', q_blk, k).astype(jnp.float32) * scale
        q_pos = start + jnp.arange(Q_BLOCK)
        mask = key_pos[None, :] <= q_pos[:, None]
        s = jnp.where(mask, s, -jnp.inf)
        p = jax.nn.softmax(s, axis=-1)
        p = jnp.einsum('g,bghqk->bhqk', coef32, p)
        return jnp.einsum('bhqk,bhkd->bhqd', p.astype(v.dtype), v)

    starts = jnp.arange(n_blk) * Q_BLOCK
    o = lax.map(one_block, (qb, starts))
    return jnp.moveaxis(o, 0, 2).reshape(B, H, S, v.shape[-1])


def hybrid_layer(x, mem, cos_r, sin_r, cos_d, sin_d, lam_init,
                 norm_pre, norm_post, w_in, q_norm, w_uq, kv_norm, w_ukv,
                 lambda_q1, lambda_k1, lambda_q2, lambda_k2, diff_subln,
                 mem_norm, w_mem_kv, b_gate, w_branch, w_out):
    B, S, D = x.shape
    h = rmsnorm(x, norm_pre)
    proj = h @ w_in
    (c_q, c_kv, k_pe, q_d, k_d, v_d, q_x, silu_gates, gate_logits) = jnp.split(proj, IN_OFFSETS, axis=-1)

    c_q = rmsnorm(c_q, q_norm)
    q_a = (c_q @ w_uq).reshape(B, S, MLA_HEADS, MLA_NOPE_DIM + MLA_ROPE_DIM).transpose(0, 2, 1, 3)
    q_nope, q_pe = q_a[..., :MLA_NOPE_DIM], apply_rope(q_a[..., MLA_NOPE_DIM:], cos_r, sin_r)
    c_kv = rmsnorm(c_kv, kv_norm)
    kv = (c_kv @ w_ukv).reshape(B, S, MLA_HEADS, MLA_NOPE_DIM + MLA_V_DIM).transpose(0, 2, 1, 3)
    k_nope, v_a = kv[..., :MLA_NOPE_DIM], kv[..., MLA_NOPE_DIM:]
    k_pe = apply_rope(k_pe[:, None], cos_r, sin_r)
    q_a = jnp.concatenate([q_nope, q_pe], axis=-1)[:, None]
    k_a = jnp.concatenate([k_nope, jnp.broadcast_to(k_pe, (B, MLA_HEADS, S, MLA_ROPE_DIM))], axis=-1)[:, None]
    o_a = causal_map_attention(q_a, k_a, v_a, jnp.ones((1,), jnp.float32),
                               (MLA_NOPE_DIM + MLA_ROPE_DIM) ** -0.5)
    o_a = o_a.transpose(0, 2, 1, 3).reshape(B, S, BRANCH_WIDTH)

    q_d = apply_rope(q_d.reshape(B, S, DIFF_HEADS, 2, DIFF_HEAD_DIM).transpose(0, 3, 2, 1, 4), cos_d, sin_d)
    k_d = apply_rope(k_d.reshape(B, S, DIFF_HEADS, 2, DIFF_HEAD_DIM).transpose(0, 3, 2, 1, 4), cos_d, sin_d)
    v_d = v_d.reshape(B, S, DIFF_HEADS, 2 * DIFF_HEAD_DIM).transpose(0, 2, 1, 3)
    lam = (jnp.exp(jnp.sum(lambda_q1.astype(jnp.float32) * lambda_k1.astype(jnp.float32)))
           - jnp.exp(jnp.sum(lambda_q2.astype(jnp.float32) * lambda_k2.astype(jnp.float32)))
           + lam_init)
    coef = jnp.stack([jnp.ones_like(lam), -lam])
    o_b = causal_map_attention(q_d, k_d, v_d, coef, DIFF_HEAD_DIM ** -0.5)
    o_b = rmsnorm(o_b, diff_subln) * (1.0 - lam_init)
    o_b = o_b.transpose(0, 2, 1, 3).reshape(B, S, BRANCH_WIDTH)

    m = rmsnorm(mem, mem_norm)
    kv_m = m @ w_mem_kv
    M = mem.shape[1]
    k_m = kv_m[..., :XATTN_HEADS * XATTN_HEAD_DIM].reshape(B, M, XATTN_HEADS, XATTN_HEAD_DIM).transpose(0, 2, 1, 3)
    v_m = kv_m[..., XATTN_HEADS * XATTN_HEAD_DIM:].reshape(B, M, XATTN_HEADS, XATTN_HEAD_DIM).transpose(0, 2, 1, 3)
    q_x = q_x.reshape(B, S, XATTN_HEADS, XATTN_HEAD_DIM).transpose(0, 2, 1, 3)
    s_x = jnp.einsum('bhqd,bhkd->bhqk', q_x, k_m).astype(jnp.float32) * XATTN_HEAD_DIM ** -0.5
    p_x = jax.nn.softmax(s_x, axis=-1)
    o_c = jnp.einsum('bhqk,bhkd->bhqd', p_x.astype(v_m.dtype), v_m)
    o_c = o_c.transpose(0, 2, 1, 3).reshape(B, S, BRANCH_WIDTH)

    g_paths = jnp.split(silu_gates, N_BRANCHES, axis=-1)
    gl = (gate_logits + b_gate).reshape(B, S, N_BRANCHES, D)
    branches = (o_a, o_b, o_c)
    merged = None
    for i in range(N_BRANCHES):
        y = (branches[i] * jax.nn.silu(g_paths[i])) @ w_branch[i]
        term = jax.nn.sigmoid(gl[:, :, i]) * y
        merged = term if merged is None else merged + term
    out = merged @ w_out
    return x + rmsnorm(out, norm_post)


def setup_inputs(seed: int = 0) -> dict:
    key = jax.random.key(seed)
    ks = jax.random.split(key, 24)
    L, D = DEPTH, D_MODEL
    nrm = lambda k, shape, scale: jax.random.normal(k, shape, jnp.float32) * scale
    gain = lambda k, shape: 1.0 + 0.05 * jax.random.normal(k, shape, jnp.float32)
    start = jax.random.randint(ks[2], (), 0, 1024, dtype=jnp.int32)
    return {
        'x': nrm(ks[0], (BATCH, SEQ, D), 1.0),
        'mem': nrm(ks[1], (BATCH, MEM_TOKENS, D), 1.0),
        'positions': (start + jnp.arange(SEQ, dtype=jnp.int32)).astype(jnp.int32),
        'norm_pre': gain(ks[3], (L, D)),
        'norm_post': gain(ks[4], (L, D)),
        'w_in': nrm(ks[5], (L, D, IN_WIDTH), D ** -0.5),
        'q_norm': gain(ks[6], (L, MLA_Q_LORA)),
        'w_uq': nrm(ks[7], (L, MLA_Q_LORA, MLA_HEADS * (MLA_NOPE_DIM + MLA_ROPE_DIM)), MLA_Q_LORA ** -0.5),
        'kv_norm': gain(ks[8], (L, MLA_KV_LORA)),
        'w_ukv': nrm(ks[9], (L, MLA_KV_LORA, MLA_HEADS * (MLA_NOPE_DIM + MLA_V_DIM)), MLA_KV_LORA ** -0.5),
        'lambda_q1': nrm(ks[10], (L, DIFF_HEAD_DIM), 0.1),
        'lambda_k1': nrm(ks[11], (L, DIFF_HEAD_DIM), 0.1),
        'lambda_q2': nrm(ks[12], (L, DIFF_HEAD_DIM), 0.1),
        'lambda_k2': nrm(ks[13], (L, DIFF_HEAD_DIM), 0.1),
        'diff_subln': gain(ks[14], (L, 2 * DIFF_HEAD_DIM)),
        'mem_norm': gain(ks[15], (L, D)),
        'w_mem_kv': nrm(ks[16], (L, D, 2 * XATTN_HEADS * XATTN_HEAD_DIM), D ** -0.5),
        'b_gate': nrm(ks[17], (L, N_BRANCHES * D), 0.1),
        'w_branch': nrm(ks[18], (L, N_BRANCHES, BRANCH_WIDTH, D), BRANCH_WIDTH ** -0.5),
        'w_out': nrm(ks[19], (L, D, D), D ** -0.5),
    }


def reference(x, mem, positions, norm_pre, norm_post, w_in, q_norm, w_uq, kv_norm, w_ukv,
              lambda_q1, lambda_k1, lambda_q2, lambda_k2, diff_subln,
              mem_norm, w_mem_kv, b_gate, w_branch, w_out):
    cos_r, sin_r = rope_tables(positions, MLA_ROPE_DIM)
    cos_d, sin_d = rope_tables(positions, DIFF_HEAD_DIM)
    for l in range(DEPTH):
        lam_init = 0.8 - 0.6 * math.exp(-0.3 * l)
        x = hybrid_layer(x, mem, cos_r, sin_r, cos_d, sin_d, lam_init,
                         norm_pre[l], norm_post[l], w_in[l], q_norm[l], w_uq[l], kv_norm[l], w_ukv[l],
                         lambda_q1[l], lambda_k1[l], lambda_q2[l], lambda_k2[l], diff_subln[l],
                         mem_norm[l], w_mem_kv[l], b_gate[l], w_branch[l], w_out[l])
    return x
```

```python
import math
import contextlib
import numpy as np
import concourse.bass as bass
import concourse.mybir as mybir
from concourse.bass_utils import run_bass_kernel_spmd
from concourse.alu_op_type import AluOpType as ALU

F32 = mybir.dt.float32
BF16 = mybir.dt.bfloat16
I32 = mybir.dt.int32
AF = mybir.ActivationFunctionType
AX = mybir.AxisListType

D = 1024
IN_W = 7072
EPS = 1e-6
OFF_CQ, OFF_CKV, OFF_KPE, OFF_QD, OFF_KD, OFF_VD, OFF_QX, OFF_SIL, OFF_GL = (
    0, 256, 384, 416, 928, 1440, 1952, 2464, 4000)
PV_NPRE, PV_MEM, PV_QN, PV_KVN, PV_BG, PV_SUB, PV_N = 0, 8, 16, 18, 19, 43, 44
MEM_T = 256
TWO_PI = 2.0 * math.pi


class Tk:
    __slots__ = ("base", "w", "r", "joined", "sem", "name")

    def __init__(self, name=""):
        self.base = []
        self.w = []
        self.r = []
        self.joined = False
        self.sem = None
        self.name = name


def _compact(lst):
    out = []
    last = {}
    for i in lst:
        if i.is_dma:
            out.append(i)
        else:
            last[i.eng] = i
    out.extend(last.values())
    return out


class Ins:
    __slots__ = ("eng", "fn", "deps", "is_dma", "sem", "val", "needed", "phase")


ENGS = ("pe", "act", "dve", "pool", "sp")
BLK = {"pe": "tensor", "act": "scalar", "dve": "vector", "pool": "gpsimd", "sp": "sync"}


class Prog:
    def __init__(self, nc):
        self.nc = nc
        self.esem = {e: nc.alloc_semaphore("pg_" + e) for e in ("pe", "act", "dve", "pool")}
        self.ecnt = {e: 0 for e in self.esem}
        self.free_sems = []
        self.semcnt = {}
        self.waited = {e: {} for e in ENGS}
        self.phase = 0
        self.n_ins = 0
        self.begin()

    def begin(self):
        self.phase += 1
        self.q = {e: [] for e in ENGS}
        self.dma_slots = []

    def _deps_of(self, ins, tks_r, tks_w, tks_j):
        deps = []
        for t in tks_r:
            deps.extend(t.w)
        for t in tks_w:
            deps.extend(t.base); deps.extend(t.w); deps.extend(t.r)
        for t in tks_j:
            deps.extend(t.base)
            if not (t.joined and not t.r):
                deps.extend(t.w); deps.extend(t.r)
        out = []
        for d in deps:
            if d.phase != self.phase or d is ins:
                continue
            if (not d.is_dma) and (not ins.is_dma) and d.eng == ins.eng and d.eng == "pe":
                continue
            if not d.is_dma:
                d.needed = True
            out.append(d)
        return out

    def _update(self, ins, tks_r, tks_w, tks_j):
        for t in tks_w:
            t.base = []; t.w = [ins]; t.r = []; t.joined = False
        for t in tks_j:
            if t.joined and not t.r:
                t.w.append(ins)
                if len(t.w) > 8:
                    t.w = _compact(t.w)
            else:
                t.base = _compact(t.w + t.r)
                t.w = [ins]; t.r = []; t.joined = True
        for t in tks_r:
            t.r.append(ins)
            if len(t.r) > 8:
                t.r = _compact(t.r)

    def op(self, eng, fn, reads=(), writes=(), joins=()):
        ins = Ins()
        ins.eng = eng; ins.fn = fn; ins.is_dma = False; ins.needed = False
        ins.phase = self.phase; ins.sem = None; ins.val = 0
        ins.deps = self._deps_of(ins, reads, writes, joins)
        self._update(ins, reads, writes, joins)
        self.q[eng].append(ins)
        return ins

    def dma(self, eng, out, in_, slot, reads=(), writes=(), joins=()):
        ins = Ins()
        ins.eng = eng; ins.is_dma = True; ins.needed = True
        ins.phase = self.phase
        ins.fn = lambda e: e.dma_start(out=out, in_=in_)
        if slot.sem is None:
            slot.sem = self.free_sems.pop() if self.free_sems else self.nc.alloc_semaphore(
                "dq%d" % len(self.semcnt))
            self.semcnt.setdefault(slot.sem, 0)
            self.dma_slots.append(slot)
        self.semcnt[slot.sem] += 16
        ins.sem = slot.sem; ins.val = self.semcnt[slot.sem]
        ins.deps = self._deps_of(ins, reads, writes, joins)
        self._update(ins, reads, writes, joins)
        self.q[eng].append(ins)
        return ins

    def flush(self, es):
        nc = self.nc
        for e in ("pe", "act", "dve", "pool"):
            for ins in self.q[e]:
                if ins.needed and not ins.is_dma:
                    self.ecnt[e] += 1
                    ins.sem = self.esem[e]; ins.val = self.ecnt[e]
        block = es.enter_context(nc.Block())
        final_waits = [(s.sem, self.semcnt[s.sem]) for s in self.dma_slots]
        for e in ENGS:
            q = self.q[e]
            waited = self.waited[e]
            fw = final_waits if e == "sp" else ()
            self.n_ins += len(q)

            def body(eng, q=q, waited=waited, fw=fw):
                for ins in q:
                    need = {}
                    for d in ins.deps:
                        if d.val > need.get(d.sem, 0):
                            need[d.sem] = d.val
                    for s, v in need.items():
                        if waited.get(s, 0) < v:
                            eng.wait_ge(s, v)
                            waited[s] = v
                    h = ins.fn(eng)
                    if ins.is_dma:
                        h.then_inc(ins.sem, 16)
                    elif ins.needed:
                        h.then_inc(ins.sem, 1)
                for s, v in fw:
                    if waited.get(s, 0) < v:
                        eng.wait_ge(s, v)
                        waited[s] = v
            getattr(block, BLK[e])(body)
        for s in self.dma_slots:
            self.free_sems.append(s.sem)
            s.sem = None
        self.begin()


class Ring:
    def __init__(self, items):
        self.items = [(t, Tk()) for t in items]
        self.i = 0

    def next(self):
        it = self.items[self.i % len(self.items)]
        self.i += 1
        return it


def build(S=4096, L=4, NB=2, dbg=False, upto=99):
    NT = S // 128
    NG = S // 512
    lam_init = [0.8 - 0.6 * math.exp(-0.3 * l) for l in range(L)]
    nc = bass.Bass("TRN2", target_bir_lowering=False)
    dt = nc.dram_tensor

    def din(name, shape, dtype=F32):
        return dt(name, list(shape), dtype, kind="ExternalInput").ap()

    x_in = din("x", [NB, S, D])
    mem_in = din("mem", [NB, MEM_T, D])
    pos_in = din("positions", [S], I32)
    norm_pre = din("norm_pre", [L, D]); norm_post = din("norm_post", [L, D])
    w_in = din("w_in", [L, D, IN_W])
    q_norm = din("q_norm", [L, 256]); w_uq = din("w_uq", [L, 256, 768])
    kv_norm = din("kv_norm", [L, 128]); w_ukv = din("w_ukv", [L, 128, 1024])
    lq1 = din("lambda_q1", [L, 32]); lk1 = din("lambda_k1", [L, 32])
    lq2 = din("lambda_q2", [L, 32]); lk2 = din("lambda_k2", [L, 32])
    diff_subln = din("diff_subln", [L, 64])
    mem_norm = din("mem_norm", [L, D]); w_mem_kv = din("w_mem_kv", [L, D, 1024])
    b_gate = din("b_gate", [L, 3 * D])
    w_branch = din("w_branch", [L, 3, 512, D]); w_out = din("w_out", [L, D, D])
    out = dt("out", [NB, S, D], F32, kind="ExternalOutput").ap()

    skind = "ExternalOutput" if dbg else "Internal"

    def dsc(name, shape, dtype=BF16):
        return dt(name, list(shape), dtype, kind=skind).ap()

    QM = dsc("s_qm", [8, 96, S]); KM = dsc("s_km", [8, 96, S]); VM = dsc("s_vm", [S, 1024])
    QD = dsc("s_qd", [512, S]); KD = dsc("s_kd", [512, S]); VD = dsc("s_vd", [S, 1024])
    QX = dsc("s_qx", [512, S]); SIL = dsc("s_sil", [1536, S]); SG = dsc("s_sg", [3072, S])
    OA = dsc("s_oa", [512, S]); OB = dsc("s_ob", [512, S])
    COS = dsc("s_cos", [128, S], F32); SIN = dsc("s_sin", [128, S], F32)

    p = Prog(nc)
    ges = contextlib.ExitStack()
    _uc = [0]

    def uname(name):
        _uc[0] += 1
        return "%s_%d" % (name, _uc[0])

    def gsb(name, shape, dtype):
        return ges.enter_context(nc.sbuf_tensor(name, list(shape), dtype))

    ident_bf = gsb("ident_bf", [128, 128], BF16)
    ident_f = gsb("ident_f", [128, 128], F32)
    mask_bf = gsb("mask_bf", [128, 128], BF16)
    ones_bf = gsb("ones_bf", [128, 128], BF16)
    pv = gsb("pv", [128, L, PV_N], F32)
    gsub = gsb("gsub", [128, L], F32)
    neglam = gsb("neglam", [128, L], F32)
    kmT = gsb("kmT", [128, 4, MEM_T], BF16)
    vm = gsb("vm", [128, 2, 4, 128], BF16)
    npost = gsb("npost", [128, D], F32)
    psum_all = ges.enter_context(nc.psum_tensor("psum_all", [128, 8, 512], F32))
    banks = [psum_all[:, i, :] for i in range(8)]
    bank_tk = [Tk("bank%d" % i) for i in range(8)]

    class BankPool:
        def __init__(self, ids):
            self.ids = ids; self.i = 0

        def next(self):
            b = self.ids[self.i % len(self.ids)]
            self.i += 1
            return banks[b], bank_tk[b]

    def mm(outap, lhsT, rhs, start, stop, reads, bk, first=None):
        if first is None:
            first = start
        kw = {"writes": (bk,)} if first else {"joins": (bk,)}
        p.op("pe", lambda e: e.matmul(outap, lhsT=lhsT, rhs=rhs, start=start, stop=stop),
             reads=reads, **kw)

    def tr(outap, in_, ident, reads, bk, first):
        kw = {"writes": (bk,)} if first else {"joins": (bk,)}
        p.op("pe", lambda e: e.transpose(outap, in_, ident), reads=reads, **kw)

    def act(outap, in_, func, reads, writes=(), joins=(), scale=None, bias=None, accum=None):
        kw = {}
        if scale is not None:
            kw["scale"] = scale
        if bias is not None:
            kw["bias"] = bias
        if accum is not None:
            kw["accum_out"] = accum
        p.op("act", lambda e: e.activation(out=outap, in_=in_, func=func, **kw),
             reads=reads, writes=writes, joins=joins)

    def tt(eng, outap, a, b, op, reads, writes=(), joins=()):
        p.op(eng, lambda e: e.tensor_tensor(out=outap, in0=a, in1=b, op=op),
             reads=reads, writes=writes, joins=joins)

    def ts(eng, outap, a, s1, op0, reads, writes=(), joins=(), s2=None, op1=None):
        if op1 is None:
            p.op(eng, lambda e: e.tensor_scalar(out=outap, in0=a, scalar1=s1, scalar2=None, op0=op0),
                 reads=reads, writes=writes, joins=joins)
        else:
            p.op(eng, lambda e: e.tensor_scalar(out=outap, in0=a, scalar1=s1, scalar2=s2,
                                                op0=op0, op1=op1),
                 reads=reads, writes=writes, joins=joins)

    def cp(eng, outap, in_, reads, writes=(), joins=()):
        if eng == "act":
            p.op("act", lambda e: e.copy(out=outap, in_=in_), reads=reads, writes=writes, joins=joins)
        else:
            p.op(eng, lambda e: e.tensor_copy(out=outap, in_=in_), reads=reads, writes=writes,
                 joins=joins)

    def stt(outap, a, scalar, b, op0, op1, reads, writes=(), joins=()):
        p.op("dve", lambda e: e.scalar_tensor_tensor(out=outap, in0=a, scalar=scalar, in1=b,
                                                     op0=op0, op1=op1),
             reads=reads, writes=writes, joins=joins)

    def recip(outap, in_, reads, writes=(), joins=()):
        p.op("dve", lambda e: e.reciprocal(out=outap, in_=in_), reads=reads, writes=writes,
             joins=joins)

    def memset(eng, ap, val, writes=(), joins=()):
        p.op(eng, lambda e: e.memset(ap, val), writes=writes, joins=joins)

    def phase_setup():
        with contextlib.ExitStack() as es:
            def sb(name, shape, dtype):
                return es.enter_context(nc.sbuf_tensor(uname(name), list(shape), dtype))
            iot = sb("iot", [128, 128], I32); t_iot = Tk()
            p.op("pool", lambda e: e.iota(iot[:], pattern=[[1, 128]], base=0, channel_multiplier=-1),
                 writes=(t_iot,))
            t_c = Tk()
            ts("dve", ident_bf[:], iot[:], 0, ALU.is_equal, (t_iot,), joins=(t_c,))
            ts("dve", ident_f[:], iot[:], 0, ALU.is_equal, (t_iot,), joins=(t_c,))
            ts("dve", mask_bf[:], iot[:], 0, ALU.is_ge, (t_iot,), joins=(t_c,))
            memset("dve", ones_bf[:], 1.0, joins=(t_c,))
            rows = sb("rows", [PV_N, L, 128], F32); t_rows = Tk()
            memset("dve", rows[:], 0.0, writes=(t_rows,))

            def ld(dst, src):
                p.dma("sp", dst, src, t_rows, joins=(t_rows,))
            for l in range(L):
                ld(rows[PV_NPRE:PV_NPRE + 8, l, :], norm_pre[l].rearrange("(c p) -> c p", p=128))
                ld(rows[PV_MEM:PV_MEM + 8, l, :], mem_norm[l].rearrange("(c p) -> c p", p=128))
                ld(rows[PV_QN:PV_QN + 2, l, :], q_norm[l].rearrange("(c p) -> c p", p=128))
                ld(rows[PV_KVN:PV_KVN + 1, l, :], kv_norm[l].rearrange("(c p) -> c p", p=128))
                ld(rows[PV_BG:PV_BG + 24, l, :], b_gate[l].rearrange("(c p) -> c p", p=128))
                ld(rows[PV_SUB:PV_SUB + 1, l, 0:64], diff_subln[l].rearrange("(c p) -> c p", p=64))
            t_pv = Tk()
            for l in range(L):
                bk, btk = banks[l % 8], bank_tk[l % 8]
                tr(bk[:, 0:PV_N], rows[:, l, :], ident_f[0:PV_N, 0:PV_N], (t_rows, t_c), btk, True)
                cp("dve", pv[:, l, :], bk[:, 0:PV_N], (btk,), joins=(t_pv,))
                ts("dve", gsub[:, l:l + 1], pv[:, l, PV_SUB:PV_SUB + 1], 1.0 - lam_init[l], ALU.mult,
                   (t_pv,), joins=(t_pv,))
            lam4 = [sb("lam%d" % i, [128, L, 32], F32) for i in range(4)]
            t_l = Tk()
            for i, src in enumerate((lq1, lk1, lq2, lk2)):
                p.dma("sp", lam4[i][:], src.partition_broadcast(128), t_l, joins=(t_l,))
            pr = sb("lampr", [128, 2, L, 32], F32); t_pr = Tk()
            tt("dve", pr[:, 0], lam4[0][:], lam4[1][:], ALU.mult, (t_l,), joins=(t_pr,))
            tt("dve", pr[:, 1], lam4[2][:], lam4[3][:], ALU.mult, (t_l,), joins=(t_pr,))
            sm = sb("lamsm", [128, 2, L], F32); t_sm = Tk()
            p.op("dve", lambda e: e.tensor_reduce(out=sm[:], in_=pr[:], axis=AX.X, op=ALU.add),
                 reads=(t_pr,), writes=(t_sm,))
            ex = sb("lamex", [128, 2, L], F32); t_ex = Tk()
            act(ex[:], sm[:], AF.Exp, (t_sm,), writes=(t_ex,))
            t_nl = Tk()
            tt("dve", neglam[:], ex[:, 1], ex[:, 0], ALU.subtract, (t_ex,), writes=(t_nl,))
            for l in range(L):
                ts("dve", neglam[:, l:l + 1], neglam[:, l:l + 1], -lam_init[l], ALU.add, (t_nl,),
                   joins=(t_nl,))
            oh_i = sb("oh_i", [128, 16], I32); t_pi = Tk()
            p.op("pool", lambda e: e.iota(oh_i[:], pattern=[[1, 16]], base=0, channel_multiplier=-1),
                 writes=(t_pi,))
            ts("dve", oh_i[:], oh_i[:], 15, ALU.bitwise_and, (t_pi,), joins=(t_pi,))
            oh = sb("oh_f", [128, 16], F32); t_pf = Tk()
            ts("dve", oh[:], oh_i[:], 0, ALU.is_equal, (t_pi,), writes=(t_pf,))
            invt = sb("invt", [128, 16], F32); t_it = Tk()
            for i in range(16):
                memset("pool", invt[:, i:i + 1], float(np.float32(10000.0) ** np.float32(-i / 16.0)), joins=(t_it,))
            tt("dve", oh[:], oh[:], invt[:], ALU.mult, (t_pf, t_it), joins=(t_pf,))
            inv = sb("inv", [128, 1], F32); t_inv = Tk()
            p.op("dve", lambda e: e.tensor_reduce(out=inv[:], in_=oh[:], axis=AX.X, op=ALU.add),
                 reads=(t_pf,), writes=(t_inv,))
            CH = min(S, 2048)
            posi = sb("posi", [128, CH], I32); t_pos = Tk()
            ang = sb("ang", [128, CH], F32); t_ang = Tk()
            kf = sb("kf", [128, CH], F32); t_kf = Tk()
            ki = sb("ki", [128, CH], I32); t_ki = Tk()
            rr = sb("rr", [128, CH], F32); t_rr = Tk()
            res = sb("ropres", [128, CH], F32); t_res = Tk()
            C1 = 6.28125
            C2 = TWO_PI - C1
            for c0 in range(0, S, CH):
                p.dma("sp", posi[:], pos_in[c0:c0 + CH].partition_broadcast(128), t_pos, writes=(t_pos,))
                cp("dve", ang[:], posi[:], (t_pos,), writes=(t_ang,))
                ts("dve", ang[:], ang[:], inv[:, 0:1], ALU.mult, (t_ang, t_inv), joins=(t_ang,))
                for which, dst in ((0, SIN), (1, COS)):
                    if which == 1:
                        ts("dve", kf[:], ang[:], math.pi / 2, ALU.add, (t_ang,), writes=(t_kf,))
                        src, t_src = kf, t_kf
                        cp("dve", rr[:], kf[:], (t_kf,), writes=(t_rr,))
                    else:
                        src, t_src = ang, t_ang
                        cp("dve", rr[:], ang[:], (t_ang,), writes=(t_rr,))
                    ts("dve", kf[:], rr[:], 1.0 / TWO_PI, ALU.mult, (t_rr,), writes=(t_kf,))
                    cp("dve", ki[:], kf[:], (t_kf,), writes=(t_ki,))
                    cp("dve", kf[:], ki[:], (t_ki,), writes=(t_kf,))
                    stt(rr[:], kf[:], -C1, rr[:], ALU.mult, ALU.add, (t_kf, t_rr), joins=(t_rr,))
                    stt(rr[:], kf[:], -C2, rr[:], ALU.mult, ALU.add, (t_kf, t_rr), joins=(t_rr,))
                    ts("dve", rr[:], rr[:], 3.1415925, ALU.min, (t_rr,), joins=(t_rr,),
                       s2=-3.1415925, op1=ALU.max)
                    act(res[:], rr[:], AF.Sin, (t_rr,), writes=(t_res,))
                    p.dma("sp", dst[:, c0:c0 + CH], res[:], t_res, reads=(t_res,))
            p.flush(es)

    def phase_X(l, b):
        with contextlib.ExitStack() as es:
            def sb(name, shape, dtype):
                return es.enter_context(nc.sbuf_tensor(uname(name), list(shape), dtype))
            bp = BankPool(list(range(8)))
            wst = sb("x_wst", [128, 8, 1024], F32); t_wst = Tk()
            wbf = sb("x_wbf", [128, 8, 1024], BF16); t_wbf = Tk()
            wsrc = w_mem_kv[l].rearrange("(kc p) n -> p kc n", p=128)
            for kc in range(8):
                p.dma("sp", wst[:, kc, :], wsrc[:, kc, :], t_wst, joins=(t_wst,))
            g_bc = pv[:, l, PV_MEM:PV_MEM + 8].unsqueeze(2).to_broadcast([128, 8, 1024])
            tt("dve", wbf[:], wst[:], g_bc, ALU.mult, (t_wst,), writes=(t_wbf,))
            mt_ = sb("x_m", [128, 2, D], F32); t_m = Tk()
            p.dma("sp", mt_[:], mem_in[b].rearrange("(t p) d -> p t d", p=128), t_m, writes=(t_m,))
            junk = sb("x_junk", [128, D], BF16); t_j = Tk()
            ss = sb("x_ss", [128, 2], F32); t_ss = Tk()
            for t in range(2):
                act(junk[:], mt_[:, t, :], AF.Square, (t_m,), writes=(t_j,), joins=(t_ss,), accum=ss[:, t:t + 1])
            lnv = sb("x_ln", [128, 2], F32); t_ln = Tk()
            act(lnv[:], ss[:], AF.Ln, (t_ss,), writes=(t_ln,), scale=1.0 / D, bias=EPS)
            rs = sb("x_rs", [128, 2], F32); t_rs = Tk()
            act(rs[:], lnv[:], AF.Exp, (t_ln,), writes=(t_rs,), scale=-0.5)
            mn = sb("x_mn", [128, 2, D], BF16); t_mn = Tk()
            for t in range(2):
                ts("dve", mn[:, t, :], mt_[:, t, :], rs[:, t:t + 1], ALU.mult, (t_m, t_rs), joins=(t_mn,))
            mT = sb("x_mT", [128, 8, MEM_T], BF16); t_mT = Tk()
            for t in range(2):
                bk, btk = bp.next()
                bv = bk[:, :].bitcast(BF16).rearrange("p (c t) -> p c t", c=8)
                for kc in range(8):
                    tr(bv[:, kc, :], mn[:, t, kc * 128:(kc + 1) * 128], ident_bf[:], (t_mn,), btk, kc == 0)
                cp("dve", mT[:, :, t * 128:(t + 1) * 128], bv, (btk,), joins=(t_mT,))
            t_kv = Tk()
            for h in range(4):
                bk, btk = bp.next()
                for kc in range(8):
                    mm(bk[:, 0:MEM_T], wbf[:, kc, h * 128:(h + 1) * 128], mT[:, kc, :], kc == 0, kc == 7,
                       (t_wbf, t_mT), btk)
                cp("act", kmT[:, h, :], bk[:, 0:MEM_T], (btk,), joins=(t_kv,))
            for t in range(2):
                bk, btk = bp.next()
                for kc in range(8):
                    mm(bk[:, :], mT[:, kc, t * 128:(t + 1) * 128], wbf[:, kc, 512:1024], kc == 0, kc == 7,
                       (t_wbf, t_mT), btk)
                cp("dve", vm[:, t, :, :], bk[:, :].rearrange("p (h d) -> p h d", h=4), (btk,), joins=(t_kv,))
            t_np = Tk()
            p.dma("sp", npost[:], norm_post[l].partition_broadcast(128), t_np, writes=(t_np,))
            p.flush(es)

    def phase_P(l, b):
        xsrc = x_in[b] if l == 0 else out[b]
        with contextlib.ExitStack() as pes:
            hT = pes.enter_context(nc.sbuf_tensor(uname("hT"), [128, 8, S], BF16))
            cosT = pes.enter_context(nc.sbuf_tensor(uname("cosT"), [128, S], F32))
            sinT = pes.enter_context(nc.sbuf_tensor(uname("sinT"), [128, S], F32))
            with contextlib.ExitStack() as es:
                def sb(name, shape, dtype):
                    return es.enter_context(nc.sbuf_tensor(uname(name), list(shape), dtype))
                t_tab = Tk()
                p.dma("sp", cosT[:], COS[:, :], t_tab, joins=(t_tab,))
                p.dma("sp", sinT[:], SIN[:, :], t_tab, joins=(t_tab,))
                bp = BankPool(list(range(8)))
                xr = Ring([sb("p1_x%d" % i, [128, D], F32) for i in range(3)])
                xnr = Ring([sb("p1_xn%d" % i, [128, D], BF16) for i in range(2)])
                junk = sb("p1_junk", [128, D], BF16); t_j = Tk()
                ssr = Ring([sb("p1_ss%d" % i, [128, 4], F32) for i in range(4)])
                t_hT = Tk()
                for i in range(NT):
                    xt, t_x = xr.next()
                    p.dma("sp", xt[:], xsrc[i * 128:(i + 1) * 128, :], t_x, writes=(t_x,))
                    sst, t_s = ssr.next()
                    act(junk[:], xt[:], AF.Square, (t_x,), writes=(t_j, t_s), accum=sst[:, 0:1])
                    act(sst[:, 1:2], sst[:, 0:1], AF.Ln, (t_s,), joins=(t_s,), scale=1.0 / D, bias=EPS)
                    act(sst[:, 2:3], sst[:, 1:2], AF.Exp, (t_s,), joins=(t_s,), scale=-0.5)
                    xn, t_xn = xnr.next()
                    ts("dve", xn[:], xt[:], sst[:, 2:3], ALU.mult, (t_x, t_s), writes=(t_xn,))
                    bk, btk = bp.next()
                    bv = bk[:, :].bitcast(BF16).rearrange("p (c t) -> p c t", c=8)
                    for kc in range(8):
                        tr(bv[:, kc, :], xn[:, kc * 128:(kc + 1) * 128], ident_bf[:], (t_xn,), btk, kc == 0)
                    cp("act" if i % 2 else "dve", hT[:, :, i * 128:(i + 1) * 128], bv, (btk,), joins=(t_hT,))
                p.flush(es)
            g_bc8 = pv[:, l, PV_NPRE:PV_NPRE + 8]
            if upto < 3:
                return
            def psb(name, shape, dtype):
                return pes.enter_context(nc.sbuf_tensor(uname(name), list(shape), dtype))
            NM = 416
            wm = psb("a_wm", [128, 8, NM], BF16); t_wm = Tk()
            wkrot = psb("a_wkrot", [128, 8, 96], BF16)
            uq = psb("a_uq", [128, 2, 768], BF16); t_uq = Tk()
            uqrot = psb("a_uqrot", [128, 2, 8, 96], BF16)
            ukv = psb("a_ukv", [128, 1024], BF16); t_ukv = Tk()
            with contextlib.ExitStack() as es:
                def sb(name, shape, dtype):
                    return es.enter_context(nc.sbuf_tensor(uname(name), list(shape), dtype))
                wst = sb("a_wst", [128, 8, NM], F32); t_wst = Tk()
                wsrc = w_in[l].rearrange("(kc p) n -> p kc n", p=128)
                for kc in range(8):
                    p.dma("sp", wst[:, kc, :], wsrc[:, kc, 0:NM], t_wst, joins=(t_wst,))
                tt("dve", wst[:], wst[:], g_bc8.unsqueeze(2).to_broadcast([128, 8, NM]), ALU.mult,
                   (t_wst,), joins=(t_wst,))
                cp("pool", wm[:], wst[:], (t_wst,), writes=(t_wm,))
                t_wk = Tk()
                memset("pool", wkrot[:, :, 0:64], 0.0, joins=(t_wk,))
                ts("pool", wkrot[:, :, 64:80], wst[:, :, 400:416], -1.0, ALU.mult, (t_wst,), joins=(t_wk,))
                cp("pool", wkrot[:, :, 80:96], wst[:, :, 384:400], (t_wst,), joins=(t_wk,))
                uqs = sb("a_uqs", [128, 2, 768], F32); t_uqs = Tk()
                p.dma("sp", uqs[:], w_uq[l].rearrange("(kc p) n -> p kc n", p=128), t_uqs, writes=(t_uqs,))
                tt("dve", uqs[:], uqs[:], pv[:, l, PV_QN:PV_QN + 2].unsqueeze(2).to_broadcast([128, 2, 768]),
                   ALU.mult, (t_uqs,), joins=(t_uqs,))
                cp("pool", uq[:], uqs[:], (t_uqs,), writes=(t_uq,))
                t_ur = Tk()
                memset("pool", uqrot[:, :, :, 0:64], 0.0, joins=(t_ur,))
                uqs4 = uqs[:].rearrange("p k (h c) -> p k h c", h=8)
                ts("pool", uqrot[:, :, :, 64:80], uqs4[:, :, :, 80:96], -1.0, ALU.mult, (t_uqs,), joins=(t_ur,))
                cp("pool", uqrot[:, :, :, 80:96], uqs4[:, :, :, 64:80], (t_uqs,), joins=(t_ur,))
                ukvs = sb("a_ukvs", [128, 1024], F32); t_ukvs = Tk()
                p.dma("sp", ukvs[:], w_ukv[l], t_ukvs, writes=(t_ukvs,))
                ts("dve", ukv[:], ukvs[:], pv[:, l, PV_KVN:PV_KVN + 1], ALU.mult, (t_ukvs,), writes=(t_ukv,))
                p.flush(es)
            with contextlib.ExitStack() as es:
                def sb(name, shape, dtype):
                    return es.enter_context(nc.sbuf_tensor(uname(name), list(shape), dtype))
                bp = BankPool(list(range(8)))
                ukv_v = ukv[:].rearrange("p (h two c) -> p h two c", h=8, two=2)[:, :, 1, :]
                sqr = Ring([sb("a_sq%d" % i, [128, 3, 512], BF16) for i in range(2)])
                rsr = Ring([sb("a_rs%d" % i, [128, 2, 512], F32) for i in range(1)])
                lnr = Ring([sb("a_ln%d" % i, [128, 2, 512], F32) for i in range(1)])
                cqnr = Ring([sb("a_cqn%d" % i, [128, 3, 512], BF16) for i in range(2)])
                t1r = Ring([sb("a_t1%d" % i, [128, 512], F32) for i in range(2)])
                t2r = Ring([sb("a_t2%d" % i, [128, 512], F32) for i in range(2)])
                kper = Ring([sb("a_kpe%d" % i, [128, 512], BF16) for i in range(2)])
                qst = Ring([sb("a_qst%d" % i, [96, 8, 512], BF16) for i in range(2)])
                kst = Ring([sb("a_kst%d" % i, [96, 8, 512], BF16) for i in range(2)])
                vst = Ring([sb("a_vst%d" % i, [128, 8, 128], BF16) for i in range(3)])
                for (vt, t_v) in vst.items:
                    memset("pool", vt[:, :, 64:128], 1.0, joins=(t_v,))
                QMv = QM.rearrange("h d t -> d h t")
                KMv = KM.rearrange("h d t -> d h t")
                cqfr = Ring([sb("a_cqf%d" % i, [128, 3, 512], F32) for i in range(2)])
                sta = {}

                def stA(g):
                    if g >= NG:
                        return
                    gs = slice(g * 512, (g + 1) * 512)
                    cb = []
                    for (c0, m) in ((0, 128), (128, 128), (256, 128), (320, 96)):
                        bk, btk = bp.next()
                        for kc in range(8):
                            mm(bk[0:m, :], wm[:, kc, c0:c0 + m], hT[:, kc, gs], kc == 0, kc == 7, (t_wm,), btk)
                        cb.append((bk, btk))
                    bkB, btkB = bp.next()
                    for kc in range(8):
                        mm(bkB[0:96, :], wkrot[:, kc, :], hT[:, kc, gs], kc == 0, kc == 7, (t_wm, t_wk), btkB)
                    sq, t_sq = sqr.next()
                    cqf, t_cqf = cqfr.next()
                    for i in range(3):
                        act(sq[:, i, :], cb[i][0][:, :], AF.Square, (cb[i][1],),
                            **({"writes": (t_sq,)} if i == 0 else {"joins": (t_sq,)}))
                        cp("act", cqf[:, i, :], cb[i][0][:, :], (cb[i][1],),
                           **({"writes": (t_cqf,)} if i == 0 else {"joins": (t_cqf,)}))
                    t1, t_1 = t1r.next(); t2, t_2 = t2r.next()
                    tt("dve", t1[64:96, :], cb[3][0][64:96, :], cosT[64:96, gs], ALU.mult, (cb[3][1], t_tab), writes=(t_1,))
                    tt("dve", t2[64:96, :], bkB[64:96, :], sinT[64:96, gs], ALU.mult, (btkB, t_tab), writes=(t_2,))
                    kpe, t_kpe = kper.next()
                    tt("pool", kpe[64:96, :], t1[64:96, :], t2[64:96, :], ALU.add, (t_1, t_2), writes=(t_kpe,))
                    ks, t_ks = kst.next()
                    cp("act", ks[64:96, :, :], kpe[64:96, :].unsqueeze(1).to_broadcast([32, 8, 512]),
                       (t_kpe,), writes=(t_ks,))
                    sta[g] = dict(sq=(sq, t_sq), cqf=(cqf, t_cqf), ks=(ks, t_ks))

                def stB(g):
                    d_ = sta[g]
                    sq, t_sq = d_["sq"]; cqf, t_cqf = d_["cqf"]
                    bq, btq = bp.next()
                    mm(bq[:, :], ones_bf[:], sq[:, 0, :], True, False, (t_sq,), btq)
                    mm(bq[:, :], ones_bf[:], sq[:, 1, :], False, True, (t_sq,), btq)
                    bkv, btkv = bp.next()
                    mm(bkv[:, :], ones_bf[:], sq[:, 2, :], True, True, (t_sq,), btkv)
                    lnv, t_ln = lnr.next()
                    act(lnv[:, 0, :], bq[:, :], AF.Ln, (btq,), writes=(t_ln,), scale=1.0 / 256, bias=EPS)
                    act(lnv[:, 1, :], bkv[:, :], AF.Ln, (btkv,), joins=(t_ln,), scale=1.0 / 128, bias=EPS)
                    rs, t_rs = rsr.next()
                    act(rs[:], lnv[:], AF.Exp, (t_ln,), writes=(t_rs,), scale=-0.5)
                    cqn, t_cqn = cqnr.next()
                    tt("dve", cqn[:, 0, :], cqf[:, 0, :], rs[:, 0, :], ALU.mult, (t_cqf, t_rs), writes=(t_cqn,))
                    tt("dve", cqn[:, 1, :], cqf[:, 1, :], rs[:, 0, :], ALU.mult, (t_cqf, t_rs), joins=(t_cqn,))
                    tt("dve", cqn[:, 2, :], cqf[:, 2, :], rs[:, 1, :], ALU.mult, (t_cqf, t_rs), joins=(t_cqn,))
                    d_["cqn"] = (cqn, t_cqn)

                def stC(g):
                    gs = slice(g * 512, (g + 1) * 512)
                    d_ = sta.pop(g)
                    cqn, t_cqn = d_["cqn"]; ks, t_ks = d_["ks"]
                    qs, t_qs = qst.next()
                    for h in range(8):
                        bA, btA = bp.next()
                        for kc in range(2):
                            mm(bA[0:96, :], uq[:, kc, h * 96:(h + 1) * 96], cqn[:, kc, :], kc == 0, kc == 1,
                               (t_uq, t_cqn), btA)
                        bB, btB = bp.next()
                        for kc in range(2):
                            mm(bB[0:96, :], uqrot[:, kc, h, :], cqn[:, kc, :], kc == 0, kc == 1,
                               (t_uq, t_ur, t_cqn), btB)
                        kwq = {"writes": (t_qs,)} if h == 0 else {"joins": (t_qs,)}
                        cp("act", qs[0:64, h, :], bA[0:64, :], (btA,), **kwq)
                        t1, t_1 = t1r.next(); t2, t_2 = t2r.next()
                        tt("dve", t1[64:96, :], bA[64:96, :], cosT[64:96, gs], ALU.mult, (btA, t_tab), writes=(t_1,))
                        tt("dve", t2[64:96, :], bB[64:96, :], sinT[64:96, gs], ALU.mult, (btB, t_tab), writes=(t_2,))
                        tt("pool", qs[64:96, h, :], t1[64:96, :], t2[64:96, :], ALU.add, (t_1, t_2), joins=(t_qs,))
                        bK, btK = bp.next()
                        mm(bK[:, :], ukv[:, h * 128:(h + 1) * 128], cqn[:, 2, :], True, True, (t_ukv, t_cqn), btK)
                        cp("act", ks[0:64, h, :], bK[0:64, :], (btK,), joins=(t_ks,))
                    p.dma("sp", QMv[:, :, gs], qs[:], t_qs, reads=(t_qs,))
                    p.dma("sp", KMv[:, :, gs], ks[:], t_ks, reads=(t_ks,))
                    for t in range(4):
                        bV, btV = bp.next()
                        mm(bV[:, :], cqn[:, 2, t * 128:(t + 1) * 128], ukv_v, True, True, (t_ukv, t_cqn), btV)
                        vt, t_v = vst.next()
                        cp("dve", vt[:, :, 0:64], bV[:, :].rearrange("p (h c) -> p h c", c=64), (btV,), joins=(t_v,))
                        r0 = (g * 4 + t) * 128
                        p.dma("sp", VM[r0:r0 + 128, :].rearrange("p (h c) -> p h c", c=128), vt[:], t_v, reads=(t_v,))

                stA(0)
                for g in range(NG):
                    stB(g)
                    stA(g + 1)
                    stC(g)
                p.flush(es)
            if upto < 4:
                return
            with contextlib.ExitStack() as es:
                def sb(name, shape, dtype):
                    return es.enter_context(nc.sbuf_tensor(uname(name), list(shape), dtype))
                bp = BankPool(list(range(8)))
                wsrc = w_in[l].rearrange("(kc p) n -> p kc n", p=128)
                wvs = sb("b_wvs", [128, 8, 512], F32); t_wvs = Tk()
                for kc in range(8):
                    p.dma("sp", wvs[:, kc, :], wsrc[:, kc, OFF_VD:OFF_VD + 512], t_wvs, joins=(t_wvs,))
                wv = sb("b_wv", [128, 8, 512], BF16); t_wv = Tk()
                tt("dve", wv[:], wvs[:], g_bc8.unsqueeze(2).to_broadcast([128, 8, 512]), ALU.mult,
                   (t_wvs,), writes=(t_wv,))
                vst = Ring([sb("b_vst%d" % i, [128, 8, 128], BF16) for i in range(3)])
                for (vt, t_v) in vst.items:
                    memset("pool", vt[:, :, 64:128], 1.0, joins=(t_v,))
                for i in range(NT):
                    bk, btk = bp.next()
                    for kc in range(8):
                        mm(bk[:, :], hT[:, kc, i * 128:(i + 1) * 128], wv[:, kc, :], kc == 0, kc == 7, (t_wv,), btk)
                    vt, t_v = vst.next()
                    cp("dve" if i % 2 else "act", vt[:, :, 0:64], bk[:, :].rearrange("p (h c) -> p h c", c=64),
                       (btk,), joins=(t_v,))
                    p.dma("sp", VD[i * 128:(i + 1) * 128, :].rearrange("p (h c) -> p h c", c=128), vt[:], t_v,
                          reads=(t_v,))
                chunks = []
                for j in range(4):
                    chunks.append((OFF_QD + j * 128, "rope", QD[j * 128:(j + 1) * 128, :], None))
                for j in range(4):
                    chunks.append((OFF_KD + j * 128, "rope", KD[j * 128:(j + 1) * 128, :], None))
                for j in range(4):
                    chunks.append((OFF_QX + j * 128, "copy", QX[j * 128:(j + 1) * 128, :], None))
                for j in range(12):
                    chunks.append((OFF_SIL + j * 128, "silu", SIL[j * 128:(j + 1) * 128, :], None))
                for j in range(24):
                    chunks.append((OFF_GL + j * 128, "sig", SG[j * 128:(j + 1) * 128, :], PV_BG + j))
                wstr = Ring([sb("b_wst%d" % i, [128, 8, 128], F32) for i in range(2)])
                wbfr = Ring([sb("b_wbf%d" % i, [128, 8, 128], BF16) for i in range(3)])
                wrotr = Ring([sb("b_wrot%d" % i, [128, 8, 128], BF16) for i in range(2)])
                stg = Ring([sb("b_stg%d" % i, [128, S], BF16) for i in range(2)])
                t1r = Ring([sb("b_t1%d" % i, [128, 512], F32) for i in range(2)])
                t2r = Ring([sb("b_t2%d" % i, [128, 512], F32) for i in range(2)])
                gb = g_bc8.unsqueeze(2).to_broadcast([128, 8, 128])
                wloaded = {}

                def wload(ci):
                    if ci < len(chunks) and ci not in wloaded:
                        ws, t_ws = wstr.next()
                        c0_ = chunks[ci][0]
                        p.dma("sp", ws[:], wsrc[:, :, c0_:c0_ + 128], t_ws, writes=(t_ws,))
                        wloaded[ci] = (ws, t_ws)
                wload(0)
                for ci, (c0, kind, dst, bcol) in enumerate(chunks):
                    ws, t_ws = wloaded[ci]
                    wb, t_wb = wbfr.next()
                    if kind == "rope":
                        tt("dve", ws[:], ws[:], gb, ALU.mult, (t_ws,), joins=(t_ws,))
                        cp("pool", wb[:], ws[:], (t_ws,), writes=(t_wb,))
                        wr, t_wr = wrotr.next()
                        ws5 = ws[:].rearrange("p k (h two c) -> p k h two c", h=4, two=2)
                        wr5 = wr[:].rearrange("p k (h two c) -> p k h two c", h=4, two=2)
                        ts("pool", wr5[:, :, :, 0, :], ws5[:, :, :, 1, :], -1.0, ALU.mult, (t_ws,), writes=(t_wr,))
                        cp("pool", wr5[:, :, :, 1, :], ws5[:, :, :, 0, :], (t_ws,), joins=(t_wr,))
                    else:
                        tt("dve", wb[:], ws[:], gb, ALU.mult, (t_ws,), writes=(t_wb,))
                    wload(ci + 1)
                    st, t_st = stg.next()
                    for g in range(NG):
                        gs = slice(g * 512, (g + 1) * 512)
                        bk, btk = bp.next()
                        for kc in range(8):
                            mm(bk[:, :], wb[:, kc, :], hT[:, kc, gs], kc == 0, kc == 7, (t_wb,), btk)
                        kw = {"writes": (t_st,)} if g == 0 else {"joins": (t_st,)}
                        if kind == "rope":
                            bk2, btk2 = bp.next()
                            for kc in range(8):
                                mm(bk2[:, :], wr[:, kc, :], hT[:, kc, gs], kc == 0, kc == 7, (t_wr,), btk2)
                            t1, t_1 = t1r.next(); t2, t_2 = t2r.next()
                            tt("dve", t1[:], bk[:, :], cosT[:, gs], ALU.mult, (btk, t_tab), writes=(t_1,))
                            tt("dve", t2[:], bk2[:, :], sinT[:, gs], ALU.mult, (btk2, t_tab), writes=(t_2,))
                            tt("pool", st[:, gs], t1[:], t2[:], ALU.add, (t_1, t_2), **kw)
                        elif kind == "copy":
                            cp("act" if g % 2 else "dve", st[:, gs], bk[:, :], (btk,), **kw)
                        elif kind == "silu":
                            act(st[:, gs], bk[:, :], AF.Silu, (btk,), **kw)
                        else:
                            act(st[:, gs], bk[:, :], AF.Sigmoid, (btk,), bias=pv[:, l, bcol:bcol + 1], **kw)
                    p.dma("sp", dst, st[:], t_st, reads=(t_st,))
                p.flush(es)

    def phase_attn(l, b, kind):
        mla = kind == "mla"
        dk = 96 if mla else 32
        nmap = 1 if mla else 2
        scale = float(dk) ** -0.5
        Vsrc = VM if mla else VD
        Odst = OA if mla else OB
        LAG = 3
        with contextlib.ExitStack() as es:
            def sb(name, shape, dtype):
                return es.enter_context(nc.sbuf_tensor(uname(name), list(shape), dtype))
            class PairPool:
                def __init__(self):
                    self.items = [(psum_all[:, 0:2, :], Tk()), (psum_all[:, 2:4, :], Tk())]
                    self.i = 0

                def next(self):
                    it = self.items[self.i % 2]
                    self.i += 1
                    return it
            spool = PairPool()
            opool = BankPool([4, 5, 6, 7])
            Vsb = sb("at_V", [128, NT, 1024], BF16); t_V = Tk()
            vsrc = Vsrc.rearrange("(kt p) c -> p kt c", p=128)
            VCH = 8
            for k0 in range(0, NT, VCH):
                k1 = min(NT, k0 + VCH)
                p.dma("sp", Vsb[:, k0:k1, :], vsrc[:, k0:k1, :], t_V, joins=(t_V,))
            dkp = 96 if mla else 128
            qr = [Ring([sb("at_q%d_%d" % (m, i), [dkp, S], BF16) for i in range(2)]) for m in range(nmap)]
            kr = [Ring([sb("at_k%d_%d" % (m, i), [dkp, S], BF16) for i in range(2)]) for m in range(nmap)]
            if not mla:
                for rg in qr + kr:
                    for (t_, tk_) in rg.items:
                        memset("pool", t_[32:64, :], 0.0, joins=(tk_,))
                        memset("pool", t_[64:128, :], 0.0, joins=(tk_,))
            pr_ = Ring([sb("at_p%d" % i, [128, 2, 512], BF16) for i in range(LAG + 3)])
            ostr = Ring([sb("at_o%d" % i, [64, S], BF16) for i in range(2)])
            rcr = Ring([sb("at_rc%d" % i, [64, nmap, 512], F32) for i in range(2)])
            if not mla:
                tnr = Ring([sb("at_tn%d" % i, [64, 2, 512], F32) for i in range(2)])
                ddr = Ring([sb("at_d%d" % i, [64, 512], F32) for i in range(4)])
                sqr = Ring([sb("at_sq%d" % i, [128, 512], BF16) for i in range(2)])
                for (t_, tk_) in sqr.items:
                    memset("pool", t_[64:128, :], 0.0, joins=(tk_,))
                lnr = Ring([sb("at_ln%d" % i, [64, 512], F32) for i in range(2)])
                rsr = Ring([sb("at_rs%d" % i, [64, 512], F32) for i in range(2)])
            pend = []
            ticks = []

            def tick():
                if ticks:
                    for f_ in ticks.pop(0):
                        f_()

            def defer(fn_):
                if not ticks:
                    ticks.append([])
                ticks[0].append(fn_)

            def drain(n):
                while len(pend) > n:
                    fn, after = pend.pop(0)
                    fn()
                    if after is not None:
                        after()

            def flush_part2():
                while ticks:
                    tick()

            qk_loaded = {}

            def qkload(h):
                if h >= 8 or h in qk_loaded:
                    return
                qt = []; kt_ = []
                for m in range(nmap):
                    q_, t_q = qr[m].next(); k_, t_k = kr[m].next()
                    if mla:
                        p.dma("sp", q_[:], QM[h], t_q, writes=(t_q,))
                        p.dma("sp", k_[:], KM[h], t_k, writes=(t_k,))
                    else:
                        r0 = (h * 2 + m) * 32
                        p.dma("sp", q_[0:32, :], QD[r0:r0 + 32, :], t_q, joins=(t_q,))
                        p.dma("sp", k_[0:32, :], KD[r0:r0 + 32, :], t_k, joins=(t_k,))
                    qt.append((q_, t_q)); kt_.append((k_, t_k))
                qk_loaded[h] = (qt, kt_)
            qkload(0)
            for h in range(8):
                qt, kt_ = qk_loaded[h]
                qkload(h + 1)
                ost, t_o = ostr.next()
                vh = Vsb[:, :, :]
                for g in range(NG):
                    obk = [opool.next() for _ in range(nmap)]
                    nkt = 4 * g + 4
                    jobs = []
                    for kt in range(nkt):
                        j = kt - 4 * g
                        q0 = 128 * j if j > 0 else 0
                        for m in range(nmap):
                            jobs.append((m, kt, q0, 512 - q0, j >= 0))
                    groups = []
                    ji = 0
                    while ji < len(jobs):
                        if (ji + 1 < len(jobs) and jobs[ji + 1][3] == jobs[ji][3] and jobs[ji + 1][4] == jobs[ji][4]
                                and (nmap == 2 or not jobs[ji][4])):
                            groups.append([jobs[ji], jobs[ji + 1]]); ji += 2
                        else:
                            groups.append([jobs[ji]]); ji += 1
                    for grp in groups:
                        pslot, pst = spool.next()
                        pt, t_p = pr_.next()
                        ngr = len(grp)
                        nq = grp[0][3]
                        for idx, (m, kt, q0, _nq, dg) in enumerate(grp):
                            q_, t_q = qt[m]; k_, t_k = kt_[m]
                            mm(pslot[:, idx, 0:nq], k_[:, kt * 128:(kt + 1) * 128],
                               q_[:, g * 512 + q0:(g + 1) * 512], True, True, (t_q, t_k), pst, first=(idx == 0))
                        act(pt[:, 0:ngr, 0:nq], pslot[:, 0:ngr, 0:nq], AF.Exp, (pst,), writes=(t_p,), scale=scale)
                        if grp[0][4]:
                            tt("pool", pt[:, 0:ngr, 0:128], pt[:, 0:ngr, 0:128],
                               mask_bf[:].unsqueeze(1).to_broadcast([128, ngr, 128]), ALU.mult, (t_p,), joins=(t_p,))
                        for idx, (m, kt, q0, _nq, dg) in enumerate(grp):
                            ob, obt = obk[m]
                            last = (kt == nkt - 1) and (m == nmap - 1)
                            lw = Vsb[:, kt, h * 128:(h + 1) * 128]

                            def pvfn(ob=ob, obt=obt, pt=pt, t_p=t_p, kt=kt, q0=q0, nq=nq, nkt=nkt, lw=lw, idx=idx):
                                mm(ob[:, q0:512], lw, pt[:, idx, 0:nq], kt == 0, kt == nkt - 1, (t_V, t_p), obt)
                            after = None
                            if last:
                                def after(obk=obk, g=g, ost=ost, t_o=t_o, h=h):
                                    tick()
                                    gs = slice(g * 512, (g + 1) * 512)
                                    rc, t_rc = rcr.next()
                                    for m2 in range(nmap):
                                        recip(rc[:, m2, :], obk[m2][0][64:128, :], (obk[m2][1],),
                                              **({"writes": (t_rc,)} if m2 == 0 else {"joins": (t_rc,)}))
                                    kwo = {"writes": (t_o,)} if g == 0 else {"joins": (t_o,)}
                                    if mla:
                                        tt("dve", ost[:, gs], obk[0][0][0:64, :], rc[:, 0, :], ALU.mult,
                                           (obk[0][1], t_rc), **kwo)
                                    else:
                                        tn, t_tn = tnr.next()
                                        tt("dve", tn[:, 0, :], obk[0][0][0:64, :], rc[:, 0, :], ALU.mult,
                                           (obk[0][1], t_rc), writes=(t_tn,))
                                        tt("dve", tn[:, 1, :], obk[1][0][0:64, :], rc[:, 1, :], ALU.mult,
                                           (obk[1][1], t_rc), joins=(t_tn,))
                                        dd, t_d = ddr.next()
                                        stt(dd[:], tn[:, 1, :], neglam[0:64, l:l + 1], tn[:, 0, :], ALU.mult, ALU.add,
                                            (t_tn,), writes=(t_d,))

                                        def stB(dd=dd, t_d=t_d, gs=gs, kwo=kwo, g=g, ost=ost, t_o=t_o, h=h):
                                            sq, t_sq = sqr.next()
                                            act(sq[0:64, :], dd[:], AF.Square, (t_d,), joins=(t_sq,))
                                            sbp2, sbt2 = spool.next()
                                            sbk2 = sbp2[:, 0, :]
                                            mm(sbk2[:, :], ones_bf[:], sq[:], True, True, (t_sq,), sbt2)

                                            if True:
                                                lnv, t_ln = lnr.next()
                                                act(lnv[:], sbk2[0:64, :], AF.Ln, (sbt2,), writes=(t_ln,),
                                                    scale=1.0 / 64, bias=EPS)
                                                rs, t_rs = rsr.next()
                                                act(rs[:], lnv[:], AF.Exp, (t_ln,), writes=(t_rs,), scale=-0.5)
                                                stt(ost[:, gs], dd[:], gsub[0:64, l:l + 1], rs[:], ALU.mult, ALU.mult,
                                                    (t_d, t_rs), **kwo)
                                                if g == NG - 1:
                                                    p.dma("sp", Odst[h * 64:(h + 1) * 64, :], ost[:], t_o, reads=(t_o,))
                                        defer(stB)
                                    if mla and g == NG - 1:
                                        p.dma("sp", Odst[h * 64:(h + 1) * 64, :], ost[:], t_o, reads=(t_o,))
                            pend.append((pvfn, after))
                            drain(LAG)
            drain(0)
            flush_part2()
            p.flush(es)

    def phase_M(l, b):
        T = 256
        NGM = S // T
        xsrc = x_in[b] if l == 0 else out[b]
        with contextlib.ExitStack() as es:
            def sb(name, shape, dtype):
                return es.enter_context(nc.sbuf_tensor(uname(name), list(shape), dtype))
            pa = BankPool([0, 1, 2, 3, 4, 5, 6, 7])
            wbr = sb("m_wbr", [128, 3, 4, D], BF16); t_wbr = Tk()
            wo = sb("m_wo", [128, 8, D], BF16); t_wo = Tk()
            wstr = Ring([sb("m_wst%d" % i, [128, 2, D], F32) for i in range(2)])
            for i in range(3):
                wbsrc = w_branch[l, i].rearrange("(kc p) n -> p kc n", p=128)
                for j in range(2):
                    ws, t_ws = wstr.next()
                    p.dma("sp", ws[:], wbsrc[:, j * 2:(j + 1) * 2, :], t_ws, writes=(t_ws,))
                    cp("act", wbr[:, i, j * 2:(j + 1) * 2, :], ws[:], (t_ws,), joins=(t_wbr,))
            wosrc = w_out[l].rearrange("(kc p) n -> p kc n", p=128)
            for i in range(4):
                ws, t_ws = wstr.next()
                p.dma("sp", ws[:], wosrc[:, i * 2:(i + 1) * 2, :], t_ws, writes=(t_ws,))
                cp("act", wo[:, i * 2:(i + 1) * 2, :], ws[:], (t_ws,), joins=(t_wo,))
            oar = Ring([sb("m_oa%d" % i, [128, 4, T], BF16) for i in range(2)])
            obr = Ring([sb("m_ob%d" % i, [128, 4, T], BF16) for i in range(2)])
            qxr = Ring([sb("m_qx%d" % i, [128, 4, T], BF16) for i in range(2)])
            silr = Ring([sb("m_sil%d" % i, [128, 12, T], BF16) for i in range(2)])
            sgr = Ring([sb("m_sg%d" % i, [128, 24, T], BF16) for i in range(2)])
            xr = Ring([sb("m_x%d" % i, [128, T // 128, D], F32) for i in range(2)])
            gtr = Ring([sb("m_gt%d" % i, [128, 12, T], BF16) for i in range(2)])
            mgr = Ring([sb("m_mg%d" % i, [128, 8, T], BF16) for i in range(2)])
            ptr_ = Ring([sb("m_pt%d" % i, [128, T], BF16) for i in range(4)])
            rcr = Ring([sb("m_rc%d" % i, [128, T], F32) for i in range(2)])
            lxr = Ring([sb("m_lx%d" % i, [128, T], F32) for i in range(2)])
            ocr = Ring([sb("m_oc%d" % i, [128, T], F32) for i in range(2)])
            tmr = Ring([sb("m_tm%d" % i, [128, 3, 2 * T], F32) for i in range(2)])
            t4r = Ring([sb("m_t4%d" % i, [128, 2 * T], F32) for i in range(2)])
            ssr = Ring([sb("m_ss%d" % i, [128, 8], F32) for i in range(4)])
            junk = sb("m_junk", [128, 512], BF16); t_j = Tk()
            rtr = Ring([sb("m_rt%d" % i, [128, D], F32) for i in range(2)])
            rer = Ring([sb("m_re%d" % i, [128, D], F32) for i in range(2)])
            t_glob = Tk()
            xscale = 128.0 ** -0.5
            stt_ = {}

            def loadA(g):
                if g >= NGM:
                    return
                gs = slice(g * T, (g + 1) * T)
                oa, t_oa = oar.next(); ob_, t_ob = obr.next(); qx, t_qx = qxr.next(); sil, t_sil = silr.next()
                p.dma("sp", qx[:], QX.rearrange("(h p) t -> p h t", p=128)[:, :, gs], t_qx, writes=(t_qx,))
                p.dma("sp", oa[:], OA.rearrange("(c p) t -> p c t", p=128)[:, :, gs], t_oa, writes=(t_oa,))
                p.dma("sp", ob_[:], OB.rearrange("(c p) t -> p c t", p=128)[:, :, gs], t_ob, writes=(t_ob,))
                p.dma("sp", sil[:], SIL.rearrange("(c p) t -> p c t", p=128)[:, :, gs], t_sil, writes=(t_sil,))
                stt_.setdefault(g, {}).update(oa=(oa, t_oa), ob=(ob_, t_ob), qx=(qx, t_qx), sil=(sil, t_sil))

            def loadB(g):
                if g >= NGM:
                    return
                gs = slice(g * T, (g + 1) * T)
                sg, t_sg = sgr.next()
                p.dma("sp", sg[:], SG.rearrange("(c p) t -> p c t", p=128)[:, :, gs], t_sg, writes=(t_sg,))
                stt_.setdefault(g, {}).update(sg=(sg, t_sg))

            def loadC(g):
                if g >= NGM or g < 0:
                    return
                xt, t_x = xr.next()
                p.dma("sp", xt[:], xsrc[g * T:(g + 1) * T, :].rearrange("(t p) d -> p t d", p=128), t_x,
                      writes=(t_x,))
                stt_.setdefault(g, {}).update(x=(xt, t_x))

            def stage1(g):
                if g >= NGM:
                    return
                d_ = stt_[g]
                oa, t_oa = d_["oa"]; ob_, t_ob = d_["ob"]; qx, t_qx = d_["qx"]; sil, t_sil = d_["sil"]
                gt, t_gt = gtr.next()
                d_["gt"] = (gt, t_gt)
                tt("pool", gt[:, 0:4, :], oa[:], sil[:, 0:4, :], ALU.mult, (t_oa, t_sil), writes=(t_gt,))
                tt("pool", gt[:, 4:8, :], ob_[:], sil[:, 4:8, :], ALU.mult, (t_ob, t_sil), joins=(t_gt,))
                for h in range(4):
                    ob2, obt2 = pa.next()
                    pts = []
                    for mt in range(2):
                        sbk, sbt = pa.next()
                        mm(sbk[:, 0:T], kmT[:, h, mt * 128:(mt + 1) * 128], qx[:, h, :], True, True, (t_qx, t_glob), sbt)
                        pt, t_p = ptr_.next()
                        act(pt[:], sbk[:, 0:T], AF.Exp, (sbt,), writes=(t_p,), scale=xscale)
                        pts.append((pt, t_p))
                    for mt in range(2):
                        pt, t_p = pts[mt]
                        mm(ob2[:, 0:T], vm[:, mt, h, :], pt[:], mt == 0, mt == 1, (t_p, t_glob), obt2, first=(mt == 0))
                    for mt in range(2):
                        pt, t_p = pts[mt]
                        mm(ob2[:, T:2 * T], ones_bf[:], pt[:], mt == 0, mt == 1, (t_p,), obt2, first=False)
                    rc, t_rc = rcr.next()
                    lx, t_lx = lxr.next()
                    act(lx[:], ob2[:, T:2 * T], AF.Ln, (obt2,), writes=(t_lx,))
                    act(rc[:], lx[:], AF.Exp, (t_lx,), writes=(t_rc,), scale=-1.0)
                    oc, t_oc = ocr.next()
                    tt("dve", oc[:], ob2[:, 0:T], rc[:], ALU.mult, (obt2, t_rc), writes=(t_oc,))
                    tt("pool", gt[:, 8 + h, :], oc[:], sil[:, 8 + h, :], ALU.mult, (t_oc, t_sil), joins=(t_gt,))

            def stage2(g):
                if g >= NGM or g < 0:
                    return
                d_ = stt_[g]
                gt, t_gt = d_["gt"]; sg, t_sg = d_["sg"]
                mg, t_mg = mgr.next()
                d_["mg"] = (mg, t_mg)
                for cp_ in range(4):
                    ybk = []
                    for i in range(3):
                        yb, ybt = pa.next()
                        for half in range(2):
                            cc = cp_ * 2 + half
                            for kc in range(4):
                                mm(yb[:, half * T:(half + 1) * T], wbr[:, i, kc, cc * 128:(cc + 1) * 128],
                                   gt[:, i * 4 + kc, :], kc == 0, kc == 3, (t_wbr, t_gt), ybt,
                                   first=(kc == 0 and half == 0))
                        ybk.append((yb, ybt))
                    tm, t_tm = tmr.next()
                    for i in range(3):
                        sgv = sg[:, i * 8 + cp_ * 2:i * 8 + cp_ * 2 + 2, :]
                        tt("dve", tm[:, i, :].rearrange("p (a t) -> p a t", a=2),
                           ybk[i][0][:, 0:2 * T].rearrange("p (a t) -> p a t", a=2), sgv, ALU.mult,
                           (ybk[i][1], t_sg), **({"writes": (t_tm,)} if i == 0 else {"joins": (t_tm,)}))
                    t4, t_t4 = t4r.next()
                    tt("pool", t4[:], tm[:, 0, :], tm[:, 1, :], ALU.add, (t_tm,), writes=(t_t4,))
                    tt("pool", mg[:, cp_ * 2:cp_ * 2 + 2, :], t4[:].rearrange("p (a t) -> p a t", a=2),
                       tm[:, 2, :].rearrange("p (a t) -> p a t", a=2), ALU.add, (t_t4, t_tm),
                       **({"writes": (t_mg,)} if cp_ == 0 else {"joins": (t_mg,)}))

            def stage3(g):
                if g >= NGM or g < 0:
                    return
                d_ = stt_.pop(g)
                mg, t_mg = d_["mg"]; xt, t_x = d_["x"]
                for t in range(T // 128):
                    bks = []
                    for half in range(2):
                        bk, btk = pa.next()
                        for kc in range(8):
                            mm(bk[:, :], mg[:, kc, t * 128:(t + 1) * 128], wo[:, kc, half * 512:(half + 1) * 512],
                               kc == 0, kc == 7, (t_mg, t_wo), btk)
                        bks.append((bk, btk))
                    sst, t_s = ssr.next()
                    act(junk[:], bks[0][0][:, :], AF.Square, (bks[0][1],), writes=(t_j, t_s), accum=sst[:, 0:1])
                    act(junk[:], bks[1][0][:, :], AF.Square, (bks[1][1],), writes=(t_j,), joins=(t_s,),
                        accum=sst[:, 1:2])
                    tt("dve", sst[:, 2:3], sst[:, 0:1], sst[:, 1:2], ALU.add, (t_s,), joins=(t_s,))
                    act(sst[:, 3:4], sst[:, 2:3], AF.Ln, (t_s,), joins=(t_s,), scale=1.0 / D, bias=EPS)
                    act(sst[:, 4:5], sst[:, 3:4], AF.Exp, (t_s,), joins=(t_s,), scale=-0.5)
                    rt, t_rt = rtr.next()
                    for half in range(2):
                        hs = slice(half * 512, (half + 1) * 512)
                        stt(rt[:, hs], bks[half][0][:, :], sst[:, 4:5], npost[:, hs], ALU.mult, ALU.mult,
                            (bks[half][1], t_s, t_glob), **({"writes": (t_rt,)} if half == 0 else {"joins": (t_rt,)}))
                    re_, t_re = rer.next()
                    tt("dve", re_[:], rt[:], xt[:, t, :], ALU.add, (t_rt, t_x), writes=(t_re,))
                    r0 = g * T + t * 128
                    p.dma("sp", out[b, r0:r0 + 128, :], re_[:], t_re, reads=(t_re,))

            loadA(0); loadA(1); loadB(0)
            stage1(0)
            for i in range(NGM + 1):
                loadA(i + 2); loadB(i + 1); loadC(i)
                stage1(i + 1)
                stage2(i)
                stage3(i - 1)
            p.flush(es)

    phase_setup()
    for l in range(L):
        for b in range(NB):
            if upto >= 1:
                phase_X(l, b)
            if upto >= 2:
                phase_P(l, b)
            if upto >= 5:
                phase_attn(l, b, "mla")
            if upto >= 6:
                phase_attn(l, b, "diff")
            if upto >= 7:
                phase_M(l, b)
    ges.close()
    return nc, p


_CACHE = {}


def kernel(**inputs):
    n = 8
    if "nc" not in _CACHE:
        _CACHE["nc"] = build()[0]
    nc = _CACHE["nc"]
    names = ["norm_pre", "norm_post", "w_in", "q_norm", "w_uq", "kv_norm", "w_ukv",
             "lambda_q1", "lambda_k1", "lambda_q2", "lambda_k2", "diff_subln",
             "mem_norm", "w_mem_kv", "b_gate", "w_branch", "w_out"]
    x = np.ascontiguousarray(inputs["x"], dtype=np.float32)
    mem = np.ascontiguousarray(inputs["mem"], dtype=np.float32)
    pos = np.ascontiguousarray(inputs["positions"], dtype=np.int32)
    shared = {k: np.ascontiguousarray(inputs[k], dtype=np.float32) for k in names}
    in_maps = []
    for c in range(n):
        m = {"x": x[2 * c:2 * c + 2], "mem": mem[2 * c:2 * c + 2], "positions": pos}
        m.update(shared)
        in_maps.append(m)
    res = run_bass_kernel_spmd(nc, in_maps, core_ids=list(range(n)))
    return np.concatenate([np.asarray(r["out"], dtype=np.float32) for r in res.results], axis=0)
```

```python
import math
import contextlib
import numpy as np
import concourse.bass as bass
import concourse.mybir as mybir
from concourse.bass_utils import run_bass_kernel_spmd
from concourse.alu_op_type import AluOpType as ALU

F32 = mybir.dt.float32
BF16 = mybir.dt.bfloat16
I32 = mybir.dt.int32
AF = mybir.ActivationFunctionType
AX = mybir.AxisListType

D = 1024
IN_W = 7072
EPS = 1e-6
OFF_CQ, OFF_CKV, OFF_KPE, OFF_QD, OFF_KD, OFF_VD, OFF_QX, OFF_SIL, OFF_GL = (
    0, 256, 384, 416, 928, 1440, 1952, 2464, 4000)
PV_NPRE, PV_MEM, PV_QN, PV_KVN, PV_BG, PV_SUB, PV_N = 0, 8, 16, 18, 19, 43, 44
MEM_T = 256
TWO_PI = 2.0 * math.pi


class Tk:
    __slots__ = ("base", "w", "r", "joined", "sem", "name")

    def __init__(self, name=""):
        self.base = []
        self.w = []
        self.r = []
        self.joined = False
        self.sem = None
        self.name = name


def _compact(lst):
    out = []
    last = {}
    for i in lst:
        if i.is_dma:
            out.append(i)
        else:
            last[i.eng] = i
    out.extend(last.values())
    return out


class Ins:
    __slots__ = ("eng", "fn", "deps", "is_dma", "sem", "val", "needed", "phase")


ENGS = ("pe", "act", "dve", "pool", "sp")
BLK = {"pe": "tensor", "act": "scalar", "dve": "vector", "pool": "gpsimd", "sp": "sync"}


class Prog:
    def __init__(self, nc):
        self.nc = nc
        self.esem = {e: nc.alloc_semaphore("pg_" + e) for e in ("pe", "act", "dve", "pool")}
        self.ecnt = {e: 0 for e in self.esem}
        self.free_sems = []
        self.semcnt = {}
        self.waited = {e: {} for e in ENGS}
        self.phase = 0
        self.n_ins = 0
        self.begin()

    def begin(self):
        self.phase += 1
        self.q = {e: [] for e in ENGS}
        self.dma_slots = []

    def _deps_of(self, ins, tks_r, tks_w, tks_j):
        deps = []
        for t in tks_r:
            deps.extend(t.w)
        for t in tks_w:
            deps.extend(t.base); deps.extend(t.w); deps.extend(t.r)
        for t in tks_j:
            deps.extend(t.base)
            if not (t.joined and not t.r):
                deps.extend(t.w); deps.extend(t.r)
        out = []
        for d in deps:
            if d.phase != self.phase or d is ins:
                continue
            if (not d.is_dma) and (not ins.is_dma) and d.eng == ins.eng and d.eng == "pe":
                continue
            if not d.is_dma:
                d.needed = True
            out.append(d)
        return out

    def _update(self, ins, tks_r, tks_w, tks_j):
        for t in tks_w:
            t.base = []; t.w = [ins]; t.r = []; t.joined = False
        for t in tks_j:
            if t.joined and not t.r:
                t.w.append(ins)
                if len(t.w) > 8:
                    t.w = _compact(t.w)
            else:
                t.base = _compact(t.w + t.r)
                t.w = [ins]; t.r = []; t.joined = True
        for t in tks_r:
            t.r.append(ins)
            if len(t.r) > 8:
                t.r = _compact(t.r)

    def op(self, eng, fn, reads=(), writes=(), joins=()):
        ins = Ins()
        ins.eng = eng; ins.fn = fn; ins.is_dma = False; ins.needed = False
        ins.phase = self.phase; ins.sem = None; ins.val = 0
        ins.deps = self._deps_of(ins, reads, writes, joins)
        self._update(ins, reads, writes, joins)
        self.q[eng].append(ins)
        return ins

    def dma(self, eng, out, in_, slot, reads=(), writes=(), joins=()):
        ins = Ins()
        ins.eng = eng; ins.is_dma = True; ins.needed = True
        ins.phase = self.phase
        ins.fn = lambda e: e.dma_start(out=out, in_=in_)
        if slot.sem is None:
            slot.sem = self.free_sems.pop() if self.free_sems else self.nc.alloc_semaphore(
                "dq%d" % len(self.semcnt))
            self.semcnt.setdefault(slot.sem, 0)
            self.dma_slots.append(slot)
        self.semcnt[slot.sem] += 16
        ins.sem = slot.sem; ins.val = self.semcnt[slot.sem]
        ins.deps = self._deps_of(ins, reads, writes, joins)
        self._update(ins, reads, writes, joins)
        self.q[eng].append(ins)
        return ins

    def flush(self, es):
        nc = self.nc
        for e in ("pe", "act", "dve", "pool"):
            for ins in self.q[e]:
                if ins.needed and not ins.is_dma:
                    self.ecnt[e] += 1
                    ins.sem = self.esem[e]; ins.val = self.ecnt[e]
        block = es.enter_context(nc.Block())
        final_waits = [(s.sem, self.semcnt[s.sem]) for s in self.dma_slots]
        for e in ENGS:
            q = self.q[e]
            waited = self.waited[e]
            fw = final_waits if e == "sp" else ()
            self.n_ins += len(q)

            def body(eng, q=q, waited=waited, fw=fw):
                for ins in q:
                    need = {}
                    for d in ins.deps:
                        if d.val > need.get(d.sem, 0):
                            need[d.sem] = d.val
                    for s, v in need.items():
                        if waited.get(s, 0) < v:
                            eng.wait_ge(s, v)
                            waited[s] = v
                    h = ins.fn(eng)
                    if ins.is_dma:
                        h.then_inc(ins.sem, 16)
                    elif ins.needed:
                        h.then_inc(ins.sem, 1)
                for s, v in fw:
                    if waited.get(s, 0) < v:
                        eng.wait_ge(s, v)
                        waited[s] = v
            getattr(block, BLK[e])(body)
        for s in self.dma_slots:
            self.free_sems.append(s.sem)
            s.sem = None
        self.begin()


class Ring:
    def __init__(self, items):
        self.items = [(t, Tk()) for t in items]
        self.i = 0

    def next(self):
        it = self.items[self.i % len(self.items)]
        self.i += 1
        return it


def build(S=4096, L=4, NB=2, dbg=False, upto=99):
    NT = S // 128
    NG = S // 512
    lam_init = [0.8 - 0.6 * math.exp(-0.3 * l) for l in range(L)]
    nc = bass.Bass("TRN2", target_bir_lowering=False)
    dt = nc.dram_tensor

    def din(name, shape, dtype=F32):
        return dt(name, list(shape), dtype, kind="ExternalInput").ap()

    x_in = din("x", [NB, S, D])
    mem_in = din("mem", [NB, MEM_T, D])
    pos_in = din("positions", [S], I32)
    norm_pre = din("norm_pre", [L, D]); norm_post = din("norm_post", [L, D])
    w_in = din("w_in", [L, D, IN_W])
    q_norm = din("q_norm", [L, 256]); w_uq = din("w_uq", [L, 256, 768])
    kv_norm = din("kv_norm", [L, 128]); w_ukv = din("w_ukv", [L, 128, 1024])
    lq1 = din("lambda_q1", [L, 32]); lk1 = din("lambda_k1", [L, 32])
    lq2 = din("lambda_q2", [L, 32]); lk2 = din("lambda_k2", [L, 32])
    diff_subln = din("diff_subln", [L, 64])
    mem_norm = din("mem_norm", [L, D]); w_mem_kv = din("w_mem_kv", [L, D, 1024])
    b_gate = din("b_gate", [L, 3 * D])
    w_branch = din("w_branch", [L, 3, 512, D]); w_out = din("w_out", [L, D, D])
    out = dt("out", [NB, S, D], F32, kind="ExternalOutput").ap()

    skind = "ExternalOutput" if dbg else "Internal"

    def dsc(name, shape, dtype=BF16):
        return dt(name, list(shape), dtype, kind=skind).ap()

    QM = dsc("s_qm", [8, 96, S]); KM = dsc("s_km", [8, 96, S]); VM = dsc("s_vm", [S, 1024])
    QD = dsc("s_qd", [512, S]); KD = dsc("s_kd", [512, S]); VD = dsc("s_vd", [S, 1024])
    QX = dsc("s_qx", [512, S]); SIL = dsc("s_sil", [1536, S]); SG = dsc("s_sg", [3072, S])
    OA = dsc("s_oa", [512, S]); OB = dsc("s_ob", [512, S])
    COS = dsc("s_cos", [128, S], F32); SIN = dsc("s_sin", [128, S], F32)

    p = Prog(nc)
    ges = contextlib.ExitStack()
    _uc = [0]

    def uname(name):
        _uc[0] += 1
        return "%s_%d" % (name, _uc[0])

    def gsb(name, shape, dtype):
        return ges.enter_context(nc.sbuf_tensor(name, list(shape), dtype))

    ident_bf = gsb("ident_bf", [128, 128], BF16)
    ident_f = gsb("ident_f", [128, 128], F32)
    mask_bf = gsb("mask_bf", [128, 128], BF16)
    ones_bf = gsb("ones_bf", [128, 128], BF16)
    pv = gsb("pv", [128, L, PV_N], F32)
    gsub = gsb("gsub", [128, L], F32)
    neglam = gsb("neglam", [128, L], F32)
    kmT = gsb("kmT", [128, 4, MEM_T], BF16)
    vm = gsb("vm", [128, 2, 4, 128], BF16)
    npost = gsb("npost", [128, D], F32)
    banks = [ges.enter_context(nc.psum_tensor("bank%d" % i, [128, 512], F32)) for i in range(8)]
    bank_tk = [Tk("bank%d" % i) for i in range(8)]

    class BankPool:
        def __init__(self, ids):
            self.ids = ids; self.i = 0

        def next(self):
            b = self.ids[self.i % len(self.ids)]
            self.i += 1
            return banks[b], bank_tk[b]

    def mm(outap, lhsT, rhs, start, stop, reads, bk, first=None):
        if first is None:
            first = start
        kw = {"writes": (bk,)} if first else {"joins": (bk,)}
        p.op("pe", lambda e: e.matmul(outap, lhsT=lhsT, rhs=rhs, start=start, stop=stop),
             reads=reads, **kw)

    def tr(outap, in_, ident, reads, bk, first):
        kw = {"writes": (bk,)} if first else {"joins": (bk,)}
        p.op("pe", lambda e: e.transpose(outap, in_, ident), reads=reads, **kw)

    def act(outap, in_, func, reads, writes=(), joins=(), scale=None, bias=None, accum=None):
        kw = {}
        if scale is not None:
            kw["scale"] = scale
        if bias is not None:
            kw["bias"] = bias
        if accum is not None:
            kw["accum_out"] = accum
        p.op("act", lambda e: e.activation(out=outap, in_=in_, func=func, **kw),
             reads=reads, writes=writes, joins=joins)

    def tt(eng, outap, a, b, op, reads, writes=(), joins=()):
        p.op(eng, lambda e: e.tensor_tensor(out=outap, in0=a, in1=b, op=op),
             reads=reads, writes=writes, joins=joins)

    def ts(eng, outap, a, s1, op0, reads, writes=(), joins=(), s2=None, op1=None):
        if op1 is None:
            p.op(eng, lambda e: e.tensor_scalar(out=outap, in0=a, scalar1=s1, scalar2=None, op0=op0),
                 reads=reads, writes=writes, joins=joins)
        else:
            p.op(eng, lambda e: e.tensor_scalar(out=outap, in0=a, scalar1=s1, scalar2=s2,
                                                op0=op0, op1=op1),
                 reads=reads, writes=writes, joins=joins)

    def cp(eng, outap, in_, reads, writes=(), joins=()):
        if eng == "act":
            p.op("act", lambda e: e.copy(out=outap, in_=in_), reads=reads, writes=writes, joins=joins)
        else:
            p.op(eng, lambda e: e.tensor_copy(out=outap, in_=in_), reads=reads, writes=writes,
                 joins=joins)

    def stt(outap, a, scalar, b, op0, op1, reads, writes=(), joins=()):
        p.op("dve", lambda e: e.scalar_tensor_tensor(out=outap, in0=a, scalar=scalar, in1=b,
                                                     op0=op0, op1=op1),
             reads=reads, writes=writes, joins=joins)

    def recip(outap, in_, reads, writes=(), joins=()):
        p.op("dve", lambda e: e.reciprocal(out=outap, in_=in_), reads=reads, writes=writes,
             joins=joins)

    def memset(eng, ap, val, writes=(), joins=()):
        p.op(eng, lambda e: e.memset(ap, val), writes=writes, joins=joins)

    def phase_setup():
        with contextlib.ExitStack() as es:
            def sb(name, shape, dtype):
                return es.enter_context(nc.sbuf_tensor(uname(name), list(shape), dtype))
            iot = sb("iot", [128, 128], I32); t_iot = Tk()
            p.op("pool", lambda e: e.iota(iot[:], pattern=[[1, 128]], base=0, channel_multiplier=-1),
                 writes=(t_iot,))
            t_c = Tk()
            ts("dve", ident_bf[:], iot[:], 0, ALU.is_equal, (t_iot,), joins=(t_c,))
            ts("dve", ident_f[:], iot[:], 0, ALU.is_equal, (t_iot,), joins=(t_c,))
            ts("dve", mask_bf[:], iot[:], 0, ALU.is_ge, (t_iot,), joins=(t_c,))
            memset("dve", ones_bf[:], 1.0, joins=(t_c,))
            rows = sb("rows", [PV_N, L, 128], F32); t_rows = Tk()
            memset("dve", rows[:], 0.0, writes=(t_rows,))

            def ld(dst, src):
                p.dma("sp", dst, src, t_rows, joins=(t_rows,))
            for l in range(L):
                ld(rows[PV_NPRE:PV_NPRE + 8, l, :], norm_pre[l].rearrange("(c p) -> c p", p=128))
                ld(rows[PV_MEM:PV_MEM + 8, l, :], mem_norm[l].rearrange("(c p) -> c p", p=128))
                ld(rows[PV_QN:PV_QN + 2, l, :], q_norm[l].rearrange("(c p) -> c p", p=128))
                ld(rows[PV_KVN:PV_KVN + 1, l, :], kv_norm[l].rearrange("(c p) -> c p", p=128))
                ld(rows[PV_BG:PV_BG + 24, l, :], b_gate[l].rearrange("(c p) -> c p", p=128))
                ld(rows[PV_SUB:PV_SUB + 1, l, 0:64], diff_subln[l].rearrange("(c p) -> c p", p=64))
            t_pv = Tk()
            for l in range(L):
                bk, btk = banks[l % 8], bank_tk[l % 8]
                tr(bk[:, 0:PV_N], rows[:, l, :], ident_f[0:PV_N, 0:PV_N], (t_rows, t_c), btk, True)
                cp("dve", pv[:, l, :], bk[:, 0:PV_N], (btk,), joins=(t_pv,))
                ts("dve", gsub[:, l:l + 1], pv[:, l, PV_SUB:PV_SUB + 1], 1.0 - lam_init[l], ALU.mult,
                   (t_pv,), joins=(t_pv,))
            lam4 = [sb("lam%d" % i, [128, L, 32], F32) for i in range(4)]
            t_l = Tk()
            for i, src in enumerate((lq1, lk1, lq2, lk2)):
                p.dma("sp", lam4[i][:], src.partition_broadcast(128), t_l, joins=(t_l,))
            pr = sb("lampr", [128, 2, L, 32], F32); t_pr = Tk()
            tt("dve", pr[:, 0], lam4[0][:], lam4[1][:], ALU.mult, (t_l,), joins=(t_pr,))
            tt("dve", pr[:, 1], lam4[2][:], lam4[3][:], ALU.mult, (t_l,), joins=(t_pr,))
            sm = sb("lamsm", [128, 2, L], F32); t_sm = Tk()
            p.op("dve", lambda e: e.tensor_reduce(out=sm[:], in_=pr[:], axis=AX.X, op=ALU.add),
                 reads=(t_pr,), writes=(t_sm,))
            ex = sb("lamex", [128, 2, L], F32); t_ex = Tk()
            act(ex[:], sm[:], AF.Exp, (t_sm,), writes=(t_ex,))
            t_nl = Tk()
            tt("dve", neglam[:], ex[:, 1], ex[:, 0], ALU.subtract, (t_ex,), writes=(t_nl,))
            for l in range(L):
                ts("dve", neglam[:, l:l + 1], neglam[:, l:l + 1], -lam_init[l], ALU.add, (t_nl,),
                   joins=(t_nl,))
            oh_i = sb("oh_i", [128, 16], I32); t_pi = Tk()
            p.op("pool", lambda e: e.iota(oh_i[:], pattern=[[1, 16]], base=0, channel_multiplier=-1),
                 writes=(t_pi,))
            ts("dve", oh_i[:], oh_i[:], 15, ALU.bitwise_and, (t_pi,), joins=(t_pi,))
            oh = sb("oh_f", [128, 16], F32); t_pf = Tk()
            ts("dve", oh[:], oh_i[:], 0, ALU.is_equal, (t_pi,), writes=(t_pf,))
            invt = sb("invt", [128, 16], F32); t_it = Tk()
            for i in range(16):
                memset("pool", invt[:, i:i + 1], float(np.float32(10000.0) ** np.float32(-i / 16.0)), joins=(t_it,))
            tt("dve", oh[:], oh[:], invt[:], ALU.mult, (t_pf, t_it), joins=(t_pf,))
            inv = sb("inv", [128, 1], F32); t_inv = Tk()
            p.op("dve", lambda e: e.tensor_reduce(out=inv[:], in_=oh[:], axis=AX.X, op=ALU.add),
                 reads=(t_pf,), writes=(t_inv,))
            CH = min(S, 2048)
            posi = sb("posi", [128, CH], I32); t_pos = Tk()
            ang = sb("ang", [128, CH], F32); t_ang = Tk()
            kf = sb("kf", [128, CH], F32); t_kf = Tk()
            ki = sb("ki", [128, CH], I32); t_ki = Tk()
            rr = sb("rr", [128, CH], F32); t_rr = Tk()
            res = sb("ropres", [128, CH], F32); t_res = Tk()
            C1 = 6.28125
            C2 = TWO_PI - C1
            for c0 in range(0, S, CH):
                p.dma("sp", posi[:], pos_in[c0:c0 + CH].partition_broadcast(128), t_pos, writes=(t_pos,))
                cp("dve", ang[:], posi[:], (t_pos,), writes=(t_ang,))
                ts("dve", ang[:], ang[:], inv[:, 0:1], ALU.mult, (t_ang, t_inv), joins=(t_ang,))
                for which, dst in ((0, SIN), (1, COS)):
                    if which == 1:
                        ts("dve", kf[:], ang[:], math.pi / 2, ALU.add, (t_ang,), writes=(t_kf,))
                        src, t_src = kf, t_kf
                        cp("dve", rr[:], kf[:], (t_kf,), writes=(t_rr,))
                    else:
                        src, t_src = ang, t_ang
                        cp("dve", rr[:], ang[:], (t_ang,), writes=(t_rr,))
                    ts("dve", kf[:], rr[:], 1.0 / TWO_PI, ALU.mult, (t_rr,), writes=(t_kf,))
                    cp("dve", ki[:], kf[:], (t_kf,), writes=(t_ki,))
                    cp("dve", kf[:], ki[:], (t_ki,), writes=(t_kf,))
                    stt(rr[:], kf[:], -C1, rr[:], ALU.mult, ALU.add, (t_kf, t_rr), joins=(t_rr,))
                    stt(rr[:], kf[:], -C2, rr[:], ALU.mult, ALU.add, (t_kf, t_rr), joins=(t_rr,))
                    ts("dve", rr[:], rr[:], 3.1415925, ALU.min, (t_rr,), joins=(t_rr,),
                       s2=-3.1415925, op1=ALU.max)
                    act(res[:], rr[:], AF.Sin, (t_rr,), writes=(t_res,))
                    p.dma("sp", dst[:, c0:c0 + CH], res[:], t_res, reads=(t_res,))
            p.flush(es)

    def phase_X(l, b):
        with contextlib.ExitStack() as es:
            def sb(name, shape, dtype):
                return es.enter_context(nc.sbuf_tensor(uname(name), list(shape), dtype))
            bp = BankPool(list(range(8)))
            wst = sb("x_wst", [128, 8, 1024], F32); t_wst = Tk()
            wbf = sb("x_wbf", [128, 8, 1024], BF16); t_wbf = Tk()
            wsrc = w_mem_kv[l].rearrange("(kc p) n -> p kc n", p=128)
            for kc in range(8):
                p.dma("sp", wst[:, kc, :], wsrc[:, kc, :], t_wst, joins=(t_wst,))
            g_bc = pv[:, l, PV_MEM:PV_MEM + 8].unsqueeze(2).to_broadcast([128, 8, 1024])
            tt("dve", wbf[:], wst[:], g_bc, ALU.mult, (t_wst,), writes=(t_wbf,))
            mt_ = sb("x_m", [128, 2, D], F32); t_m = Tk()
            p.dma("sp", mt_[:], mem_in[b].rearrange("(t p) d -> p t d", p=128), t_m, writes=(t_m,))
            junk = sb("x_junk", [128, D], BF16); t_j = Tk()
            ss = sb("x_ss", [128, 2], F32); t_ss = Tk()
            for t in range(2):
                act(junk[:], mt_[:, t, :], AF.Square, (t_m,), writes=(t_j,), joins=(t_ss,), accum=ss[:, t:t + 1])
            lnv = sb("x_ln", [128, 2], F32); t_ln = Tk()
            act(lnv[:], ss[:], AF.Ln, (t_ss,), writes=(t_ln,), scale=1.0 / D, bias=EPS)
            rs = sb("x_rs", [128, 2], F32); t_rs = Tk()
            act(rs[:], lnv[:], AF.Exp, (t_ln,), writes=(t_rs,), scale=-0.5)
            mn = sb("x_mn", [128, 2, D], BF16); t_mn = Tk()
            for t in range(2):
                ts("dve", mn[:, t, :], mt_[:, t, :], rs[:, t:t + 1], ALU.mult, (t_m, t_rs), joins=(t_mn,))
            mT = sb("x_mT", [128, 8, MEM_T], BF16); t_mT = Tk()
            for t in range(2):
                bk, btk = bp.next()
                bv = bk[:, :].bitcast(BF16).rearrange("p (c t) -> p c t", c=8)
                for kc in range(8):
                    tr(bv[:, kc, :], mn[:, t, kc * 128:(kc + 1) * 128], ident_bf[:], (t_mn,), btk, kc == 0)
                cp("dve", mT[:, :, t * 128:(t + 1) * 128], bv, (btk,), joins=(t_mT,))
            t_kv = Tk()
            for h in range(4):
                bk, btk = bp.next()
                for kc in range(8):
                    mm(bk[:, 0:MEM_T], wbf[:, kc, h * 128:(h + 1) * 128], mT[:, kc, :], kc == 0, kc == 7,
                       (t_wbf, t_mT), btk)
                cp("act", kmT[:, h, :], bk[:, 0:MEM_T], (btk,), joins=(t_kv,))
            for t in range(2):
                bk, btk = bp.next()
                for kc in range(8):
                    mm(bk[:, :], mT[:, kc, t * 128:(t + 1) * 128], wbf[:, kc, 512:1024], kc == 0, kc == 7,
                       (t_wbf, t_mT), btk)
                cp("dve", vm[:, t, :, :], bk[:, :].rearrange("p (h d) -> p h d", h=4), (btk,), joins=(t_kv,))
            t_np = Tk()
            p.dma("sp", npost[:], norm_post[l].partition_broadcast(128), t_np, writes=(t_np,))
            p.flush(es)

    def phase_P(l, b):
        xsrc = x_in[b] if l == 0 else out[b]
        with contextlib.ExitStack() as pes:
            hT = pes.enter_context(nc.sbuf_tensor(uname("hT"), [128, 8, S], BF16))
            cosT = pes.enter_context(nc.sbuf_tensor(uname("cosT"), [128, S], F32))
            sinT = pes.enter_context(nc.sbuf_tensor(uname("sinT"), [128, S], F32))
            with contextlib.ExitStack() as es:
                def sb(name, shape, dtype):
                    return es.enter_context(nc.sbuf_tensor(uname(name), list(shape), dtype))
                t_tab = Tk()
                p.dma("sp", cosT[:], COS[:, :], t_tab, joins=(t_tab,))
                p.dma("sp", sinT[:], SIN[:, :], t_tab, joins=(t_tab,))
                bp = BankPool(list(range(8)))
                xr = Ring([sb("p1_x%d" % i, [128, D], F32) for i in range(3)])
                xnr = Ring([sb("p1_xn%d" % i, [128, D], BF16) for i in range(2)])
                junk = sb("p1_junk", [128, D], BF16); t_j = Tk()
                ssr = Ring([sb("p1_ss%d" % i, [128, 4], F32) for i in range(4)])
                t_hT = Tk()
                for i in range(NT):
                    xt, t_x = xr.next()
                    p.dma("sp", xt[:], xsrc[i * 128:(i + 1) * 128, :], t_x, writes=(t_x,))
                    sst, t_s = ssr.next()
                    act(junk[:], xt[:], AF.Square, (t_x,), writes=(t_j, t_s), accum=sst[:, 0:1])
                    act(sst[:, 1:2], sst[:, 0:1], AF.Ln, (t_s,), joins=(t_s,), scale=1.0 / D, bias=EPS)
                    act(sst[:, 2:3], sst[:, 1:2], AF.Exp, (t_s,), joins=(t_s,), scale=-0.5)
                    xn, t_xn = xnr.next()
                    ts("dve", xn[:], xt[:], sst[:, 2:3], ALU.mult, (t_x, t_s), writes=(t_xn,))
                    bk, btk = bp.next()
                    bv = bk[:, :].bitcast(BF16).rearrange("p (c t) -> p c t", c=8)
                    for kc in range(8):
                        tr(bv[:, kc, :], xn[:, kc * 128:(kc + 1) * 128], ident_bf[:], (t_xn,), btk, kc == 0)
                    cp("act" if i % 2 else "dve", hT[:, :, i * 128:(i + 1) * 128], bv, (btk,), joins=(t_hT,))
                p.flush(es)
            g_bc8 = pv[:, l, PV_NPRE:PV_NPRE + 8]
            if upto < 3:
                return
            def psb(name, shape, dtype):
                return pes.enter_context(nc.sbuf_tensor(uname(name), list(shape), dtype))
            NM = 416
            wm = psb("a_wm", [128, 8, NM], BF16); t_wm = Tk()
            wkrot = psb("a_wkrot", [128, 8, 96], BF16)
            uq = psb("a_uq", [128, 2, 768], BF16); t_uq = Tk()
            uqrot = psb("a_uqrot", [128, 2, 8, 96], BF16)
            ukv = psb("a_ukv", [128, 1024], BF16); t_ukv = Tk()
            with contextlib.ExitStack() as es:
                def sb(name, shape, dtype):
                    return es.enter_context(nc.sbuf_tensor(uname(name), list(shape), dtype))
                wst = sb("a_wst", [128, 8, NM], F32); t_wst = Tk()
                wsrc = w_in[l].rearrange("(kc p) n -> p kc n", p=128)
                for kc in range(8):
                    p.dma("sp", wst[:, kc, :], wsrc[:, kc, 0:NM], t_wst, joins=(t_wst,))
                tt("dve", wst[:], wst[:], g_bc8.unsqueeze(2).to_broadcast([128, 8, NM]), ALU.mult,
                   (t_wst,), joins=(t_wst,))
                cp("pool", wm[:], wst[:], (t_wst,), writes=(t_wm,))
                t_wk = Tk()
                memset("pool", wkrot[:, :, 0:64], 0.0, joins=(t_wk,))
                ts("pool", wkrot[:, :, 64:80], wst[:, :, 400:416], -1.0, ALU.mult, (t_wst,), joins=(t_wk,))
                cp("pool", wkrot[:, :, 80:96], wst[:, :, 384:400], (t_wst,), joins=(t_wk,))
                uqs = sb("a_uqs", [128, 2, 768], F32); t_uqs = Tk()
                p.dma("sp", uqs[:], w_uq[l].rearrange("(kc p) n -> p kc n", p=128), t_uqs, writes=(t_uqs,))
                tt("dve", uqs[:], uqs[:], pv[:, l, PV_QN:PV_QN + 2].unsqueeze(2).to_broadcast([128, 2, 768]),
                   ALU.mult, (t_uqs,), joins=(t_uqs,))
                cp("pool", uq[:], uqs[:], (t_uqs,), writes=(t_uq,))
                t_ur = Tk()
                memset("pool", uqrot[:, :, :, 0:64], 0.0, joins=(t_ur,))
                uqs4 = uqs[:].rearrange("p k (h c) -> p k h c", h=8)
                ts("pool", uqrot[:, :, :, 64:80], uqs4[:, :, :, 80:96], -1.0, ALU.mult, (t_uqs,), joins=(t_ur,))
                cp("pool", uqrot[:, :, :, 80:96], uqs4[:, :, :, 64:80], (t_uqs,), joins=(t_ur,))
                ukvs = sb("a_ukvs", [128, 1024], F32); t_ukvs = Tk()
                p.dma("sp", ukvs[:], w_ukv[l], t_ukvs, writes=(t_ukvs,))
                ts("dve", ukv[:], ukvs[:], pv[:, l, PV_KVN:PV_KVN + 1], ALU.mult, (t_ukvs,), writes=(t_ukv,))
                p.flush(es)
            with contextlib.ExitStack() as es:
                def sb(name, shape, dtype):
                    return es.enter_context(nc.sbuf_tensor(uname(name), list(shape), dtype))
                bp = BankPool(list(range(8)))
                ukv_v = ukv[:].rearrange("p (h two c) -> p h two c", h=8, two=2)[:, :, 1, :]
                sqr = Ring([sb("a_sq%d" % i, [128, 3, 512], BF16) for i in range(2)])
                rsr = Ring([sb("a_rs%d" % i, [128, 2, 512], F32) for i in range(1)])
                lnr = Ring([sb("a_ln%d" % i, [128, 2, 512], F32) for i in range(1)])
                cqnr = Ring([sb("a_cqn%d" % i, [128, 3, 512], BF16) for i in range(2)])
                t1r = Ring([sb("a_t1%d" % i, [128, 512], F32) for i in range(2)])
                t2r = Ring([sb("a_t2%d" % i, [128, 512], F32) for i in range(2)])
                kper = Ring([sb("a_kpe%d" % i, [128, 512], BF16) for i in range(2)])
                qst = Ring([sb("a_qst%d" % i, [96, 8, 512], BF16) for i in range(2)])
                kst = Ring([sb("a_kst%d" % i, [96, 8, 512], BF16) for i in range(2)])
                vst = Ring([sb("a_vst%d" % i, [128, 8, 128], BF16) for i in range(3)])
                for (vt, t_v) in vst.items:
                    memset("pool", vt[:, :, 64:128], 1.0, joins=(t_v,))
                QMv = QM.rearrange("h d t -> d h t")
                KMv = KM.rearrange("h d t -> d h t")
                cqfr = Ring([sb("a_cqf%d" % i, [128, 3, 512], F32) for i in range(2)])
                sta = {}

                def stA(g):
                    if g >= NG:
                        return
                    gs = slice(g * 512, (g + 1) * 512)
                    cb = []
                    for (c0, m) in ((0, 128), (128, 128), (256, 128), (320, 96)):
                        bk, btk = bp.next()
                        for kc in range(8):
                            mm(bk[0:m, :], wm[:, kc, c0:c0 + m], hT[:, kc, gs], kc == 0, kc == 7, (t_wm,), btk)
                        cb.append((bk, btk))
                    bkB, btkB = bp.next()
                    for kc in range(8):
                        mm(bkB[0:96, :], wkrot[:, kc, :], hT[:, kc, gs], kc == 0, kc == 7, (t_wm, t_wk), btkB)
                    sq, t_sq = sqr.next()
                    cqf, t_cqf = cqfr.next()
                    for i in range(3):
                        act(sq[:, i, :], cb[i][0][:, :], AF.Square, (cb[i][1],),
                            **({"writes": (t_sq,)} if i == 0 else {"joins": (t_sq,)}))
                        cp("act", cqf[:, i, :], cb[i][0][:, :], (cb[i][1],),
                           **({"writes": (t_cqf,)} if i == 0 else {"joins": (t_cqf,)}))
                    t1, t_1 = t1r.next(); t2, t_2 = t2r.next()
                    tt("dve", t1[64:96, :], cb[3][0][64:96, :], cosT[64:96, gs], ALU.mult, (cb[3][1], t_tab), writes=(t_1,))
                    tt("dve", t2[64:96, :], bkB[64:96, :], sinT[64:96, gs], ALU.mult, (btkB, t_tab), writes=(t_2,))
                    kpe, t_kpe = kper.next()
                    tt("pool", kpe[64:96, :], t1[64:96, :], t2[64:96, :], ALU.add, (t_1, t_2), writes=(t_kpe,))
                    ks, t_ks = kst.next()
                    cp("act", ks[64:96, :, :], kpe[64:96, :].unsqueeze(1).to_broadcast([32, 8, 512]),
                       (t_kpe,), writes=(t_ks,))
                    sta[g] = dict(sq=(sq, t_sq), cqf=(cqf, t_cqf), ks=(ks, t_ks))

                def stB(g):
                    d_ = sta[g]
                    sq, t_sq = d_["sq"]; cqf, t_cqf = d_["cqf"]
                    bq, btq = bp.next()
                    mm(bq[:, :], ones_bf[:], sq[:, 0, :], True, False, (t_sq,), btq)
                    mm(bq[:, :], ones_bf[:], sq[:, 1, :], False, True, (t_sq,), btq)
                    bkv, btkv = bp.next()
                    mm(bkv[:, :], ones_bf[:], sq[:, 2, :], True, True, (t_sq,), btkv)
                    lnv, t_ln = lnr.next()
                    act(lnv[:, 0, :], bq[:, :], AF.Ln, (btq,), writes=(t_ln,), scale=1.0 / 256, bias=EPS)
                    act(lnv[:, 1, :], bkv[:, :], AF.Ln, (btkv,), joins=(t_ln,), scale=1.0 / 128, bias=EPS)
                    rs, t_rs = rsr.next()
                    act(rs[:], lnv[:], AF.Exp, (t_ln,), writes=(t_rs,), scale=-0.5)
                    cqn, t_cqn = cqnr.next()
                    tt("dve", cqn[:, 0, :], cqf[:, 0, :], rs[:, 0, :], ALU.mult, (t_cqf, t_rs), writes=(t_cqn,))
                    tt("dve", cqn[:, 1, :], cqf[:, 1, :], rs[:, 0, :], ALU.mult, (t_cqf, t_rs), joins=(t_cqn,))
                    tt("dve", cqn[:, 2, :], cqf[:, 2, :], rs[:, 1, :], ALU.mult, (t_cqf, t_rs), joins=(t_cqn,))
                    d_["cqn"] = (cqn, t_cqn)

                def stC(g):
                    gs = slice(g * 512, (g + 1) * 512)
                    d_ = sta.pop(g)
                    cqn, t_cqn = d_["cqn"]; ks, t_ks = d_["ks"]
                    qs, t_qs = qst.next()
                    for h in range(8):
                        bA, btA = bp.next()
                        for kc in range(2):
                            mm(bA[0:96, :], uq[:, kc, h * 96:(h + 1) * 96], cqn[:, kc, :], kc == 0, kc == 1,
                               (t_uq, t_cqn), btA)
                        bB, btB = bp.next()
                        for kc in range(2):
                            mm(bB[0:96, :], uqrot[:, kc, h, :], cqn[:, kc, :], kc == 0, kc == 1,
                               (t_uq, t_ur, t_cqn), btB)
                        kwq = {"writes": (t_qs,)} if h == 0 else {"joins": (t_qs,)}
                        cp("act", qs[0:64, h, :], bA[0:64, :], (btA,), **kwq)
                        t1, t_1 = t1r.next(); t2, t_2 = t2r.next()
                        tt("dve", t1[64:96, :], bA[64:96, :], cosT[64:96, gs], ALU.mult, (btA, t_tab), writes=(t_1,))
                        tt("dve", t2[64:96, :], bB[64:96, :], sinT[64:96, gs], ALU.mult, (btB, t_tab), writes=(t_2,))
                        tt("pool", qs[64:96, h, :], t1[64:96, :], t2[64:96, :], ALU.add, (t_1, t_2), joins=(t_qs,))
                        bK, btK = bp.next()
                        mm(bK[:, :], ukv[:, h * 128:(h + 1) * 128], cqn[:, 2, :], True, True, (t_ukv, t_cqn), btK)
                        cp("act", ks[0:64, h, :], bK[0:64, :], (btK,), joins=(t_ks,))
                    p.dma("sp", QMv[:, :, gs], qs[:], t_qs, reads=(t_qs,))
                    p.dma("sp", KMv[:, :, gs], ks[:], t_ks, reads=(t_ks,))
                    for t in range(4):
                        bV, btV = bp.next()
                        mm(bV[:, :], cqn[:, 2, t * 128:(t + 1) * 128], ukv_v, True, True, (t_ukv, t_cqn), btV)
                        vt, t_v = vst.next()
                        cp("dve", vt[:, :, 0:64], bV[:, :].rearrange("p (h c) -> p h c", c=64), (btV,), joins=(t_v,))
                        r0 = (g * 4 + t) * 128
                        p.dma("sp", VM[r0:r0 + 128, :].rearrange("p (h c) -> p h c", c=128), vt[:], t_v, reads=(t_v,))

                stA(0)
                for g in range(NG):
                    stB(g)
                    stA(g + 1)
                    stC(g)
                p.flush(es)
            if upto < 4:
                return
            with contextlib.ExitStack() as es:
                def sb(name, shape, dtype):
                    return es.enter_context(nc.sbuf_tensor(uname(name), list(shape), dtype))
                bp = BankPool(list(range(8)))
                wsrc = w_in[l].rearrange("(kc p) n -> p kc n", p=128)
                wvs = sb("b_wvs", [128, 8, 512], F32); t_wvs = Tk()
                for kc in range(8):
                    p.dma("sp", wvs[:, kc, :], wsrc[:, kc, OFF_VD:OFF_VD + 512], t_wvs, joins=(t_wvs,))
                wv = sb("b_wv", [128, 8, 512], BF16); t_wv = Tk()
                tt("dve", wv[:], wvs[:], g_bc8.unsqueeze(2).to_broadcast([128, 8, 512]), ALU.mult,
                   (t_wvs,), writes=(t_wv,))
                vst = Ring([sb("b_vst%d" % i, [128, 8, 128], BF16) for i in range(3)])
                for (vt, t_v) in vst.items:
                    memset("pool", vt[:, :, 64:128], 1.0, joins=(t_v,))
                for i in range(NT):
                    bk, btk = bp.next()
                    for kc in range(8):
                        mm(bk[:, :], hT[:, kc, i * 128:(i + 1) * 128], wv[:, kc, :], kc == 0, kc == 7, (t_wv,), btk)
                    vt, t_v = vst.next()
                    cp("dve" if i % 2 else "act", vt[:, :, 0:64], bk[:, :].rearrange("p (h c) -> p h c", c=64),
                       (btk,), joins=(t_v,))
                    p.dma("sp", VD[i * 128:(i + 1) * 128, :].rearrange("p (h c) -> p h c", c=128), vt[:], t_v,
                          reads=(t_v,))
                chunks = []
                for j in range(4):
                    chunks.append((OFF_QD + j * 128, "rope", QD[j * 128:(j + 1) * 128, :], None))
                for j in range(4):
                    chunks.append((OFF_KD + j * 128, "rope", KD[j * 128:(j + 1) * 128, :], None))
                for j in range(4):
                    chunks.append((OFF_QX + j * 128, "copy", QX[j * 128:(j + 1) * 128, :], None))
                for j in range(12):
                    chunks.append((OFF_SIL + j * 128, "silu", SIL[j * 128:(j + 1) * 128, :], None))
                for j in range(24):
                    chunks.append((OFF_GL + j * 128, "sig", SG[j * 128:(j + 1) * 128, :], PV_BG + j))
                wstr = Ring([sb("b_wst%d" % i, [128, 8, 128], F32) for i in range(2)])
                wbfr = Ring([sb("b_wbf%d" % i, [128, 8, 128], BF16) for i in range(3)])
                wrotr = Ring([sb("b_wrot%d" % i, [128, 8, 128], BF16) for i in range(2)])
                stg = Ring([sb("b_stg%d" % i, [128, S], BF16) for i in range(2)])
                t1r = Ring([sb("b_t1%d" % i, [128, 512], F32) for i in range(2)])
                t2r = Ring([sb("b_t2%d" % i, [128, 512], F32) for i in range(2)])
                gb = g_bc8.unsqueeze(2).to_broadcast([128, 8, 128])
                wloaded = {}

                def wload(ci):
                    if ci < len(chunks) and ci not in wloaded:
                        ws, t_ws = wstr.next()
                        c0_ = chunks[ci][0]
                        p.dma("sp", ws[:], wsrc[:, :, c0_:c0_ + 128], t_ws, writes=(t_ws,))
                        wloaded[ci] = (ws, t_ws)
                wload(0)
                for ci, (c0, kind, dst, bcol) in enumerate(chunks):
                    ws, t_ws = wloaded[ci]
                    wb, t_wb = wbfr.next()
                    if kind == "rope":
                        tt("dve", ws[:], ws[:], gb, ALU.mult, (t_ws,), joins=(t_ws,))
                        cp("pool", wb[:], ws[:], (t_ws,), writes=(t_wb,))
                        wr, t_wr = wrotr.next()
                        ws5 = ws[:].rearrange("p k (h two c) -> p k h two c", h=4, two=2)
                        wr5 = wr[:].rearrange("p k (h two c) -> p k h two c", h=4, two=2)
                        ts("pool", wr5[:, :, :, 0, :], ws5[:, :, :, 1, :], -1.0, ALU.mult, (t_ws,), writes=(t_wr,))
                        cp("pool", wr5[:, :, :, 1, :], ws5[:, :, :, 0, :], (t_ws,), joins=(t_wr,))
                    else:
                        tt("dve", wb[:], ws[:], gb, ALU.mult, (t_ws,), writes=(t_wb,))
                    wload(ci + 1)
                    st, t_st = stg.next()
                    for g in range(NG):
                        gs = slice(g * 512, (g + 1) * 512)
                        bk, btk = bp.next()
                        for kc in range(8):
                            mm(bk[:, :], wb[:, kc, :], hT[:, kc, gs], kc == 0, kc == 7, (t_wb,), btk)
                        kw = {"writes": (t_st,)} if g == 0 else {"joins": (t_st,)}
                        if kind == "rope":
                            bk2, btk2 = bp.next()
                            for kc in range(8):
                                mm(bk2[:, :], wr[:, kc, :], hT[:, kc, gs], kc == 0, kc == 7, (t_wr,), btk2)
                            t1, t_1 = t1r.next(); t2, t_2 = t2r.next()
                            tt("dve", t1[:], bk[:, :], cosT[:, gs], ALU.mult, (btk, t_tab), writes=(t_1,))
                            tt("dve", t2[:], bk2[:, :], sinT[:, gs], ALU.mult, (btk2, t_tab), writes=(t_2,))
                            tt("pool", st[:, gs], t1[:], t2[:], ALU.add, (t_1, t_2), **kw)
                        elif kind == "copy":
                            cp("act" if g % 2 else "dve", st[:, gs], bk[:, :], (btk,), **kw)
                        elif kind == "silu":
                            act(st[:, gs], bk[:, :], AF.Silu, (btk,), **kw)
                        else:
                            act(st[:, gs], bk[:, :], AF.Sigmoid, (btk,), bias=pv[:, l, bcol:bcol + 1], **kw)
                    p.dma("sp", dst, st[:], t_st, reads=(t_st,))
                p.flush(es)

    def phase_attn(l, b, kind):
        mla = kind == "mla"
        dk = 96 if mla else 32
        nmap = 1 if mla else 2
        scale = float(dk) ** -0.5
        Vsrc = VM if mla else VD
        Odst = OA if mla else OB
        LAG = 3
        with contextlib.ExitStack() as es:
            def sb(name, shape, dtype):
                return es.enter_context(nc.sbuf_tensor(uname(name), list(shape), dtype))
            spool = BankPool([0, 1, 2, 3])
            opool = BankPool([4, 5, 6, 7])
            Vsb = sb("at_V", [128, NT, 1024], BF16); t_V = Tk()
            vsrc = Vsrc.rearrange("(kt p) c -> p kt c", p=128)
            VCH = 8
            for k0 in range(0, NT, VCH):
                k1 = min(NT, k0 + VCH)
                p.dma("sp", Vsb[:, k0:k1, :], vsrc[:, k0:k1, :], t_V, joins=(t_V,))
            dkp = 96 if mla else 128
            qr = [Ring([sb("at_q%d_%d" % (m, i), [dkp, S], BF16) for i in range(2)]) for m in range(nmap)]
            kr = [Ring([sb("at_k%d_%d" % (m, i), [dkp, S], BF16) for i in range(2)]) for m in range(nmap)]
            if not mla:
                for rg in qr + kr:
                    for (t_, tk_) in rg.items:
                        memset("pool", t_[32:64, :], 0.0, joins=(tk_,))
                        memset("pool", t_[64:128, :], 0.0, joins=(tk_,))
            pr_ = Ring([sb("at_p%d" % i, [128, 512], BF16) for i in range(LAG + 3)])
            ostr = Ring([sb("at_o%d" % i, [64, S], BF16) for i in range(2)])
            rcr = Ring([sb("at_rc%d" % i, [64, nmap, 512], F32) for i in range(2)])
            if not mla:
                tnr = Ring([sb("at_tn%d" % i, [64, 2, 512], F32) for i in range(2)])
                ddr = Ring([sb("at_d%d" % i, [64, 512], F32) for i in range(4)])
                sqr = Ring([sb("at_sq%d" % i, [128, 512], BF16) for i in range(2)])
                for (t_, tk_) in sqr.items:
                    memset("pool", t_[64:128, :], 0.0, joins=(tk_,))
                lnr = Ring([sb("at_ln%d" % i, [64, 512], F32) for i in range(2)])
                rsr = Ring([sb("at_rs%d" % i, [64, 512], F32) for i in range(2)])
            pend = []
            ticks = []

            def tick():
                if ticks:
                    for f_ in ticks.pop(0):
                        f_()

            def defer(fn_, k=1):
                while len(ticks) <= k:
                    ticks.append([])
                ticks[k].append(fn_)

            def drain(n):
                while len(pend) > n:
                    fn, after = pend.pop(0)
                    fn()
                    if after is not None:
                        after()

            def flush_part2():
                while ticks:
                    tick()

            qk_loaded = {}

            def qkload(h):
                if h >= 8 or h in qk_loaded:
                    return
                qt = []; kt_ = []
                for m in range(nmap):
                    q_, t_q = qr[m].next(); k_, t_k = kr[m].next()
                    if mla:
                        p.dma("sp", q_[:], QM[h], t_q, writes=(t_q,))
                        p.dma("sp", k_[:], KM[h], t_k, writes=(t_k,))
                    else:
                        r0 = (h * 2 + m) * 32
                        p.dma("sp", q_[0:32, :], QD[r0:r0 + 32, :], t_q, joins=(t_q,))
                        p.dma("sp", k_[0:32, :], KD[r0:r0 + 32, :], t_k, joins=(t_k,))
                    qt.append((q_, t_q)); kt_.append((k_, t_k))
                qk_loaded[h] = (qt, kt_)
            qkload(0)
            for h in range(8):
                qt, kt_ = qk_loaded[h]
                qkload(h + 1)
                ost, t_o = ostr.next()
                vh = Vsb[:, :, :]
                for g in range(NG):
                    obk = [opool.next() for _ in range(nmap)]
                    nkt = 4 * g + 4
                    for kt in range(nkt):
                        j = kt - 4 * g
                        q0 = 128 * j if j > 0 else 0
                        nq = 512 - q0
                        for m in range(nmap):
                            sbk, sbt = spool.next()
                            q_, t_q = qt[m]; k_, t_k = kt_[m]
                            mm(sbk[:, 0:nq], k_[:, kt * 128:(kt + 1) * 128],
                               q_[:, g * 512 + q0:(g + 1) * 512], True, True, (t_q, t_k), sbt)
                            pt, t_p = pr_.next()
                            act(pt[:, 0:nq], sbk[:, 0:nq], AF.Exp, (sbt,), writes=(t_p,), scale=scale)
                            if j >= 0:
                                tt("pool", pt[:, 0:128], pt[:, 0:128], mask_bf[:], ALU.mult, (t_p,), joins=(t_p,))
                            ob, obt = obk[m]
                            last = (kt == nkt - 1) and (m == nmap - 1)
                            lw = Vsb[:, kt, h * 128:(h + 1) * 128]

                            def pvfn(ob=ob, obt=obt, pt=pt, t_p=t_p, kt=kt, q0=q0, nq=nq, nkt=nkt, lw=lw):
                                mm(ob[:, q0:512], lw, pt[:, 0:nq], kt == 0, kt == nkt - 1, (t_V, t_p), obt)
                            after = None
                            if last:
                                def after(obk=obk, g=g, ost=ost, t_o=t_o, h=h):
                                    tick()
                                    gs = slice(g * 512, (g + 1) * 512)
                                    rc, t_rc = rcr.next()
                                    for m2 in range(nmap):
                                        recip(rc[:, m2, :], obk[m2][0][64:128, :], (obk[m2][1],),
                                              **({"writes": (t_rc,)} if m2 == 0 else {"joins": (t_rc,)}))
                                    kwo = {"writes": (t_o,)} if g == 0 else {"joins": (t_o,)}
                                    if mla:
                                        tt("dve", ost[:, gs], obk[0][0][0:64, :], rc[:, 0, :], ALU.mult,
                                           (obk[0][1], t_rc), **kwo)
                                    else:
                                        tn, t_tn = tnr.next()
                                        tt("dve", tn[:, 0, :], obk[0][0][0:64, :], rc[:, 0, :], ALU.mult,
                                           (obk[0][1], t_rc), writes=(t_tn,))
                                        tt("dve", tn[:, 1, :], obk[1][0][0:64, :], rc[:, 1, :], ALU.mult,
                                           (obk[1][1], t_rc), joins=(t_tn,))
                                        dd, t_d = ddr.next()
                                        stt(dd[:], tn[:, 1, :], neglam[0:64, l:l + 1], tn[:, 0, :], ALU.mult, ALU.add,
                                            (t_tn,), writes=(t_d,))

                                        def stB(dd=dd, t_d=t_d, gs=gs, kwo=kwo, g=g, ost=ost, t_o=t_o, h=h):
                                            sq, t_sq = sqr.next()
                                            tt("dve", sq[0:64, :], dd[:], dd[:], ALU.mult, (t_d,), joins=(t_sq,))
                                            sbk2, sbt2 = spool.next()
                                            mm(sbk2[:, :], ones_bf[:], sq[:], True, True, (t_sq,), sbt2)

                                            if True:
                                                lnv, t_ln = lnr.next()
                                                act(lnv[:], sbk2[0:64, :], AF.Ln, (sbt2,), writes=(t_ln,),
                                                    scale=1.0 / 64, bias=EPS)
                                                rs, t_rs = rsr.next()
                                                act(rs[:], lnv[:], AF.Exp, (t_ln,), writes=(t_rs,), scale=-0.5)
                                                stt(ost[:, gs], dd[:], gsub[0:64, l:l + 1], rs[:], ALU.mult, ALU.mult,
                                                    (t_d, t_rs), **kwo)
                                                if g == NG - 1:
                                                    p.dma("sp", Odst[h * 64:(h + 1) * 64, :], ost[:], t_o, reads=(t_o,))
                                        defer(stB)
                                    if mla and g == NG - 1:
                                        p.dma("sp", Odst[h * 64:(h + 1) * 64, :], ost[:], t_o, reads=(t_o,))
                            pend.append((pvfn, after))
                            drain(LAG)
            drain(0)
            flush_part2()
            p.flush(es)

    def phase_M(l, b):
        T = 256
        NGM = S // T
        xsrc = x_in[b] if l == 0 else out[b]
        with contextlib.ExitStack() as es:
            def sb(name, shape, dtype):
                return es.enter_context(nc.sbuf_tensor(uname(name), list(shape), dtype))
            pa = BankPool([0, 1, 2, 3, 4, 5, 6, 7])
            wbr = sb("m_wbr", [128, 3, 4, D], BF16); t_wbr = Tk()
            wo = sb("m_wo", [128, 8, D], BF16); t_wo = Tk()
            wstr = Ring([sb("m_wst%d" % i, [128, 2, D], F32) for i in range(2)])
            for i in range(3):
                wbsrc = w_branch[l, i].rearrange("(kc p) n -> p kc n", p=128)
                for j in range(2):
                    ws, t_ws = wstr.next()
                    p.dma("sp", ws[:], wbsrc[:, j * 2:(j + 1) * 2, :], t_ws, writes=(t_ws,))
                    cp("act", wbr[:, i, j * 2:(j + 1) * 2, :], ws[:], (t_ws,), joins=(t_wbr,))
            wosrc = w_out[l].rearrange("(kc p) n -> p kc n", p=128)
            for i in range(4):
                ws, t_ws = wstr.next()
                p.dma("sp", ws[:], wosrc[:, i * 2:(i + 1) * 2, :], t_ws, writes=(t_ws,))
                cp("act", wo[:, i * 2:(i + 1) * 2, :], ws[:], (t_ws,), joins=(t_wo,))
            oar = Ring([sb("m_oa%d" % i, [128, 4, T], BF16) for i in range(2)])
            obr = Ring([sb("m_ob%d" % i, [128, 4, T], BF16) for i in range(2)])
            qxr = Ring([sb("m_qx%d" % i, [128, 4, T], BF16) for i in range(2)])
            silr = Ring([sb("m_sil%d" % i, [128, 12, T], BF16) for i in range(2)])
            sgr = Ring([sb("m_sg%d" % i, [128, 24, T], BF16) for i in range(2)])
            xr = Ring([sb("m_x%d" % i, [128, T // 128, D], F32) for i in range(2)])
            gtr = Ring([sb("m_gt%d" % i, [128, 12, T], BF16) for i in range(2)])
            mgr = Ring([sb("m_mg%d" % i, [128, 8, T], BF16) for i in range(2)])
            ptr_ = Ring([sb("m_pt%d" % i, [128, T], BF16) for i in range(4)])
            rcr = Ring([sb("m_rc%d" % i, [128, T], F32) for i in range(2)])
            lxr = Ring([sb("m_lx%d" % i, [128, T], F32) for i in range(2)])
            ocr = Ring([sb("m_oc%d" % i, [128, T], F32) for i in range(2)])
            tmr = Ring([sb("m_tm%d" % i, [128, 3, 2 * T], F32) for i in range(2)])
            t4r = Ring([sb("m_t4%d" % i, [128, 2 * T], F32) for i in range(2)])
            ssr = Ring([sb("m_ss%d" % i, [128, 8], F32) for i in range(4)])
            junk = sb("m_junk", [128, 512], BF16); t_j = Tk()
            rtr = Ring([sb("m_rt%d" % i, [128, D], F32) for i in range(2)])
            rer = Ring([sb("m_re%d" % i, [128, D], F32) for i in range(2)])
            t_glob = Tk()
            xscale = 128.0 ** -0.5
            stt_ = {}

            def loadA(g):
                if g >= NGM:
                    return
                gs = slice(g * T, (g + 1) * T)
                oa, t_oa = oar.next(); ob_, t_ob = obr.next(); qx, t_qx = qxr.next(); sil, t_sil = silr.next()
                p.dma("sp", qx[:], QX.rearrange("(h p) t -> p h t", p=128)[:, :, gs], t_qx, writes=(t_qx,))
                p.dma("sp", oa[:], OA.rearrange("(c p) t -> p c t", p=128)[:, :, gs], t_oa, writes=(t_oa,))
                p.dma("sp", ob_[:], OB.rearrange("(c p) t -> p c t", p=128)[:, :, gs], t_ob, writes=(t_ob,))
                p.dma("sp", sil[:], SIL.rearrange("(c p) t -> p c t", p=128)[:, :, gs], t_sil, writes=(t_sil,))
                stt_.setdefault(g, {}).update(oa=(oa, t_oa), ob=(ob_, t_ob), qx=(qx, t_qx), sil=(sil, t_sil))

            def loadB(g):
                if g >= NGM:
                    return
                gs = slice(g * T, (g + 1) * T)
                sg, t_sg = sgr.next()
                p.dma("sp", sg[:], SG.rearrange("(c p) t -> p c t", p=128)[:, :, gs], t_sg, writes=(t_sg,))
                stt_.setdefault(g, {}).update(sg=(sg, t_sg))

            def loadC(g):
                if g >= NGM or g < 0:
                    return
                xt, t_x = xr.next()
                p.dma("sp", xt[:], xsrc[g * T:(g + 1) * T, :].rearrange("(t p) d -> p t d", p=128), t_x,
                      writes=(t_x,))
                stt_.setdefault(g, {}).update(x=(xt, t_x))

            def stage1(g):
                if g >= NGM:
                    return
                d_ = stt_[g]
                oa, t_oa = d_["oa"]; ob_, t_ob = d_["ob"]; qx, t_qx = d_["qx"]; sil, t_sil = d_["sil"]
                gt, t_gt = gtr.next()
                d_["gt"] = (gt, t_gt)
                tt("pool", gt[:, 0:4, :], oa[:], sil[:, 0:4, :], ALU.mult, (t_oa, t_sil), writes=(t_gt,))
                tt("pool", gt[:, 4:8, :], ob_[:], sil[:, 4:8, :], ALU.mult, (t_ob, t_sil), joins=(t_gt,))
                for h in range(4):
                    ob2, obt2 = pa.next()
                    pts = []
                    for mt in range(2):
                        sbk, sbt = pa.next()
                        mm(sbk[:, 0:T], kmT[:, h, mt * 128:(mt + 1) * 128], qx[:, h, :], True, True, (t_qx, t_glob), sbt)
                        pt, t_p = ptr_.next()
                        act(pt[:], sbk[:, 0:T], AF.Exp, (sbt,), writes=(t_p,), scale=xscale)
                        pts.append((pt, t_p))
                    for mt in range(2):
                        pt, t_p = pts[mt]
                        mm(ob2[:, 0:T], vm[:, mt, h, :], pt[:], mt == 0, mt == 1, (t_p, t_glob), obt2, first=(mt == 0))
                    for mt in range(2):
                        pt, t_p = pts[mt]
                        mm(ob2[:, T:2 * T], ones_bf[:], pt[:], mt == 0, mt == 1, (t_p,), obt2, first=False)
                    rc, t_rc = rcr.next()
                    lx, t_lx = lxr.next()
                    act(lx[:], ob2[:, T:2 * T], AF.Ln, (obt2,), writes=(t_lx,))
                    act(rc[:], lx[:], AF.Exp, (t_lx,), writes=(t_rc,), scale=-1.0)
                    oc, t_oc = ocr.next()
                    tt("dve", oc[:], ob2[:, 0:T], rc[:], ALU.mult, (obt2, t_rc), writes=(t_oc,))
                    tt("pool", gt[:, 8 + h, :], oc[:], sil[:, 8 + h, :], ALU.mult, (t_oc, t_sil), joins=(t_gt,))

            def stage2(g):
                if g >= NGM or g < 0:
                    return
                d_ = stt_[g]
                gt, t_gt = d_["gt"]; sg, t_sg = d_["sg"]
                mg, t_mg = mgr.next()
                d_["mg"] = (mg, t_mg)
                for cp_ in range(4):
                    ybk = []
                    for i in range(3):
                        yb, ybt = pa.next()
                        for half in range(2):
                            cc = cp_ * 2 + half
                            for kc in range(4):
                                mm(yb[:, half * T:(half + 1) * T], wbr[:, i, kc, cc * 128:(cc + 1) * 128],
                                   gt[:, i * 4 + kc, :], kc == 0, kc == 3, (t_wbr, t_gt), ybt,
                                   first=(kc == 0 and half == 0))
                        ybk.append((yb, ybt))
                    tm, t_tm = tmr.next()
                    for i in range(3):
                        sgv = sg[:, i * 8 + cp_ * 2:i * 8 + cp_ * 2 + 2, :]
                        tt("dve", tm[:, i, :].rearrange("p (a t) -> p a t", a=2),
                           ybk[i][0][:, 0:2 * T].rearrange("p (a t) -> p a t", a=2), sgv, ALU.mult,
                           (ybk[i][1], t_sg), **({"writes": (t_tm,)} if i == 0 else {"joins": (t_tm,)}))
                    t4, t_t4 = t4r.next()
                    tt("pool", t4[:], tm[:, 0, :], tm[:, 1, :], ALU.add, (t_tm,), writes=(t_t4,))
                    tt("pool", mg[:, cp_ * 2:cp_ * 2 + 2, :], t4[:].rearrange("p (a t) -> p a t", a=2),
                       tm[:, 2, :].rearrange("p (a t) -> p a t", a=2), ALU.add, (t_t4, t_tm),
                       **({"writes": (t_mg,)} if cp_ == 0 else {"joins": (t_mg,)}))

            def stage3(g):
                if g >= NGM or g < 0:
                    return
                d_ = stt_.pop(g)
                mg, t_mg = d_["mg"]; xt, t_x = d_["x"]
                for t in range(T // 128):
                    bks = []
                    for half in range(2):
                        bk, btk = pa.next()
                        for kc in range(8):
                            mm(bk[:, :], mg[:, kc, t * 128:(t + 1) * 128], wo[:, kc, half * 512:(half + 1) * 512],
                               kc == 0, kc == 7, (t_mg, t_wo), btk)
                        bks.append((bk, btk))
                    sst, t_s = ssr.next()
                    act(junk[:], bks[0][0][:, :], AF.Square, (bks[0][1],), writes=(t_j, t_s), accum=sst[:, 0:1])
                    act(junk[:], bks[1][0][:, :], AF.Square, (bks[1][1],), writes=(t_j,), joins=(t_s,),
                        accum=sst[:, 1:2])
                    tt("dve", sst[:, 2:3], sst[:, 0:1], sst[:, 1:2], ALU.add, (t_s,), joins=(t_s,))
                    act(sst[:, 3:4], sst[:, 2:3], AF.Ln, (t_s,), joins=(t_s,), scale=1.0 / D, bias=EPS)
                    act(sst[:, 4:5], sst[:, 3:4], AF.Exp, (t_s,), joins=(t_s,), scale=-0.5)
                    rt, t_rt = rtr.next()
                    for half in range(2):
                        hs = slice(half * 512, (half + 1) * 512)
                        stt(rt[:, hs], bks[half][0][:, :], sst[:, 4:5], npost[:, hs], ALU.mult, ALU.mult,
                            (bks[half][1], t_s, t_glob), **({"writes": (t_rt,)} if half == 0 else {"joins": (t_rt,)}))
                    re_, t_re = rer.next()
                    tt("dve", re_[:], rt[:], xt[:, t, :], ALU.add, (t_rt, t_x), writes=(t_re,))
                    r0 = g * T + t * 128
                    p.dma("sp", out[b, r0:r0 + 128, :], re_[:], t_re, reads=(t_re,))

            loadA(0); loadA(1); loadB(0)
            stage1(0)
            for i in range(NGM + 1):
                loadA(i + 2); loadB(i + 1); loadC(i)
                stage1(i + 1)
                stage2(i)
                stage3(i - 1)
            p.flush(es)

    phase_setup()
    for l in range(L):
        for b in range(NB):
            if upto >= 1:
                phase_X(l, b)
            if upto >= 2:
                phase_P(l, b)
            if upto >= 5:
                phase_attn(l, b, "mla")
            if upto >= 6:
                phase_attn(l, b, "diff")
            if upto >= 7:
                phase_M(l, b)
    ges.close()
    return nc, p


_CACHE = {}


def kernel(**inputs):
    n = 8
    if "nc" not in _CACHE:
        _CACHE["nc"] = build()[0]
    nc = _CACHE["nc"]
    names = ["norm_pre", "norm_post", "w_in", "q_norm", "w_uq", "kv_norm", "w_ukv",
             "lambda_q1", "lambda_k1", "lambda_q2", "lambda_k2", "diff_subln",
             "mem_norm", "w_mem_kv", "b_gate", "w_branch", "w_out"]
    x = np.ascontiguousarray(inputs["x"], dtype=np.float32)
    mem = np.ascontiguousarray(inputs["mem"], dtype=np.float32)
    pos = np.ascontiguousarray(inputs["positions"], dtype=np.int32)
    shared = {k: np.ascontiguousarray(inputs[k], dtype=np.float32) for k in names}
    in_maps = []
    for c in range(n):
        m = {"x": x[2 * c:2 * c + 2], "mem": mem[2 * c:2 * c + 2], "positions": pos}
        m.update(shared)
        in_maps.append(m)
    res = run_bass_kernel_spmd(nc, in_maps, core_ids=list(range(n)))
    return np.concatenate([np.asarray(r["out"], dtype=np.float32) for r in res.results], axis=0)
```

```python
import math
import contextlib
import numpy as np
import concourse.bass as bass
import concourse.mybir as mybir
from concourse.bass_utils import run_bass_kernel_spmd
from concourse.alu_op_type import AluOpType as ALU

F32 = mybir.dt.float32
BF16 = mybir.dt.bfloat16
I32 = mybir.dt.int32
AF = mybir.ActivationFunctionType
AX = mybir.AxisListType

D = 1024
IN_W = 7072
EPS = 1e-6
OFF_CQ, OFF_CKV, OFF_KPE, OFF_QD, OFF_KD, OFF_VD, OFF_QX, OFF_SIL, OFF_GL = (
    0, 256, 384, 416, 928, 1440, 1952, 2464, 4000)
PV_NPRE, PV_MEM, PV_QN, PV_KVN, PV_BG, PV_SUB, PV_N = 0, 8, 16, 18, 19, 43, 44
MEM_T = 256
TWO_PI = 2.0 * math.pi


class Tk:
    __slots__ = ("base", "w", "r", "joined", "sem", "name")

    def __init__(self, name=""):
        self.base = []
        self.w = []
        self.r = []
        self.joined = False
        self.sem = None
        self.name = name


def _compact(lst):
    out = []
    last = {}
    for i in lst:
        if i.is_dma:
            out.append(i)
        else:
            last[i.eng] = i
    out.extend(last.values())
    return out


class Ins:
    __slots__ = ("eng", "fn", "deps", "is_dma", "sem", "val", "needed", "phase")


ENGS = ("pe", "act", "dve", "pool", "sp")
BLK = {"pe": "tensor", "act": "scalar", "dve": "vector", "pool": "gpsimd", "sp": "sync"}


class Prog:
    def __init__(self, nc):
        self.nc = nc
        self.esem = {e: nc.alloc_semaphore("pg_" + e) for e in ("pe", "act", "dve", "pool")}
        self.ecnt = {e: 0 for e in self.esem}
        self.free_sems = []
        self.semcnt = {}
        self.waited = {e: {} for e in ENGS}
        self.phase = 0
        self.n_ins = 0
        self.begin()

    def begin(self):
        self.phase += 1
        self.q = {e: [] for e in ENGS}
        self.dma_slots = []

    def _deps_of(self, ins, tks_r, tks_w, tks_j):
        deps = []
        for t in tks_r:
            deps.extend(t.w)
        for t in tks_w:
            deps.extend(t.base); deps.extend(t.w); deps.extend(t.r)
        for t in tks_j:
            deps.extend(t.base)
            if not (t.joined and not t.r):
                deps.extend(t.w); deps.extend(t.r)
        out = []
        for d in deps:
            if d.phase != self.phase or d is ins:
                continue
            if (not d.is_dma) and (not ins.is_dma) and d.eng == ins.eng and d.eng == "pe":
                continue
            if not d.is_dma:
                d.needed = True
            out.append(d)
        return out

    def _update(self, ins, tks_r, tks_w, tks_j):
        for t in tks_w:
            t.base = []; t.w = [ins]; t.r = []; t.joined = False
        for t in tks_j:
            if t.joined and not t.r:
                t.w.append(ins)
                if len(t.w) > 8:
                    t.w = _compact(t.w)
            else:
                t.base = _compact(t.w + t.r)
                t.w = [ins]; t.r = []; t.joined = True
        for t in tks_r:
            t.r.append(ins)
            if len(t.r) > 8:
                t.r = _compact(t.r)

    def op(self, eng, fn, reads=(), writes=(), joins=()):
        ins = Ins()
        ins.eng = eng; ins.fn = fn; ins.is_dma = False; ins.needed = False
        ins.phase = self.phase; ins.sem = None; ins.val = 0
        ins.deps = self._deps_of(ins, reads, writes, joins)
        self._update(ins, reads, writes, joins)
        self.q[eng].append(ins)
        return ins

    def dma(self, eng, out, in_, slot, reads=(), writes=(), joins=()):
        ins = Ins()
        ins.eng = eng; ins.is_dma = True; ins.needed = True
        ins.phase = self.phase
        ins.fn = lambda e: e.dma_start(out=out, in_=in_)
        if slot.sem is None:
            slot.sem = self.free_sems.pop() if self.free_sems else self.nc.alloc_semaphore(
                "dq%d" % len(self.semcnt))
            self.semcnt.setdefault(slot.sem, 0)
            self.dma_slots.append(slot)
        self.semcnt[slot.sem] += 16
        ins.sem = slot.sem; ins.val = self.semcnt[slot.sem]
        ins.deps = self._deps_of(ins, reads, writes, joins)
        self._update(ins, reads, writes, joins)
        self.q[eng].append(ins)
        return ins

    def flush(self, es):
        nc = self.nc
        for e in ("pe", "act", "dve", "pool"):
            for ins in self.q[e]:
                if ins.needed and not ins.is_dma:
                    self.ecnt[e] += 1
                    ins.sem = self.esem[e]; ins.val = self.ecnt[e]
        block = es.enter_context(nc.Block())
        final_waits = [(s.sem, self.semcnt[s.sem]) for s in self.dma_slots]
        for e in ENGS:
            q = self.q[e]
            waited = self.waited[e]
            fw = final_waits if e == "sp" else ()
            self.n_ins += len(q)

            def body(eng, q=q, waited=waited, fw=fw):
                for ins in q:
                    need = {}
                    for d in ins.deps:
                        if d.val > need.get(d.sem, 0):
                            need[d.sem] = d.val
                    for s, v in need.items():
                        if waited.get(s, 0) < v:
                            eng.wait_ge(s, v)
                            waited[s] = v
                    h = ins.fn(eng)
                    if ins.is_dma:
                        h.then_inc(ins.sem, 16)
                    elif ins.needed:
                        h.then_inc(ins.sem, 1)
                for s, v in fw:
                    if waited.get(s, 0) < v:
                        eng.wait_ge(s, v)
                        waited[s] = v
            getattr(block, BLK[e])(body)
        for s in self.dma_slots:
            self.free_sems.append(s.sem)
            s.sem = None
        self.begin()


class Ring:
    def __init__(self, items):
        self.items = [(t, Tk()) for t in items]
        self.i = 0

    def next(self):
        it = self.items[self.i % len(self.items)]
        self.i += 1
        return it


def build(S=4096, L=4, NB=2, dbg=False, upto=99):
    NT = S // 128
    NG = S // 512
    lam_init = [0.8 - 0.6 * math.exp(-0.3 * l) for l in range(L)]
    nc = bass.Bass("TRN2", target_bir_lowering=False)
    dt = nc.dram_tensor

    def din(name, shape, dtype=F32):
        return dt(name, list(shape), dtype, kind="ExternalInput").ap()

    x_in = din("x", [NB, S, D])
    mem_in = din("mem", [NB, MEM_T, D])
    pos_in = din("positions", [S], I32)
    norm_pre = din("norm_pre", [L, D]); norm_post = din("norm_post", [L, D])
    w_in = din("w_in", [L, D, IN_W])
    q_norm = din("q_norm", [L, 256]); w_uq = din("w_uq", [L, 256, 768])
    kv_norm = din("kv_norm", [L, 128]); w_ukv = din("w_ukv", [L, 128, 1024])
    lq1 = din("lambda_q1", [L, 32]); lk1 = din("lambda_k1", [L, 32])
    lq2 = din("lambda_q2", [L, 32]); lk2 = din("lambda_k2", [L, 32])
    diff_subln = din("diff_subln", [L, 64])
    mem_norm = din("mem_norm", [L, D]); w_mem_kv = din("w_mem_kv", [L, D, 1024])
    b_gate = din("b_gate", [L, 3 * D])
    w_branch = din("w_branch", [L, 3, 512, D]); w_out = din("w_out", [L, D, D])
    out = dt("out", [NB, S, D], F32, kind="ExternalOutput").ap()

    skind = "ExternalOutput" if dbg else "Internal"

    def dsc(name, shape, dtype=BF16):
        return dt(name, list(shape), dtype, kind=skind).ap()

    QM = dsc("s_qm", [8, 96, S]); KM = dsc("s_km", [8, 96, S]); VM = dsc("s_vm", [S, 1024])
    QD = dsc("s_qd", [512, S]); KD = dsc("s_kd", [512, S]); VD = dsc("s_vd", [S, 1024])
    QX = dsc("s_qx", [512, S]); SIL = dsc("s_sil", [1536, S]); SG = dsc("s_sg", [3072, S])
    OA = dsc("s_oa", [512, S]); OB = dsc("s_ob", [512, S])
    COS = dsc("s_cos", [128, S], F32); SIN = dsc("s_sin", [128, S], F32)

    p = Prog(nc)
    ges = contextlib.ExitStack()
    _uc = [0]

    def uname(name):
        _uc[0] += 1
        return "%s_%d" % (name, _uc[0])

    def gsb(name, shape, dtype):
        return ges.enter_context(nc.sbuf_tensor(name, list(shape), dtype))

    ident_bf = gsb("ident_bf", [128, 128], BF16)
    ident_f = gsb("ident_f", [128, 128], F32)
    mask_bf = gsb("mask_bf", [128, 128], BF16)
    ones_bf = gsb("ones_bf", [128, 128], BF16)
    pv = gsb("pv", [128, L, PV_N], F32)
    gsub = gsb("gsub", [128, L], F32)
    neglam = gsb("neglam", [128, L], F32)
    kmT = gsb("kmT", [128, 4, MEM_T], BF16)
    vm = gsb("vm", [128, 2, 4, 128], BF16)
    npost = gsb("npost", [128, D], F32)
    banks = [ges.enter_context(nc.psum_tensor("bank%d" % i, [128, 512], F32)) for i in range(8)]
    bank_tk = [Tk("bank%d" % i) for i in range(8)]

    class BankPool:
        def __init__(self, ids):
            self.ids = ids; self.i = 0

        def next(self):
            b = self.ids[self.i % len(self.ids)]
            self.i += 1
            return banks[b], bank_tk[b]

    def mm(outap, lhsT, rhs, start, stop, reads, bk, first=None):
        if first is None:
            first = start
        kw = {"writes": (bk,)} if first else {"joins": (bk,)}
        p.op("pe", lambda e: e.matmul(outap, lhsT=lhsT, rhs=rhs, start=start, stop=stop),
             reads=reads, **kw)

    def tr(outap, in_, ident, reads, bk, first):
        kw = {"writes": (bk,)} if first else {"joins": (bk,)}
        p.op("pe", lambda e: e.transpose(outap, in_, ident), reads=reads, **kw)

    def act(outap, in_, func, reads, writes=(), joins=(), scale=None, bias=None, accum=None):
        kw = {}
        if scale is not None:
            kw["scale"] = scale
        if bias is not None:
            kw["bias"] = bias
        if accum is not None:
            kw["accum_out"] = accum
        p.op("act", lambda e: e.activation(out=outap, in_=in_, func=func, **kw),
             reads=reads, writes=writes, joins=joins)

    def tt(eng, outap, a, b, op, reads, writes=(), joins=()):
        p.op(eng, lambda e: e.tensor_tensor(out=outap, in0=a, in1=b, op=op),
             reads=reads, writes=writes, joins=joins)

    def ts(eng, outap, a, s1, op0, reads, writes=(), joins=(), s2=None, op1=None):
        if op1 is None:
            p.op(eng, lambda e: e.tensor_scalar(out=outap, in0=a, scalar1=s1, scalar2=None, op0=op0),
                 reads=reads, writes=writes, joins=joins)
        else:
            p.op(eng, lambda e: e.tensor_scalar(out=outap, in0=a, scalar1=s1, scalar2=s2,
                                                op0=op0, op1=op1),
                 reads=reads, writes=writes, joins=joins)

    def cp(eng, outap, in_, reads, writes=(), joins=()):
        if eng == "act":
            p.op("act", lambda e: e.copy(out=outap, in_=in_), reads=reads, writes=writes, joins=joins)
        else:
            p.op(eng, lambda e: e.tensor_copy(out=outap, in_=in_), reads=reads, writes=writes,
                 joins=joins)

    def stt(outap, a, scalar, b, op0, op1, reads, writes=(), joins=()):
        p.op("dve", lambda e: e.scalar_tensor_tensor(out=outap, in0=a, scalar=scalar, in1=b,
                                                     op0=op0, op1=op1),
             reads=reads, writes=writes, joins=joins)

    def recip(outap, in_, reads, writes=(), joins=()):
        p.op("dve", lambda e: e.reciprocal(out=outap, in_=in_), reads=reads, writes=writes,
             joins=joins)

    def memset(eng, ap, val, writes=(), joins=()):
        p.op(eng, lambda e: e.memset(ap, val), writes=writes, joins=joins)

    def phase_setup():
        with contextlib.ExitStack() as es:
            def sb(name, shape, dtype):
                return es.enter_context(nc.sbuf_tensor(uname(name), list(shape), dtype))
            iot = sb("iot", [128, 128], I32); t_iot = Tk()
            p.op("pool", lambda e: e.iota(iot[:], pattern=[[1, 128]], base=0, channel_multiplier=-1),
                 writes=(t_iot,))
            t_c = Tk()
            ts("dve", ident_bf[:], iot[:], 0, ALU.is_equal, (t_iot,), joins=(t_c,))
            ts("dve", ident_f[:], iot[:], 0, ALU.is_equal, (t_iot,), joins=(t_c,))
            ts("dve", mask_bf[:], iot[:], 0, ALU.is_ge, (t_iot,), joins=(t_c,))
            memset("dve", ones_bf[:], 1.0, joins=(t_c,))
            rows = sb("rows", [PV_N, L, 128], F32); t_rows = Tk()
            memset("dve", rows[:], 0.0, writes=(t_rows,))

            def ld(dst, src):
                p.dma("sp", dst, src, t_rows, joins=(t_rows,))
            for l in range(L):
                ld(rows[PV_NPRE:PV_NPRE + 8, l, :], norm_pre[l].rearrange("(c p) -> c p", p=128))
                ld(rows[PV_MEM:PV_MEM + 8, l, :], mem_norm[l].rearrange("(c p) -> c p", p=128))
                ld(rows[PV_QN:PV_QN + 2, l, :], q_norm[l].rearrange("(c p) -> c p", p=128))
                ld(rows[PV_KVN:PV_KVN + 1, l, :], kv_norm[l].rearrange("(c p) -> c p", p=128))
                ld(rows[PV_BG:PV_BG + 24, l, :], b_gate[l].rearrange("(c p) -> c p", p=128))
                ld(rows[PV_SUB:PV_SUB + 1, l, 0:64], diff_subln[l].rearrange("(c p) -> c p", p=64))
            t_pv = Tk()
            for l in range(L):
                bk, btk = banks[l % 8], bank_tk[l % 8]
                tr(bk[:, 0:PV_N], rows[:, l, :], ident_f[0:PV_N, 0:PV_N], (t_rows, t_c), btk, True)
                cp("dve", pv[:, l, :], bk[:, 0:PV_N], (btk,), joins=(t_pv,))
                ts("dve", gsub[:, l:l + 1], pv[:, l, PV_SUB:PV_SUB + 1], 1.0 - lam_init[l], ALU.mult,
                   (t_pv,), joins=(t_pv,))
            lam4 = [sb("lam%d" % i, [128, L, 32], F32) for i in range(4)]
            t_l = Tk()
            for i, src in enumerate((lq1, lk1, lq2, lk2)):
                p.dma("sp", lam4[i][:], src.partition_broadcast(128), t_l, joins=(t_l,))
            pr = sb("lampr", [128, 2, L, 32], F32); t_pr = Tk()
            tt("dve", pr[:, 0], lam4[0][:], lam4[1][:], ALU.mult, (t_l,), joins=(t_pr,))
            tt("dve", pr[:, 1], lam4[2][:], lam4[3][:], ALU.mult, (t_l,), joins=(t_pr,))
            sm = sb("lamsm", [128, 2, L], F32); t_sm = Tk()
            p.op("dve", lambda e: e.tensor_reduce(out=sm[:], in_=pr[:], axis=AX.X, op=ALU.add),
                 reads=(t_pr,), writes=(t_sm,))
            ex = sb("lamex", [128, 2, L], F32); t_ex = Tk()
            act(ex[:], sm[:], AF.Exp, (t_sm,), writes=(t_ex,))
            t_nl = Tk()
            tt("dve", neglam[:], ex[:, 1], ex[:, 0], ALU.subtract, (t_ex,), writes=(t_nl,))
            for l in range(L):
                ts("dve", neglam[:, l:l + 1], neglam[:, l:l + 1], -lam_init[l], ALU.add, (t_nl,),
                   joins=(t_nl,))
            oh_i = sb("oh_i", [128, 16], I32); t_pi = Tk()
            p.op("pool", lambda e: e.iota(oh_i[:], pattern=[[1, 16]], base=0, channel_multiplier=-1),
                 writes=(t_pi,))
            ts("dve", oh_i[:], oh_i[:], 15, ALU.bitwise_and, (t_pi,), joins=(t_pi,))
            oh = sb("oh_f", [128, 16], F32); t_pf = Tk()
            ts("dve", oh[:], oh_i[:], 0, ALU.is_equal, (t_pi,), writes=(t_pf,))
            invt = sb("invt", [128, 16], F32); t_it = Tk()
            for i in range(16):
                memset("pool", invt[:, i:i + 1], float(np.float32(10000.0) ** np.float32(-i / 16.0)), joins=(t_it,))
            tt("dve", oh[:], oh[:], invt[:], ALU.mult, (t_pf, t_it), joins=(t_pf,))
            inv = sb("inv", [128, 1], F32); t_inv = Tk()
            p.op("dve", lambda e: e.tensor_reduce(out=inv[:], in_=oh[:], axis=AX.X, op=ALU.add),
                 reads=(t_pf,), writes=(t_inv,))
            CH = min(S, 2048)
            posi = sb("posi", [128, CH], I32); t_pos = Tk()
            ang = sb("ang", [128, CH], F32); t_ang = Tk()
            kf = sb("kf", [128, CH], F32); t_kf = Tk()
            ki = sb("ki", [128, CH], I32); t_ki = Tk()
            rr = sb("rr", [128, CH], F32); t_rr = Tk()
            res = sb("ropres", [128, CH], F32); t_res = Tk()
            C1 = 6.28125
            C2 = TWO_PI - C1
            for c0 in range(0, S, CH):
                p.dma("sp", posi[:], pos_in[c0:c0 + CH].partition_broadcast(128), t_pos, writes=(t_pos,))
                cp("dve", ang[:], posi[:], (t_pos,), writes=(t_ang,))
                ts("dve", ang[:], ang[:], inv[:, 0:1], ALU.mult, (t_ang, t_inv), joins=(t_ang,))
                for which, dst in ((0, SIN), (1, COS)):
                    if which == 1:
                        ts("dve", kf[:], ang[:], math.pi / 2, ALU.add, (t_ang,), writes=(t_kf,))
                        src, t_src = kf, t_kf
                        cp("dve", rr[:], kf[:], (t_kf,), writes=(t_rr,))
                    else:
                        src, t_src = ang, t_ang
                        cp("dve", rr[:], ang[:], (t_ang,), writes=(t_rr,))
                    ts("dve", kf[:], rr[:], 1.0 / TWO_PI, ALU.mult, (t_rr,), writes=(t_kf,))
                    cp("dve", ki[:], kf[:], (t_kf,), writes=(t_ki,))
                    cp("dve", kf[:], ki[:], (t_ki,), writes=(t_kf,))
                    stt(rr[:], kf[:], -C1, rr[:], ALU.mult, ALU.add, (t_kf, t_rr), joins=(t_rr,))
                    stt(rr[:], kf[:], -C2, rr[:], ALU.mult, ALU.add, (t_kf, t_rr), joins=(t_rr,))
                    ts("dve", rr[:], rr[:], 3.1415925, ALU.min, (t_rr,), joins=(t_rr,),
                       s2=-3.1415925, op1=ALU.max)
                    act(res[:], rr[:], AF.Sin, (t_rr,), writes=(t_res,))
                    p.dma("sp", dst[:, c0:c0 + CH], res[:], t_res, reads=(t_res,))
            p.flush(es)

    def phase_X(l, b):
        with contextlib.ExitStack() as es:
            def sb(name, shape, dtype):
                return es.enter_context(nc.sbuf_tensor(uname(name), list(shape), dtype))
            bp = BankPool(list(range(8)))
            wst = sb("x_wst", [128, 8, 1024], F32); t_wst = Tk()
            wbf = sb("x_wbf", [128, 8, 1024], BF16); t_wbf = Tk()
            wsrc = w_mem_kv[l].rearrange("(kc p) n -> p kc n", p=128)
            for kc in range(8):
                p.dma("sp", wst[:, kc, :], wsrc[:, kc, :], t_wst, joins=(t_wst,))
            g_bc = pv[:, l, PV_MEM:PV_MEM + 8].unsqueeze(2).to_broadcast([128, 8, 1024])
            tt("dve", wbf[:], wst[:], g_bc, ALU.mult, (t_wst,), writes=(t_wbf,))
            mt_ = sb("x_m", [128, 2, D], F32); t_m = Tk()
            p.dma("sp", mt_[:], mem_in[b].rearrange("(t p) d -> p t d", p=128), t_m, writes=(t_m,))
            junk = sb("x_junk", [128, D], BF16); t_j = Tk()
            ss = sb("x_ss", [128, 2], F32); t_ss = Tk()
            for t in range(2):
                act(junk[:], mt_[:, t, :], AF.Square, (t_m,), writes=(t_j,), joins=(t_ss,), accum=ss[:, t:t + 1])
            lnv = sb("x_ln", [128, 2], F32); t_ln = Tk()
            act(lnv[:], ss[:], AF.Ln, (t_ss,), writes=(t_ln,), scale=1.0 / D, bias=EPS)
            rs = sb("x_rs", [128, 2], F32); t_rs = Tk()
            act(rs[:], lnv[:], AF.Exp, (t_ln,), writes=(t_rs,), scale=-0.5)
            mn = sb("x_mn", [128, 2, D], BF16); t_mn = Tk()
            for t in range(2):
                ts("dve", mn[:, t, :], mt_[:, t, :], rs[:, t:t + 1], ALU.mult, (t_m, t_rs), joins=(t_mn,))
            mT = sb("x_mT", [128, 8, MEM_T], BF16); t_mT = Tk()
            for t in range(2):
                bk, btk = bp.next()
                bv = bk[:, :].bitcast(BF16).rearrange("p (c t) -> p c t", c=8)
                for kc in range(8):
                    tr(bv[:, kc, :], mn[:, t, kc * 128:(kc + 1) * 128], ident_bf[:], (t_mn,), btk, kc == 0)
                cp("dve", mT[:, :, t * 128:(t + 1) * 128], bv, (btk,), joins=(t_mT,))
            t_kv = Tk()
            for h in range(4):
                bk, btk = bp.next()
                for kc in range(8):
                    mm(bk[:, 0:MEM_T], wbf[:, kc, h * 128:(h + 1) * 128], mT[:, kc, :], kc == 0, kc == 7,
                       (t_wbf, t_mT), btk)
                cp("act", kmT[:, h, :], bk[:, 0:MEM_T], (btk,), joins=(t_kv,))
            for t in range(2):
                bk, btk = bp.next()
                for kc in range(8):
                    mm(bk[:, :], mT[:, kc, t * 128:(t + 1) * 128], wbf[:, kc, 512:1024], kc == 0, kc == 7,
                       (t_wbf, t_mT), btk)
                cp("dve", vm[:, t, :, :], bk[:, :].rearrange("p (h d) -> p h d", h=4), (btk,), joins=(t_kv,))
            t_np = Tk()
            p.dma("sp", npost[:], norm_post[l].partition_broadcast(128), t_np, writes=(t_np,))
            p.flush(es)

    def phase_P(l, b):
        xsrc = x_in[b] if l == 0 else out[b]
        with contextlib.ExitStack() as pes:
            hT = pes.enter_context(nc.sbuf_tensor(uname("hT"), [128, 8, S], BF16))
            cosT = pes.enter_context(nc.sbuf_tensor(uname("cosT"), [128, S], F32))
            sinT = pes.enter_context(nc.sbuf_tensor(uname("sinT"), [128, S], F32))
            with contextlib.ExitStack() as es:
                def sb(name, shape, dtype):
                    return es.enter_context(nc.sbuf_tensor(uname(name), list(shape), dtype))
                t_tab = Tk()
                p.dma("sp", cosT[:], COS[:, :], t_tab, joins=(t_tab,))
                p.dma("sp", sinT[:], SIN[:, :], t_tab, joins=(t_tab,))
                bp = BankPool(list(range(8)))
                xr = Ring([sb("p1_x%d" % i, [128, D], F32) for i in range(3)])
                xnr = Ring([sb("p1_xn%d" % i, [128, D], BF16) for i in range(2)])
                junk = sb("p1_junk", [128, D], BF16); t_j = Tk()
                ssr = Ring([sb("p1_ss%d" % i, [128, 4], F32) for i in range(4)])
                t_hT = Tk()
                for i in range(NT):
                    xt, t_x = xr.next()
                    p.dma("sp", xt[:], xsrc[i * 128:(i + 1) * 128, :], t_x, writes=(t_x,))
                    sst, t_s = ssr.next()
                    act(junk[:], xt[:], AF.Square, (t_x,), writes=(t_j, t_s), accum=sst[:, 0:1])
                    act(sst[:, 1:2], sst[:, 0:1], AF.Ln, (t_s,), joins=(t_s,), scale=1.0 / D, bias=EPS)
                    act(sst[:, 2:3], sst[:, 1:2], AF.Exp, (t_s,), joins=(t_s,), scale=-0.5)
                    xn, t_xn = xnr.next()
                    ts("dve", xn[:], xt[:], sst[:, 2:3], ALU.mult, (t_x, t_s), writes=(t_xn,))
                    bk, btk = bp.next()
                    bv = bk[:, :].bitcast(BF16).rearrange("p (c t) -> p c t", c=8)
                    for kc in range(8):
                        tr(bv[:, kc, :], xn[:, kc * 128:(kc + 1) * 128], ident_bf[:], (t_xn,), btk, kc == 0)
                    cp("act" if i % 2 else "dve", hT[:, :, i * 128:(i + 1) * 128], bv, (btk,), joins=(t_hT,))
                p.flush(es)
            g_bc8 = pv[:, l, PV_NPRE:PV_NPRE + 8]
            if upto < 3:
                return
            def psb(name, shape, dtype):
                return pes.enter_context(nc.sbuf_tensor(uname(name), list(shape), dtype))
            NM = 416
            wm = psb("a_wm", [128, 8, NM], BF16); t_wm = Tk()
            wkrot = psb("a_wkrot", [128, 8, 96], BF16)
            uq = psb("a_uq", [128, 2, 768], BF16); t_uq = Tk()
            uqrot = psb("a_uqrot", [128, 2, 8, 96], BF16)
            ukv = psb("a_ukv", [128, 1024], BF16); t_ukv = Tk()
            with contextlib.ExitStack() as es:
                def sb(name, shape, dtype):
                    return es.enter_context(nc.sbuf_tensor(uname(name), list(shape), dtype))
                wst = sb("a_wst", [128, 8, NM], F32); t_wst = Tk()
                wsrc = w_in[l].rearrange("(kc p) n -> p kc n", p=128)
                for kc in range(8):
                    p.dma("sp", wst[:, kc, :], wsrc[:, kc, 0:NM], t_wst, joins=(t_wst,))
                tt("dve", wst[:], wst[:], g_bc8.unsqueeze(2).to_broadcast([128, 8, NM]), ALU.mult,
                   (t_wst,), joins=(t_wst,))
                cp("pool", wm[:], wst[:], (t_wst,), writes=(t_wm,))
                t_wk = Tk()
                memset("pool", wkrot[:, :, 0:64], 0.0, joins=(t_wk,))
                ts("pool", wkrot[:, :, 64:80], wst[:, :, 400:416], -1.0, ALU.mult, (t_wst,), joins=(t_wk,))
                cp("pool", wkrot[:, :, 80:96], wst[:, :, 384:400], (t_wst,), joins=(t_wk,))
                uqs = sb("a_uqs", [128, 2, 768], F32); t_uqs = Tk()
                p.dma("sp", uqs[:], w_uq[l].rearrange("(kc p) n -> p kc n", p=128), t_uqs, writes=(t_uqs,))
                tt("dve", uqs[:], uqs[:], pv[:, l, PV_QN:PV_QN + 2].unsqueeze(2).to_broadcast([128, 2, 768]),
                   ALU.mult, (t_uqs,), joins=(t_uqs,))
                cp("pool", uq[:], uqs[:], (t_uqs,), writes=(t_uq,))
                t_ur = Tk()
                memset("pool", uqrot[:, :, :, 0:64], 0.0, joins=(t_ur,))
                uqs4 = uqs[:].rearrange("p k (h c) -> p k h c", h=8)
                ts("pool", uqrot[:, :, :, 64:80], uqs4[:, :, :, 80:96], -1.0, ALU.mult, (t_uqs,), joins=(t_ur,))
                cp("pool", uqrot[:, :, :, 80:96], uqs4[:, :, :, 64:80], (t_uqs,), joins=(t_ur,))
                ukvs = sb("a_ukvs", [128, 1024], F32); t_ukvs = Tk()
                p.dma("sp", ukvs[:], w_ukv[l], t_ukvs, writes=(t_ukvs,))
                ts("dve", ukv[:], ukvs[:], pv[:, l, PV_KVN:PV_KVN + 1], ALU.mult, (t_ukvs,), writes=(t_ukv,))
                p.flush(es)
            with contextlib.ExitStack() as es:
                def sb(name, shape, dtype):
                    return es.enter_context(nc.sbuf_tensor(uname(name), list(shape), dtype))
                bp = BankPool(list(range(8)))
                ukv_v = ukv[:].rearrange("p (h two c) -> p h two c", h=8, two=2)[:, :, 1, :]
                sqr = Ring([sb("a_sq%d" % i, [128, 3, 512], BF16) for i in range(2)])
                rsr = Ring([sb("a_rs%d" % i, [128, 2, 512], F32) for i in range(1)])
                lnr = Ring([sb("a_ln%d" % i, [128, 2, 512], F32) for i in range(1)])
                cqnr = Ring([sb("a_cqn%d" % i, [128, 3, 512], BF16) for i in range(2)])
                t1r = Ring([sb("a_t1%d" % i, [128, 512], F32) for i in range(2)])
                t2r = Ring([sb("a_t2%d" % i, [128, 512], F32) for i in range(2)])
                kper = Ring([sb("a_kpe%d" % i, [128, 512], BF16) for i in range(2)])
                qst = Ring([sb("a_qst%d" % i, [96, 8, 512], BF16) for i in range(2)])
                kst = Ring([sb("a_kst%d" % i, [96, 8, 512], BF16) for i in range(2)])
                vst = Ring([sb("a_vst%d" % i, [128, 8, 128], BF16) for i in range(3)])
                for (vt, t_v) in vst.items:
                    memset("pool", vt[:, :, 64:128], 1.0, joins=(t_v,))
                QMv = QM.rearrange("h d t -> d h t")
                KMv = KM.rearrange("h d t -> d h t")
                cqfr = Ring([sb("a_cqf%d" % i, [128, 3, 512], F32) for i in range(2)])
                sta = {}

                def stA(g):
                    if g >= NG:
                        return
                    gs = slice(g * 512, (g + 1) * 512)
                    cb = []
                    for (c0, m) in ((0, 128), (128, 128), (256, 128), (320, 96)):
                        bk, btk = bp.next()
                        for kc in range(8):
                            mm(bk[0:m, :], wm[:, kc, c0:c0 + m], hT[:, kc, gs], kc == 0, kc == 7, (t_wm,), btk)
                        cb.append((bk, btk))
                    bkB, btkB = bp.next()
                    for kc in range(8):
                        mm(bkB[0:96, :], wkrot[:, kc, :], hT[:, kc, gs], kc == 0, kc == 7, (t_wm, t_wk), btkB)
                    sq, t_sq = sqr.next()
                    cqf, t_cqf = cqfr.next()
                    for i in range(3):
                        act(sq[:, i, :], cb[i][0][:, :], AF.Square, (cb[i][1],),
                            **({"writes": (t_sq,)} if i == 0 else {"joins": (t_sq,)}))
                        cp("act", cqf[:, i, :], cb[i][0][:, :], (cb[i][1],),
                           **({"writes": (t_cqf,)} if i == 0 else {"joins": (t_cqf,)}))
                    t1, t_1 = t1r.next(); t2, t_2 = t2r.next()
                    tt("dve", t1[64:96, :], cb[3][0][64:96, :], cosT[64:96, gs], ALU.mult, (cb[3][1], t_tab), writes=(t_1,))
                    tt("dve", t2[64:96, :], bkB[64:96, :], sinT[64:96, gs], ALU.mult, (btkB, t_tab), writes=(t_2,))
                    kpe, t_kpe = kper.next()
                    tt("pool", kpe[64:96, :], t1[64:96, :], t2[64:96, :], ALU.add, (t_1, t_2), writes=(t_kpe,))
                    ks, t_ks = kst.next()
                    cp("act", ks[64:96, :, :], kpe[64:96, :].unsqueeze(1).to_broadcast([32, 8, 512]),
                       (t_kpe,), writes=(t_ks,))
                    sta[g] = dict(sq=(sq, t_sq), cqf=(cqf, t_cqf), ks=(ks, t_ks))

                def stB(g):
                    d_ = sta[g]
                    sq, t_sq = d_["sq"]; cqf, t_cqf = d_["cqf"]
                    bq, btq = bp.next()
                    mm(bq[:, :], ones_bf[:], sq[:, 0, :], True, False, (t_sq,), btq)
                    mm(bq[:, :], ones_bf[:], sq[:, 1, :], False, True, (t_sq,), btq)
                    bkv, btkv = bp.next()
                    mm(bkv[:, :], ones_bf[:], sq[:, 2, :], True, True, (t_sq,), btkv)
                    lnv, t_ln = lnr.next()
                    act(lnv[:, 0, :], bq[:, :], AF.Ln, (btq,), writes=(t_ln,), scale=1.0 / 256, bias=EPS)
                    act(lnv[:, 1, :], bkv[:, :], AF.Ln, (btkv,), joins=(t_ln,), scale=1.0 / 128, bias=EPS)
                    rs, t_rs = rsr.next()
                    act(rs[:], lnv[:], AF.Exp, (t_ln,), writes=(t_rs,), scale=-0.5)
                    cqn, t_cqn = cqnr.next()
                    tt("dve", cqn[:, 0, :], cqf[:, 0, :], rs[:, 0, :], ALU.mult, (t_cqf, t_rs), writes=(t_cqn,))
                    tt("dve", cqn[:, 1, :], cqf[:, 1, :], rs[:, 0, :], ALU.mult, (t_cqf, t_rs), joins=(t_cqn,))
                    tt("dve", cqn[:, 2, :], cqf[:, 2, :], rs[:, 1, :], ALU.mult, (t_cqf, t_rs), joins=(t_cqn,))
                    d_["cqn"] = (cqn, t_cqn)

                def stC(g):
                    gs = slice(g * 512, (g + 1) * 512)
                    d_ = sta.pop(g)
                    cqn, t_cqn = d_["cqn"]; ks, t_ks = d_["ks"]
                    qs, t_qs = qst.next()
                    for h in range(8):
                        bA, btA = bp.next()
                        for kc in range(2):
                            mm(bA[0:96, :], uq[:, kc, h * 96:(h + 1) * 96], cqn[:, kc, :], kc == 0, kc == 1,
                               (t_uq, t_cqn), btA)
                        bB, btB = bp.next()
                        for kc in range(2):
                            mm(bB[0:96, :], uqrot[:, kc, h, :], cqn[:, kc, :], kc == 0, kc == 1,
                               (t_uq, t_ur, t_cqn), btB)
                        kwq = {"writes": (t_qs,)} if h == 0 else {"joins": (t_qs,)}
                        cp("act", qs[0:64, h, :], bA[0:64, :], (btA,), **kwq)
                        t1, t_1 = t1r.next(); t2, t_2 = t2r.next()
                        tt("dve", t1[64:96, :], bA[64:96, :], cosT[64:96, gs], ALU.mult, (btA, t_tab), writes=(t_1,))
                        tt("dve", t2[64:96, :], bB[64:96, :], sinT[64:96, gs], ALU.mult, (btB, t_tab), writes=(t_2,))
                        tt("pool", qs[64:96, h, :], t1[64:96, :], t2[64:96, :], ALU.add, (t_1, t_2), joins=(t_qs,))
                        bK, btK = bp.next()
                        mm(bK[:, :], ukv[:, h * 128:(h + 1) * 128], cqn[:, 2, :], True, True, (t_ukv, t_cqn), btK)
                        cp("act", ks[0:64, h, :], bK[0:64, :], (btK,), joins=(t_ks,))
                    p.dma("sp", QMv[:, :, gs], qs[:], t_qs, reads=(t_qs,))
                    p.dma("sp", KMv[:, :, gs], ks[:], t_ks, reads=(t_ks,))
                    for t in range(4):
                        bV, btV = bp.next()
                        mm(bV[:, :], cqn[:, 2, t * 128:(t + 1) * 128], ukv_v, True, True, (t_ukv, t_cqn), btV)
                        vt, t_v = vst.next()
                        cp("dve", vt[:, :, 0:64], bV[:, :].rearrange("p (h c) -> p h c", c=64), (btV,), joins=(t_v,))
                        r0 = (g * 4 + t) * 128
                        p.dma("sp", VM[r0:r0 + 128, :].rearrange("p (h c) -> p h c", c=128), vt[:], t_v, reads=(t_v,))

                stA(0)
                for g in range(NG):
                    stB(g)
                    stA(g + 1)
                    stC(g)
                p.flush(es)
            if upto < 4:
                return
            with contextlib.ExitStack() as es:
                def sb(name, shape, dtype):
                    return es.enter_context(nc.sbuf_tensor(uname(name), list(shape), dtype))
                bp = BankPool(list(range(8)))
                wsrc = w_in[l].rearrange("(kc p) n -> p kc n", p=128)
                wvs = sb("b_wvs", [128, 8, 512], F32); t_wvs = Tk()
                for kc in range(8):
                    p.dma("sp", wvs[:, kc, :], wsrc[:, kc, OFF_VD:OFF_VD + 512], t_wvs, joins=(t_wvs,))
                wv = sb("b_wv", [128, 8, 512], BF16); t_wv = Tk()
                tt("dve", wv[:], wvs[:], g_bc8.unsqueeze(2).to_broadcast([128, 8, 512]), ALU.mult,
                   (t_wvs,), writes=(t_wv,))
                vst = Ring([sb("b_vst%d" % i, [128, 8, 128], BF16) for i in range(3)])
                for (vt, t_v) in vst.items:
                    memset("pool", vt[:, :, 64:128], 1.0, joins=(t_v,))
                for i in range(NT):
                    bk, btk = bp.next()
                    for kc in range(8):
                        mm(bk[:, :], hT[:, kc, i * 128:(i + 1) * 128], wv[:, kc, :], kc == 0, kc == 7, (t_wv,), btk)
                    vt, t_v = vst.next()
                    cp("dve" if i % 2 else "act", vt[:, :, 0:64], bk[:, :].rearrange("p (h c) -> p h c", c=64),
                       (btk,), joins=(t_v,))
                    p.dma("sp", VD[i * 128:(i + 1) * 128, :].rearrange("p (h c) -> p h c", c=128), vt[:], t_v,
                          reads=(t_v,))
                chunks = []
                for j in range(4):
                    chunks.append((OFF_QD + j * 128, "rope", QD[j * 128:(j + 1) * 128, :], None))
                for j in range(4):
                    chunks.append((OFF_KD + j * 128, "rope", KD[j * 128:(j + 1) * 128, :], None))
                for j in range(4):
                    chunks.append((OFF_QX + j * 128, "copy", QX[j * 128:(j + 1) * 128, :], None))
                for j in range(12):
                    chunks.append((OFF_SIL + j * 128, "silu", SIL[j * 128:(j + 1) * 128, :], None))
                for j in range(24):
                    chunks.append((OFF_GL + j * 128, "sig", SG[j * 128:(j + 1) * 128, :], PV_BG + j))
                wstr = Ring([sb("b_wst%d" % i, [128, 8, 128], F32) for i in range(2)])
                wbfr = Ring([sb("b_wbf%d" % i, [128, 8, 128], BF16) for i in range(3)])
                wrotr = Ring([sb("b_wrot%d" % i, [128, 8, 128], BF16) for i in range(2)])
                stg = Ring([sb("b_stg%d" % i, [128, S], BF16) for i in range(2)])
                t1r = Ring([sb("b_t1%d" % i, [128, 512], F32) for i in range(2)])
                t2r = Ring([sb("b_t2%d" % i, [128, 512], F32) for i in range(2)])
                gb = g_bc8.unsqueeze(2).to_broadcast([128, 8, 128])
                wloaded = {}

                def wload(ci):
                    if ci < len(chunks) and ci not in wloaded:
                        ws, t_ws = wstr.next()
                        c0_ = chunks[ci][0]
                        p.dma("sp", ws[:], wsrc[:, :, c0_:c0_ + 128], t_ws, writes=(t_ws,))
                        wloaded[ci] = (ws, t_ws)
                conv_d = {}

                def wconv(ci):
                    if ci >= len(chunks) or ci in conv_d:
                        return
                    ws, t_ws = wloaded.pop(ci)
                    kind_ = chunks[ci][1]
                    wb, t_wb = wbfr.next()
                    wr, t_wr = None, None
                    if kind_ == "rope":
                        tt("dve", ws[:], ws[:], gb, ALU.mult, (t_ws,), joins=(t_ws,))
                        cp("act", wb[:], ws[:], (t_ws,), writes=(t_wb,))
                        wr, t_wr = wrotr.next()
                        ws5 = ws[:].rearrange("p k (h two c) -> p k h two c", h=4, two=2)
                        wr5 = wr[:].rearrange("p k (h two c) -> p k h two c", h=4, two=2)
                        ts("pool", wr5[:, :, :, 0, :], ws5[:, :, :, 1, :], -1.0, ALU.mult, (t_ws,), writes=(t_wr,))
                        cp("pool", wr5[:, :, :, 1, :], ws5[:, :, :, 0, :], (t_ws,), joins=(t_wr,))
                    else:
                        tt("dve", wb[:], ws[:], gb, ALU.mult, (t_ws,), writes=(t_wb,))
                    conv_d[ci] = (wb, t_wb, wr, t_wr)
                wload(0); wload(1)
                wconv(0)
                for ci, (c0, kind, dst, bcol) in enumerate(chunks):
                    wconv(ci + 1)
                    wload(ci + 2)
                    wb, t_wb, wr, t_wr = conv_d.pop(ci)
                    st, t_st = stg.next()
                    for g in range(NG):
                        gs = slice(g * 512, (g + 1) * 512)
                        bk, btk = bp.next()
                        for kc in range(8):
                            mm(bk[:, :], wb[:, kc, :], hT[:, kc, gs], kc == 0, kc == 7, (t_wb,), btk)
                        kw = {"writes": (t_st,)} if g == 0 else {"joins": (t_st,)}
                        if kind == "rope":
                            bk2, btk2 = bp.next()
                            for kc in range(8):
                                mm(bk2[:, :], wr[:, kc, :], hT[:, kc, gs], kc == 0, kc == 7, (t_wr,), btk2)
                            t1, t_1 = t1r.next(); t2, t_2 = t2r.next()
                            tt("dve", t1[:], bk[:, :], cosT[:, gs], ALU.mult, (btk, t_tab), writes=(t_1,))
                            tt("dve", t2[:], bk2[:, :], sinT[:, gs], ALU.mult, (btk2, t_tab), writes=(t_2,))
                            tt("pool", st[:, gs], t1[:], t2[:], ALU.add, (t_1, t_2), **kw)
                        elif kind == "copy":
                            cp("act" if g % 2 else "dve", st[:, gs], bk[:, :], (btk,), **kw)
                        elif kind == "silu":
                            act(st[:, gs], bk[:, :], AF.Silu, (btk,), **kw)
                        else:
                            act(st[:, gs], bk[:, :], AF.Sigmoid, (btk,), bias=pv[:, l, bcol:bcol + 1], **kw)
                    p.dma("sp", dst, st[:], t_st, reads=(t_st,))
                p.flush(es)

    def phase_attn(l, b, kind):
        mla = kind == "mla"
        dk = 96 if mla else 32
        nmap = 1 if mla else 2
        scale = float(dk) ** -0.5
        Vsrc = VM if mla else VD
        Odst = OA if mla else OB
        LAG = 3
        with contextlib.ExitStack() as es:
            def sb(name, shape, dtype):
                return es.enter_context(nc.sbuf_tensor(uname(name), list(shape), dtype))
            spool = BankPool([0, 1, 2, 3])
            opool = BankPool([4, 5, 6, 7])
            Vsb = sb("at_V", [128, NT, 1024], BF16); t_V = Tk()
            vsrc = Vsrc.rearrange("(kt p) c -> p kt c", p=128)
            VCH = 8
            for k0 in range(0, NT, VCH):
                k1 = min(NT, k0 + VCH)
                p.dma("sp", Vsb[:, k0:k1, :], vsrc[:, k0:k1, :], t_V, joins=(t_V,))
            dkp = 96 if mla else 128
            qr = [Ring([sb("at_q%d_%d" % (m, i), [dkp, S], BF16) for i in range(2)]) for m in range(nmap)]
            kr = [Ring([sb("at_k%d_%d" % (m, i), [dkp, S], BF16) for i in range(2)]) for m in range(nmap)]
            if not mla:
                for rg in qr + kr:
                    for (t_, tk_) in rg.items:
                        memset("pool", t_[32:64, :], 0.0, joins=(tk_,))
                        memset("pool", t_[64:128, :], 0.0, joins=(tk_,))
            pr_ = Ring([sb("at_p%d" % i, [128, 512], BF16) for i in range(LAG + 3)])
            ostr = Ring([sb("at_o%d" % i, [64, S], BF16) for i in range(2)])
            rcr = Ring([sb("at_rc%d" % i, [64, nmap, 512], F32) for i in range(2)])
            if not mla:
                tnr = Ring([sb("at_tn%d" % i, [64, 2, 512], F32) for i in range(2)])
                ddr = Ring([sb("at_d%d" % i, [64, 512], F32) for i in range(4)])
                sqr = Ring([sb("at_sq%d" % i, [128, 512], BF16) for i in range(2)])
                for (t_, tk_) in sqr.items:
                    memset("pool", t_[64:128, :], 0.0, joins=(tk_,))
                lnr = Ring([sb("at_ln%d" % i, [64, 512], F32) for i in range(2)])
                rsr = Ring([sb("at_rs%d" % i, [64, 512], F32) for i in range(2)])
            pend = []
            ticks = []

            def tick():
                if ticks:
                    for f_ in ticks.pop(0):
                        f_()

            def defer(fn_, k=1):
                while len(ticks) <= k:
                    ticks.append([])
                ticks[k].append(fn_)

            def drain(n):
                while len(pend) > n:
                    fn, after = pend.pop(0)
                    fn()
                    if after is not None:
                        after()

            def flush_part2():
                while ticks:
                    tick()

            qk_loaded = {}

            def qkload(h):
                if h >= 8 or h in qk_loaded:
                    return
                qt = []; kt_ = []
                for m in range(nmap):
                    q_, t_q = qr[m].next(); k_, t_k = kr[m].next()
                    if mla:
                        p.dma("sp", q_[:], QM[h], t_q, writes=(t_q,))
                        p.dma("sp", k_[:], KM[h], t_k, writes=(t_k,))
                    else:
                        r0 = (h * 2 + m) * 32
                        p.dma("sp", q_[0:32, :], QD[r0:r0 + 32, :], t_q, joins=(t_q,))
                        p.dma("sp", k_[0:32, :], KD[r0:r0 + 32, :], t_k, joins=(t_k,))
                    qt.append((q_, t_q)); kt_.append((k_, t_k))
                qk_loaded[h] = (qt, kt_)
            qkload(0)
            for h in range(8):
                qt, kt_ = qk_loaded[h]
                qkload(h + 1)
                ost, t_o = ostr.next()
                vh = Vsb[:, :, :]
                for g in range(NG):
                    obk = [opool.next() for _ in range(nmap)]
                    nkt = 4 * g + 4
                    for kt in range(nkt):
                        j = kt - 4 * g
                        q0 = 128 * j if j > 0 else 0
                        nq = 512 - q0
                        for m in range(nmap):
                            sbk, sbt = spool.next()
                            q_, t_q = qt[m]; k_, t_k = kt_[m]
                            mm(sbk[:, 0:nq], k_[:, kt * 128:(kt + 1) * 128],
                               q_[:, g * 512 + q0:(g + 1) * 512], True, True, (t_q, t_k), sbt)
                            pt, t_p = pr_.next()
                            act(pt[:, 0:nq], sbk[:, 0:nq], AF.Exp, (sbt,), writes=(t_p,), scale=scale)
                            if j >= 0:
                                tt("pool", pt[:, 0:128], pt[:, 0:128], mask_bf[:], ALU.mult, (t_p,), joins=(t_p,))
                            ob, obt = obk[m]
                            last = (kt == nkt - 1) and (m == nmap - 1)
                            lw = Vsb[:, kt, h * 128:(h + 1) * 128]

                            def pvfn(ob=ob, obt=obt, pt=pt, t_p=t_p, kt=kt, q0=q0, nq=nq, nkt=nkt, lw=lw):
                                mm(ob[:, q0:512], lw, pt[:, 0:nq], kt == 0, kt == nkt - 1, (t_V, t_p), obt)
                            after = None
                            if last:
                                def after(obk=obk, g=g, ost=ost, t_o=t_o, h=h):
                                    tick()
                                    gs = slice(g * 512, (g + 1) * 512)
                                    rc, t_rc = rcr.next()
                                    for m2 in range(nmap):
                                        recip(rc[:, m2, :], obk[m2][0][64:128, :], (obk[m2][1],),
                                              **({"writes": (t_rc,)} if m2 == 0 else {"joins": (t_rc,)}))
                                    kwo = {"writes": (t_o,)} if g == 0 else {"joins": (t_o,)}
                                    if mla:
                                        tt("dve", ost[:, gs], obk[0][0][0:64, :], rc[:, 0, :], ALU.mult,
                                           (obk[0][1], t_rc), **kwo)
                                    else:
                                        tn, t_tn = tnr.next()
                                        tt("dve", tn[:, 0, :], obk[0][0][0:64, :], rc[:, 0, :], ALU.mult,
                                           (obk[0][1], t_rc), writes=(t_tn,))
                                        tt("dve", tn[:, 1, :], obk[1][0][0:64, :], rc[:, 1, :], ALU.mult,
                                           (obk[1][1], t_rc), joins=(t_tn,))
                                        dd, t_d = ddr.next()
                                        stt(dd[:], tn[:, 1, :], neglam[0:64, l:l + 1], tn[:, 0, :], ALU.mult, ALU.add,
                                            (t_tn,), writes=(t_d,))

                                        def stB(dd=dd, t_d=t_d, gs=gs, kwo=kwo, g=g, ost=ost, t_o=t_o, h=h):
                                            sq, t_sq = sqr.next()
                                            tt("dve", sq[0:64, :], dd[:], dd[:], ALU.mult, (t_d,), joins=(t_sq,))
                                            sbk2, sbt2 = spool.next()
                                            mm(sbk2[:, :], ones_bf[:], sq[:], True, True, (t_sq,), sbt2)

                                            if True:
                                                lnv, t_ln = lnr.next()
                                                act(lnv[:], sbk2[0:64, :], AF.Ln, (sbt2,), writes=(t_ln,),
                                                    scale=1.0 / 64, bias=EPS)
                                                rs, t_rs = rsr.next()
                                                act(rs[:], lnv[:], AF.Exp, (t_ln,), writes=(t_rs,), scale=-0.5)
                                                stt(ost[:, gs], dd[:], gsub[0:64, l:l + 1], rs[:], ALU.mult, ALU.mult,
                                                    (t_d, t_rs), **kwo)
                                                if g == NG - 1:
                                                    p.dma("sp", Odst[h * 64:(h + 1) * 64, :], ost[:], t_o, reads=(t_o,))
                                        defer(stB)
                                    if mla and g == NG - 1:
                                        p.dma("sp", Odst[h * 64:(h + 1) * 64, :], ost[:], t_o, reads=(t_o,))
                            pend.append((pvfn, after))
                            drain(LAG)
            drain(0)
            flush_part2()
            p.flush(es)

    def phase_M(l, b):
        T = 256
        NGM = S // T
        xsrc = x_in[b] if l == 0 else out[b]
        with contextlib.ExitStack() as es:
            def sb(name, shape, dtype):
                return es.enter_context(nc.sbuf_tensor(uname(name), list(shape), dtype))
            pa = BankPool([0, 1, 2, 3, 4, 5, 6, 7])
            wbr = sb("m_wbr", [128, 3, 4, D], BF16); t_wbr = Tk()
            wo = sb("m_wo", [128, 8, D], BF16); t_wo = Tk()
            wstr = Ring([sb("m_wst%d" % i, [128, 2, D], F32) for i in range(2)])
            for i in range(3):
                wbsrc = w_branch[l, i].rearrange("(kc p) n -> p kc n", p=128)
                for j in range(2):
                    ws, t_ws = wstr.next()
                    p.dma("sp", ws[:], wbsrc[:, j * 2:(j + 1) * 2, :], t_ws, writes=(t_ws,))
                    cp("act", wbr[:, i, j * 2:(j + 1) * 2, :], ws[:], (t_ws,), joins=(t_wbr,))
            wosrc = w_out[l].rearrange("(kc p) n -> p kc n", p=128)
            for i in range(4):
                ws, t_ws = wstr.next()
                p.dma("sp", ws[:], wosrc[:, i * 2:(i + 1) * 2, :], t_ws, writes=(t_ws,))
                cp("act", wo[:, i * 2:(i + 1) * 2, :], ws[:], (t_ws,), joins=(t_wo,))
            oar = Ring([sb("m_oa%d" % i, [128, 4, T], BF16) for i in range(2)])
            obr = Ring([sb("m_ob%d" % i, [128, 4, T], BF16) for i in range(2)])
            qxr = Ring([sb("m_qx%d" % i, [128, 4, T], BF16) for i in range(2)])
            silr = Ring([sb("m_sil%d" % i, [128, 12, T], BF16) for i in range(2)])
            sgr = Ring([sb("m_sg%d" % i, [128, 24, T], BF16) for i in range(2)])
            xr = Ring([sb("m_x%d" % i, [128, T // 128, D], F32) for i in range(2)])
            gtr = Ring([sb("m_gt%d" % i, [128, 12, T], BF16) for i in range(2)])
            mgr = Ring([sb("m_mg%d" % i, [128, 8, T], BF16) for i in range(2)])
            ptr_ = Ring([sb("m_pt%d" % i, [128, T], BF16) for i in range(4)])
            rcr = Ring([sb("m_rc%d" % i, [128, T], F32) for i in range(2)])
            lxr = Ring([sb("m_lx%d" % i, [128, T], F32) for i in range(2)])
            ocr = Ring([sb("m_oc%d" % i, [128, T], F32) for i in range(2)])
            tmr = Ring([sb("m_tm%d" % i, [128, 3, 2 * T], F32) for i in range(2)])
            t4r = Ring([sb("m_t4%d" % i, [128, 2 * T], F32) for i in range(2)])
            ssr = Ring([sb("m_ss%d" % i, [128, 8], F32) for i in range(4)])
            junk = sb("m_junk", [128, 512], BF16); t_j = Tk()
            rtr = Ring([sb("m_rt%d" % i, [128, D], F32) for i in range(2)])
            rer = Ring([sb("m_re%d" % i, [128, D], F32) for i in range(2)])
            t_glob = Tk()
            xscale = 128.0 ** -0.5
            stt_ = {}

            def loadA(g):
                if g >= NGM:
                    return
                gs = slice(g * T, (g + 1) * T)
                oa, t_oa = oar.next(); ob_, t_ob = obr.next(); qx, t_qx = qxr.next(); sil, t_sil = silr.next()
                p.dma("sp", qx[:], QX.rearrange("(h p) t -> p h t", p=128)[:, :, gs], t_qx, writes=(t_qx,))
                p.dma("sp", oa[:], OA.rearrange("(c p) t -> p c t", p=128)[:, :, gs], t_oa, writes=(t_oa,))
                p.dma("sp", ob_[:], OB.rearrange("(c p) t -> p c t", p=128)[:, :, gs], t_ob, writes=(t_ob,))
                p.dma("sp", sil[:], SIL.rearrange("(c p) t -> p c t", p=128)[:, :, gs], t_sil, writes=(t_sil,))
                stt_.setdefault(g, {}).update(oa=(oa, t_oa), ob=(ob_, t_ob), qx=(qx, t_qx), sil=(sil, t_sil))

            def loadB(g):
                if g >= NGM:
                    return
                gs = slice(g * T, (g + 1) * T)
                sg, t_sg = sgr.next()
                p.dma("sp", sg[:], SG.rearrange("(c p) t -> p c t", p=128)[:, :, gs], t_sg, writes=(t_sg,))
                stt_.setdefault(g, {}).update(sg=(sg, t_sg))

            def loadC(g):
                if g >= NGM or g < 0:
                    return
                xt, t_x = xr.next()
                p.dma("sp", xt[:], xsrc[g * T:(g + 1) * T, :].rearrange("(t p) d -> p t d", p=128), t_x,
                      writes=(t_x,))
                stt_.setdefault(g, {}).update(x=(xt, t_x))

            def stage1(g):
                if g >= NGM:
                    return
                d_ = stt_[g]
                oa, t_oa = d_["oa"]; ob_, t_ob = d_["ob"]; qx, t_qx = d_["qx"]; sil, t_sil = d_["sil"]
                gt, t_gt = gtr.next()
                d_["gt"] = (gt, t_gt)
                tt("pool", gt[:, 0:4, :], oa[:], sil[:, 0:4, :], ALU.mult, (t_oa, t_sil), writes=(t_gt,))
                tt("pool", gt[:, 4:8, :], ob_[:], sil[:, 4:8, :], ALU.mult, (t_ob, t_sil), joins=(t_gt,))
                for h in range(4):
                    ob2, obt2 = pa.next()
                    pts = []
                    for mt in range(2):
                        sbk, sbt = pa.next()
                        mm(sbk[:, 0:T], kmT[:, h, mt * 128:(mt + 1) * 128], qx[:, h, :], True, True, (t_qx, t_glob), sbt)
                        pt, t_p = ptr_.next()
                        act(pt[:], sbk[:, 0:T], AF.Exp, (sbt,), writes=(t_p,), scale=xscale)
                        pts.append((pt, t_p))
                    for mt in range(2):
                        pt, t_p = pts[mt]
                        mm(ob2[:, 0:T], vm[:, mt, h, :], pt[:], mt == 0, mt == 1, (t_p, t_glob), obt2, first=(mt == 0))
                    for mt in range(2):
                        pt, t_p = pts[mt]
                        mm(ob2[:, T:2 * T], ones_bf[:], pt[:], mt == 0, mt == 1, (t_p,), obt2, first=False)
                    rc, t_rc = rcr.next()
                    lx, t_lx = lxr.next()
                    act(lx[:], ob2[:, T:2 * T], AF.Ln, (obt2,), writes=(t_lx,))
                    act(rc[:], lx[:], AF.Exp, (t_lx,), writes=(t_rc,), scale=-1.0)
                    oc, t_oc = ocr.next()
                    tt("dve", oc[:], ob2[:, 0:T], rc[:], ALU.mult, (obt2, t_rc), writes=(t_oc,))
                    tt("pool", gt[:, 8 + h, :], oc[:], sil[:, 8 + h, :], ALU.mult, (t_oc, t_sil), joins=(t_gt,))

            def stage2(g):
                if g >= NGM or g < 0:
                    return
                d_ = stt_[g]
                gt, t_gt = d_["gt"]; sg, t_sg = d_["sg"]
                mg, t_mg = mgr.next()
                d_["mg"] = (mg, t_mg)
                for cp_ in range(4):
                    ybk = []
                    for i in range(3):
                        yb, ybt = pa.next()
                        for half in range(2):
                            cc = cp_ * 2 + half
                            for kc in range(4):
                                mm(yb[:, half * T:(half + 1) * T], wbr[:, i, kc, cc * 128:(cc + 1) * 128],
                                   gt[:, i * 4 + kc, :], kc == 0, kc == 3, (t_wbr, t_gt), ybt,
                                   first=(kc == 0 and half == 0))
                        ybk.append((yb, ybt))
                    tm, t_tm = tmr.next()
                    for i in range(3):
                        sgv = sg[:, i * 8 + cp_ * 2:i * 8 + cp_ * 2 + 2, :]
                        tt("dve", tm[:, i, :].rearrange("p (a t) -> p a t", a=2),
                           ybk[i][0][:, 0:2 * T].rearrange("p (a t) -> p a t", a=2), sgv, ALU.mult,
                           (ybk[i][1], t_sg), **({"writes": (t_tm,)} if i == 0 else {"joins": (t_tm,)}))
                    t4, t_t4 = t4r.next()
                    tt("pool", t4[:], tm[:, 0, :], tm[:, 1, :], ALU.add, (t_tm,), writes=(t_t4,))
                    tt("pool", mg[:, cp_ * 2:cp_ * 2 + 2, :], t4[:].rearrange("p (a t) -> p a t", a=2),
                       tm[:, 2, :].rearrange("p (a t) -> p a t", a=2), ALU.add, (t_t4, t_tm),
                       **({"writes": (t_mg,)} if cp_ == 0 else {"joins": (t_mg,)}))

            def stage3(g):
                if g >= NGM or g < 0:
                    return
                d_ = stt_.pop(g)
                mg, t_mg = d_["mg"]; xt, t_x = d_["x"]
                for t in range(T // 128):
                    bks = []
                    for half in range(2):
                        bk, btk = pa.next()
                        for kc in range(8):
                            mm(bk[:, :], mg[:, kc, t * 128:(t + 1) * 128], wo[:, kc, half * 512:(half + 1) * 512],
                               kc == 0, kc == 7, (t_mg, t_wo), btk)
                        bks.append((bk, btk))
                    sst, t_s = ssr.next()
                    act(junk[:], bks[0][0][:, :], AF.Square, (bks[0][1],), writes=(t_j, t_s), accum=sst[:, 0:1])
                    act(junk[:], bks[1][0][:, :], AF.Square, (bks[1][1],), writes=(t_j,), joins=(t_s,),
                        accum=sst[:, 1:2])
                    tt("dve", sst[:, 2:3], sst[:, 0:1], sst[:, 1:2], ALU.add, (t_s,), joins=(t_s,))
                    act(sst[:, 3:4], sst[:, 2:3], AF.Ln, (t_s,), joins=(t_s,), scale=1.0 / D, bias=EPS)
                    act(sst[:, 4:5], sst[:, 3:4], AF.Exp, (t_s,), joins=(t_s,), scale=-0.5)
                    rt, t_rt = rtr.next()
                    for half in range(2):
                        hs = slice(half * 512, (half + 1) * 512)
                        stt(rt[:, hs], bks[half][0][:, :], sst[:, 4:5], npost[:, hs], ALU.mult, ALU.mult,
                            (bks[half][1], t_s, t_glob), **({"writes": (t_rt,)} if half == 0 else {"joins": (t_rt,)}))
                    re_, t_re = rer.next()
                    tt("dve", re_[:], rt[:], xt[:, t, :], ALU.add, (t_rt, t_x), writes=(t_re,))
                    r0 = g * T + t * 128
                    p.dma("sp", out[b, r0:r0 + 128, :], re_[:], t_re, reads=(t_re,))

            loadA(0); loadA(1); loadB(0)
            stage1(0)
            for i in range(NGM + 1):
                loadA(i + 2); loadB(i + 1); loadC(i)
                stage1(i + 1)
                stage2(i)
                stage3(i - 1)
            p.flush(es)

    phase_setup()
    for l in range(L):
        for b in range(NB):
            if upto >= 1:
                phase_X(l, b)
            if upto >= 2:
                phase_P(l, b)
            if upto >= 5:
                phase_attn(l, b, "mla")
            if upto >= 6:
                phase_attn(l, b, "diff")
            if upto >= 7:
                phase_M(l, b)
    ges.close()
    return nc, p


_CACHE = {}


def kernel(**inputs):
    n = 8
    if "nc" not in _CACHE:
        _CACHE["nc"] = build()[0]
    nc = _CACHE["nc"]
    names = ["norm_pre", "norm_post", "w_in", "q_norm", "w_uq", "kv_norm", "w_ukv",
             "lambda_q1", "lambda_k1", "lambda_q2", "lambda_k2", "diff_subln",
             "mem_norm", "w_mem_kv", "b_gate", "w_branch", "w_out"]
    x = np.ascontiguousarray(inputs["x"], dtype=np.float32)
    mem = np.ascontiguousarray(inputs["mem"], dtype=np.float32)
    pos = np.ascontiguousarray(inputs["positions"], dtype=np.int32)
    shared = {k: np.ascontiguousarray(inputs[k], dtype=np.float32) for k in names}
    in_maps = []
    for c in range(n):
        m = {"x": x[2 * c:2 * c + 2], "mem": mem[2 * c:2 * c + 2], "positions": pos}
        m.update(shared)
        in_maps.append(m)
    res = run_bass_kernel_spmd(nc, in_maps, core_ids=list(range(n)))
    return np.concatenate([np.asarray(r["out"], dtype=np.float32) for r in res.results], axis=0)
```
